# Optimizing a Trainium2 kernel written in Bass

```python
import math
import jax
import jax.numpy as jnp
from jax import lax
import numpy as np

D_MODEL = 1024
BATCH = 8
SEQ = 2048
DEPTH = 4
DEC_BATCH = 128
DEC_SEQ = 4
PAST_LEN = 8192
PAGE_SIZE = 128

N_MIXERS = 3
N_SWA = (DEPTH + 2) // 3
N_GDN = (DEPTH + 1) // 3
N_HGRN = DEPTH // 3

SWA_HEAD_DIM = 64
SWA_HEADS = D_MODEL // SWA_HEAD_DIM
SWA_KV_HEADS = SWA_HEADS // 4
SWA_GROUP = SWA_HEADS // SWA_KV_HEADS
WINDOW = 128
ROT_DIM = SWA_HEAD_DIM // 4
ROPE_THETA = 500000.0
SWA_IN = (SWA_HEADS + 2 * SWA_KV_HEADS) * SWA_HEAD_DIM

GDN_HEAD_DIM = 128
GDN_QK_HEADS = D_MODEL // GDN_HEAD_DIM
GDN_V_HEADS = 2 * GDN_QK_HEADS
GDN_KEY_DIM = GDN_QK_HEADS * GDN_HEAD_DIM
GDN_V_DIM = GDN_V_HEADS * GDN_HEAD_DIM
GDN_CONV_DIM = 2 * GDN_KEY_DIM + GDN_V_DIM
GDN_CONV = 4
GDN_CHUNK = 64
GDN_IN = GDN_CONV_DIM + GDN_V_DIM + 2 * GDN_V_HEADS

HG_HEAD_DIM = 128
HG_HEADS = D_MODEL // HG_HEAD_DIM
HG_DIM = HG_HEADS * HG_HEAD_DIM
HG_CHUNK = 32

D_FF = ((8 * D_MODEL // 3 + 127) // 128) * 128
NORM_EPS = 1e-6
NEG_INF = -1e30

kernel_name = "hybrid_swa_gdn_hgrn2_macaron_step"


def _rmsnorm(x, g):
    xf = x.astype(jnp.float32)
    y = xf * lax.rsqrt(jnp.mean(xf * xf, axis=-1, keepdims=True) + NORM_EPS)
    return (y * g.astype(jnp.float32)).astype(x.dtype)


def _l2norm(x):
    return x * lax.rsqrt(jnp.sum(x * x, axis=-1, keepdims=True) + NORM_EPS)


def _swiglu(h, w_in, w_out):
    gate, up = jnp.split(h @ w_in, 2, axis=-1)
    return (jax.nn.silu(gate) * up) @ w_out


def _rope_partial(x, pos):
    half = ROT_DIM // 2
    inv_freq = ROPE_THETA ** (-jnp.arange(0, ROT_DIM, 2, dtype=jnp.float32) / ROT_DIM)
    ang = pos.astype(jnp.float32)[:, None] * inv_freq[None, :]
    cos = jnp.cos(ang)[None, :, None, :]
    sin = jnp.sin(ang)[None, :, None, :]
    xr = x[..., :ROT_DIM].astype(jnp.float32)
    x1, x2 = xr[..., :half], xr[..., half:]
    rot = jnp.concatenate([x1 * cos - x2 * sin, x2 * cos + x1 * sin], axis=-1)
    return jnp.concatenate([rot.astype(x.dtype), x[..., ROT_DIM:]], axis=-1)


def _sink_attention(q, k, v, mask, sinks):
    s = jnp.einsum('...qhgd,...khd->...hgqk', q.astype(jnp.float32), k.astype(jnp.float32))
    s = jnp.where(mask, s * (SWA_HEAD_DIM ** -0.5), NEG_INF)
    sink = sinks.astype(jnp.float32).reshape(SWA_KV_HEADS, SWA_GROUP, 1, 1)
    m = jnp.maximum(jnp.max(s, axis=-1, keepdims=True), sink)
    p = jnp.exp(s - m)
    denom = jnp.sum(p, axis=-1, keepdims=True) + jnp.exp(sink - m)
    return jnp.einsum('...hgqk,...khd->...qhgd', p / denom, v.astype(jnp.float32))


def _swa_mixer(h, pos0, w_in, w_out, sinks, cache_k, cache_v):
    b, l, _ = h.shape
    q, k, v = jnp.split(h @ w_in, [SWA_HEADS * SWA_HEAD_DIM, (SWA_HEADS + SWA_KV_HEADS) * SWA_HEAD_DIM], axis=-1)
    pos = pos0 + jnp.arange(l)
    q = _rope_partial(q.reshape(b, l, SWA_HEADS, SWA_HEAD_DIM), pos)
    k = _rope_partial(k.reshape(b, l, SWA_KV_HEADS, SWA_HEAD_DIM), pos)
    v = v.reshape(b, l, SWA_KV_HEADS, SWA_HEAD_DIM)
    q = q.reshape(b, l, SWA_KV_HEADS, SWA_GROUP, SWA_HEAD_DIM)
    if cache_k is None:
        nb = l // WINDOW
        qb = q.reshape(b, nb, WINDOW, SWA_KV_HEADS, SWA_GROUP, SWA_HEAD_DIM)
        kb = k.reshape(b, nb, WINDOW, SWA_KV_HEADS, SWA_HEAD_DIM)
        vb = v.reshape(b, nb, WINDOW, SWA_KV_HEADS, SWA_HEAD_DIM)
        padw = ((0, 0), (1, 0), (0, 0), (0, 0), (0, 0))
        kk = jnp.concatenate([jnp.pad(kb, padw)[:, :-1], kb], axis=2)
        vv = jnp.concatenate([jnp.pad(vb, padw)[:, :-1], vb], axis=2)
        qi = jnp.arange(WINDOW)[:, None]
        kj = jnp.arange(2 * WINDOW)[None, :]
        d = qi + WINDOW - kj
        band = (d >= 0) & (d < WINDOW)
        valid = (jnp.arange(nb)[:, None, None] > 0) | (kj[None] >= WINDOW)
        mask = (band[None] & valid)[None, :, None, None]
        o = _sink_attention(qb, kk, vv, mask, sinks).reshape(b, l, SWA_HEADS * SWA_HEAD_DIM)
        k_buf, v_buf = k[:, l - WINDOW:], v[:, l - WINDOW:]
    else:
        kk = jnp.concatenate([cache_k.astype(k.dtype), k], axis=1)
        vv = jnp.concatenate([cache_v.astype(v.dtype), v], axis=1)
        qi = jnp.arange(l)[:, None]
        kj = jnp.arange(WINDOW + l)[None, :]
        d = qi + WINDOW - kj
        mask = (d >= 0) & (d < WINDOW)
        o = _sink_attention(q, kk, vv, mask, sinks).reshape(b, l, SWA_HEADS * SWA_HEAD_DIM)
        k_buf, v_buf = kk[:, l:], vv[:, l:]
    return o.astype(h.dtype) @ w_out, k_buf, v_buf


def _to_chunks(x, c, n):
    bsz, l = x.shape[:2]
    x = jnp.pad(x, [(0, 0), (0, n * c - l)] + [(0, 0)] * (x.ndim - 2))
    x = x.reshape((bsz, n, c) + x.shape[2:])
    return jnp.swapaxes(jnp.moveaxis(x, 1, 0), 2, 3)


def _from_chunks(o, l):
    o = jnp.moveaxis(jnp.swapaxes(o, 2, 3), 0, 1)
    return o.reshape((o.shape[0], -1) + o.shape[3:])[:, :l]


def _gated_delta_rule(q, k, v, g, beta, s0):
    l = q.shape[1]
    c = min(GDN_CHUNK, l)
    n = -(-l // c)
    dv = v.shape[-1]
    incl = jnp.tril(jnp.ones((c, c), dtype=bool))
    strict = jnp.tril(jnp.ones((c, c), dtype=bool), -1)
    eye = jnp.eye(c, dtype=jnp.float32)

    def step(s, xs):
        qc, kc, vc, gc, bc = xs
        gcum = jnp.cumsum(gc, axis=-1)
        decay = jnp.exp(jnp.where(incl, gcum[..., :, None] - gcum[..., None, :], -jnp.inf))
        kb = kc * bc[..., None]
        a = jnp.where(strict, jnp.einsum('bhid,bhjd->bhij', kb, kc) * decay, 0.0)
        rhs = jnp.concatenate([vc * bc[..., None], kb * jnp.exp(gcum)[..., None]], axis=-1)
        sol = lax.linalg.triangular_solve(eye + a, rhs, left_side=True, lower=True, unit_diagonal=True)
        u, w = sol[..., :dv], sol[..., dv:]
        v_new = u - jnp.einsum('bhik,bhkv->bhiv', w, s)
        attn = jnp.einsum('bhid,bhjd->bhij', qc, kc) * decay
        o = (jnp.einsum('bhik,bhkv->bhiv', qc * jnp.exp(gcum)[..., None], s)
             + jnp.einsum('bhij,bhjv->bhiv', attn, v_new))
        g_last = gcum[..., -1:]
        s = (s * jnp.exp(g_last)[..., None]
             + jnp.einsum('bhik,bhiv->bhkv', kc * jnp.exp(g_last - gcum)[..., None], v_new))
        return s, o

    xs = (_to_chunks(q, c, n), _to_chunks(k, c, n), _to_chunks(v, c, n),
          _to_chunks(g, c, n), _to_chunks(beta, c, n))
    s_fin, o = lax.scan(step, s0, xs)
    return _from_chunks(o, l), s_fin


def _gated_linear_recurrence(q, k, v, log_f, s0):
    l = q.shape[1]
    c = min(HG_CHUNK, l)
    n = -(-l // c)
    incl = jnp.tril(jnp.ones((c, c), dtype=bool))[:, :, None]

    def step(s, xs):
        qc, kc, vc, lc = xs
        bcum = jnp.cumsum(lc, axis=2)
        diff = bcum[:, :, :, None, :] - bcum[:, :, None, :, :]
        decay = jnp.exp(jnp.where(incl, diff, -jnp.inf))
        attn = jnp.einsum('bhid,bhjd,bhijd->bhij', qc, kc, decay)
        o = (jnp.einsum('bhik,bhkv->bhiv', qc * jnp.exp(bcum), s)
             + jnp.einsum('bhij,bhjv->bhiv', attn, vc))
        b_last = bcum[:, :, -1:]
        s = (s * jnp.exp(bcum[:, :, -1])[..., None]
             + jnp.einsum('bhjk,bhjv->bhkv', kc * jnp.exp(b_last - bcum), vc))
        return s, o

    xs = (_to_chunks(q, c, n), _to_chunks(k, c, n), _to_chunks(v, c, n), _to_chunks(log_f, c, n))
    s_fin, o = lax.scan(step, s0, xs)
    return _from_chunks(o, l), s_fin


def _gdn_mixer(h, w_in, conv_w, a_log, dt_bias, norm_g, w_out, s0, conv0):
    b, l, _ = h.shape
    qkv, z, beta_in, a_in = jnp.split(
        h @ w_in, [GDN_CONV_DIM, GDN_CONV_DIM + GDN_V_DIM, GDN_CONV_DIM + GDN_V_DIM + GDN_V_HEADS], axis=-1)
    if conv0 is None:
        conv0 = jnp.zeros((b, GDN_CONV - 1, GDN_CONV_DIM), h.dtype)
    xp = jnp.concatenate([conv0.astype(h.dtype), qkv], axis=1)
    acc = xp[:, 0:l] * conv_w[0]
    for t in range(1, GDN_CONV):
        acc = acc + xp[:, t:t + l] * conv_w[t]
    qkv_c = jax.nn.silu(acc.astype(jnp.float32))
    q, k, v = jnp.split(qkv_c, [GDN_KEY_DIM, 2 * GDN_KEY_DIM], axis=-1)
    rep = GDN_V_HEADS // GDN_QK_HEADS
    q = jnp.repeat(_l2norm(q.reshape(b, l, GDN_QK_HEADS, GDN_HEAD_DIM)), rep, axis=2) * (GDN_HEAD_DIM ** -0.5)
    k = jnp.repeat(_l2norm(k.reshape(b, l, GDN_QK_HEADS, GDN_HEAD_DIM)), rep, axis=2)
    v = v.reshape(b, l, GDN_V_HEADS, GDN_HEAD_DIM)
    beta = jax.nn.sigmoid(beta_in.astype(jnp.float32))
    g = -jnp.exp(a_log.astype(jnp.float32)) * jax.nn.softplus(a_in.astype(jnp.float32) + dt_bias.astype(jnp.float32))
    if s0 is None:
        s_init = jnp.zeros((b, GDN_V_HEADS, GDN_HEAD_DIM, GDN_HEAD_DIM), jnp.float32)
    else:
        s_init = s0.astype(jnp.float32)
    o, s_fin = _gated_delta_rule(q, k, v, g, beta, s_init)
    o = _rmsnorm(o, norm_g) * jax.nn.silu(z.astype(jnp.float32).reshape(b, l, GDN_V_HEADS, GDN_HEAD_DIM))
    y = o.reshape(b, l, GDN_V_DIM).astype(h.dtype) @ w_out
    return y, s_fin.astype(h.dtype), xp[:, l:]


def _hgrn_mixer(h, lb, w_in, norm_g, w_out, s0):
    b, l, _ = h.shape
    shp = (b, l, HG_HEADS, HG_HEAD_DIM)
    q, fz, i_in, gate = jnp.split((h @ w_in).astype(jnp.float32), 4, axis=-1)
    lb = lb.reshape(HG_HEADS, HG_HEAD_DIM)
    fz = fz.reshape(shp)
    log_f = jnp.logaddexp(jnp.log(lb), jnp.log1p(-lb) + jax.nn.log_sigmoid(fz))
    k = (1.0 - lb) * jax.nn.sigmoid(-fz)
    q = jax.nn.silu(q).reshape(shp)
    v = i_in.reshape(shp)
    if s0 is None:
        s_init = jnp.zeros((b, HG_HEADS, HG_HEAD_DIM, HG_HEAD_DIM), jnp.float32)
    else:
        s_init = s0.astype(jnp.float32)
    o, s_fin = _gated_linear_recurrence(q, k, v, log_f, s_init)
    o = _rmsnorm(o, norm_g) * jax.nn.silu(gate.reshape(shp))
    y = o.reshape(b, l, HG_DIM).astype(h.dtype) @ w_out
    return y, s_fin.astype(h.dtype)


def _trunk(x, pos0, cache_swa_k, cache_swa_v, state_gdn, state_gdn_conv, state_hgrn, w):
    new_k, new_v, new_gdn, new_conv, new_hgrn = [], [], [], [], []
    p_lb = jax.nn.softmax(w['hgrn_lb_logits'].astype(jnp.float32), axis=0)
    lb_all = jnp.cumsum(p_lb, axis=0) - p_lb[:1]
    for i in range(DEPTH):
        kind, j = i % N_MIXERS, i // N_MIXERS
        x = x + 0.5 * _swiglu(_rmsnorm(x, w['norm_ffn'][i, 0]), w['ffn_w_in'][i, 0], w['ffn_w_out'][i, 0])
        h = _rmsnorm(x, w['norm_mix'][i])
        if kind == 0:
            y, kb, vb = _swa_mixer(h, pos0, w['swa_w_in'][j], w['swa_w_out'][j], w['swa_sinks'][j],
                                   None if cache_swa_k is None else cache_swa_k[j],
                                   None if cache_swa_v is None else cache_swa_v[j])
            new_k.append(kb)
            new_v.append(vb)
        elif kind == 1:
            y, s_new, c_new = _gdn_mixer(h, w['gdn_w_in'][j], w['gdn_conv_w'][j], w['gdn_a_log'][j],
                                         w['gdn_dt_bias'][j], w['gdn_norm'][j], w['gdn_w_out'][j],
                                         None if state_gdn is None else state_gdn[j],
                                         None if state_gdn_conv is None else state_gdn_conv[j])
            new_gdn.append(s_new)
            new_conv.append(c_new)
        else:
            y, s_new = _hgrn_mixer(h, lb_all[i], w['hgrn_w_in'][j], w['hgrn_norm'][j], w['hgrn_w_out'][j],
                                   None if state_hgrn is None else state_hgrn[j])
            new_hgrn.append(s_new)
        x = x + y
        x = x + 0.5 * _swiglu(_rmsnorm(x, w['norm_ffn'][i, 1]), w['ffn_w_in'][i, 1], w['ffn_w_out'][i, 1])
    y_out = _rmsnorm(x, w['final_norm'])
    return (y_out, jnp.stack(new_k), jnp.stack(new_v), jnp.stack(new_gdn),
            jnp.stack(new_conv), jnp.stack(new_hgrn))


def setup_inputs(seed: int = 0) -> dict:
    key = jax.random.key(seed)
    ks = jax.random.split(key, 32)

    def nrm(i, shape, scale):
        return scale * jax.random.normal(ks[i], shape, jnp.float32)

    dt = jnp.exp(jax.random.uniform(ks[20], (N_GDN, GDN_V_HEADS), jnp.float32,
                                    math.log(1e-3), math.log(1e-1)))
    return {
        'x_prompt': nrm(0, (BATCH, SEQ, D_MODEL), 1.0),
        'x_sample': nrm(1, (DEC_BATCH, DEC_SEQ, D_MODEL), 1.0),
        'cache_swa_k': nrm(2, (N_SWA, DEC_BATCH, WINDOW, SWA_KV_HEADS, SWA_HEAD_DIM), 1.0),
        'cache_swa_v': nrm(3, (N_SWA, DEC_BATCH, WINDOW, SWA_KV_HEADS, SWA_HEAD_DIM), 1.0),
        'state_gdn': nrm(4, (N_GDN, DEC_BATCH, GDN_V_HEADS, GDN_HEAD_DIM, GDN_HEAD_DIM), 0.1),
        'state_gdn_conv': nrm(5, (N_GDN, DEC_BATCH, GDN_CONV - 1, GDN_CONV_DIM), 1.0),
        'state_hgrn': nrm(6, (N_HGRN, DEC_BATCH, HG_HEADS, HG_HEAD_DIM, HG_HEAD_DIM), 0.3),
        'norm_ffn': 1.0 + nrm(7, (DEPTH, 2, D_MODEL), 0.02),
        'ffn_w_in': nrm(8, (DEPTH, 2, D_MODEL, 2 * D_FF), D_MODEL ** -0.5),
        'ffn_w_out': nrm(9, (DEPTH, 2, D_FF, D_MODEL), D_FF ** -0.5),
        'norm_mix': 1.0 + nrm(10, (DEPTH, D_MODEL), 0.02),
        'swa_w_in': nrm(11, (N_SWA, D_MODEL, SWA_IN), D_MODEL ** -0.5),
        'swa_w_out': nrm(12, (N_SWA, SWA_HEADS * SWA_HEAD_DIM, D_MODEL), (SWA_HEADS * SWA_HEAD_DIM) ** -0.5),
        'swa_sinks': nrm(13, (N_SWA, SWA_HEADS), 0.5),
        'gdn_w_in': nrm(14, (N_GDN, D_MODEL, GDN_IN), D_MODEL ** -0.5),
        'gdn_conv_w': nrm(15, (N_GDN, GDN_CONV, GDN_CONV_DIM), 0.5),
        'gdn_a_log': jnp.log(jax.random.uniform(ks[21], (N_GDN, GDN_V_HEADS), jnp.float32, 1.0, 16.0)),
        'gdn_dt_bias': dt + jnp.log(-jnp.expm1(-dt)),
        'gdn_norm': 1.0 + nrm(16, (N_GDN, GDN_HEAD_DIM), 0.02),
        'gdn_w_out': nrm(17, (N_GDN, GDN_V_DIM, D_MODEL), GDN_V_DIM ** -0.5),
        'hgrn_w_in': nrm(18, (N_HGRN, D_MODEL, 4 * HG_DIM), D_MODEL ** -0.5),
        'hgrn_lb_logits': nrm(19, (DEPTH, HG_DIM), 0.5),
        'hgrn_norm': 1.0 + nrm(22, (N_HGRN, HG_HEAD_DIM), 0.02),
        'hgrn_w_out': nrm(23, (N_HGRN, HG_DIM, D_MODEL), HG_DIM ** -0.5),
        'final_norm': 1.0 + nrm(24, (D_MODEL,), 0.02),
    }


def reference(x_prompt, x_sample, cache_swa_k, cache_swa_v, state_gdn, state_gdn_conv, state_hgrn,
              norm_ffn, ffn_w_in, ffn_w_out, norm_mix, swa_w_in, swa_w_out, swa_sinks,
              gdn_w_in, gdn_conv_w, gdn_a_log, gdn_dt_bias, gdn_norm, gdn_w_out,
              hgrn_w_in, hgrn_lb_logits, hgrn_norm, hgrn_w_out, final_norm):
    w = dict(norm_ffn=norm_ffn, ffn_w_in=ffn_w_in, ffn_w_out=ffn_w_out, norm_mix=norm_mix,
             swa_w_in=swa_w_in, swa_w_out=swa_w_out, swa_sinks=swa_sinks,
             gdn_w_in=gdn_w_in, gdn_conv_w=gdn_conv_w, gdn_a_log=gdn_a_log, gdn_dt_bias=gdn_dt_bias,
             gdn_norm=gdn_norm, gdn_w_out=gdn_w_out, hgrn_w_in=hgrn_w_in, hgrn_lb_logits=hgrn_lb_logits,
             hgrn_norm=hgrn_norm, hgrn_w_out=hgrn_w_out, final_norm=final_norm)
    y_prompt, pk, pv, pg, pc, ph = _trunk(x_prompt, 0, None, None, None, None, None, w)
    y_sample, sk, sv, sg, sc, sh = _trunk(x_sample, PAST_LEN, cache_swa_k, cache_swa_v,
                                          state_gdn, state_gdn_conv, state_hgrn, w)
    return (y_prompt, y_sample, pk, pv, pg, pc, ph, sk, sv, sg, sc, sh)
```

```python
import numpy as np
import concourse.bass as bass
import concourse.mybir as mybir
from concourse.bass_utils import run_bass_kernel_spmd

F32 = mybir.dt.float32
BF16 = mybir.dt.bfloat16
AF = mybir.ActivationFunctionType
ALU = mybir.AluOpType
AX = mybir.AxisListType
SEM_CAP = 30000

NCORE = 8
D = 1024
NPR = 2048
NSB = 16
NSM = NSB * 4
TOK = NPR + NSM
TT = [(0, 512), (512, 512), (1024, 512), (1536, 512), (2048, 64)]
DFF = 2816
NJ = DFF // 128
EPS = 1e-6
DEPTH = 4


class Prog:
    def __init__(self, nc, n_dma_sems=24):
        self.nc = nc
        self.eng = {'pe': nc.tensor, 'dve': nc.vector, 'act': nc.scalar,
                    'pool': nc.gpsimd, 'sp': nc.sync}
        self.ops = []
        self.last_w = {}
        self.readers = {}
        self.n_dma_sems = n_dma_sems
        self.dma_rr = 0
        self.dma_last = [None] * n_dma_sems
        self.fence_ops = []
        self.plan = False

    def _deps(self, reads, writes, nofence):
        deps = set()
        for r in reads:
            if r in self.last_w:
                deps.add(self.last_w[r])
        for w in writes:
            if w in self.last_w:
                deps.add(self.last_w[w])
            deps.update(self.readers.get(w, ()))
        if not nofence:
            deps.update(self.fence_ops)
        return deps

    def _commit(self, idx, reads, writes):
        for r in reads:
            self.readers.setdefault(r, []).append(idx)
        for w in writes:
            self.last_w[w] = idx
            self.readers[w] = []

    def op(self, eng, fn, reads=(), writes=(), nofence=False):
        if self.plan:
            return
        psr = [r for r in reads if r.startswith('ps')]
        if psr:
            reads = [r for r in reads if not r.startswith('ps')]
            writes = list(writes) + psr
        deps = self._deps(reads, writes, nofence)
        idx = len(self.ops)
        self.ops.append(dict(eng=eng, fn=fn, deps=deps, dma=None, nofence=nofence))
        self._commit(idx, reads, writes)

    def dma(self, eng, fn, reads=(), writes=(), nofence=False):
        if self.plan:
            return
        deps = self._deps(reads, writes, nofence)
        s = self.dma_rr
        self.dma_rr = (self.dma_rr + 1) % self.n_dma_sems
        if self.dma_last[s] is not None:
            deps.add(self.dma_last[s])
        idx = len(self.ops)
        self.dma_last[s] = idx
        self.ops.append(dict(eng=eng, fn=fn, deps=deps, dma=s, nofence=nofence))
        self._commit(idx, reads, writes)

    def fence(self):
        if self.plan:
            return
        last = {}
        for i, o in enumerate(self.ops):
            if o['nofence']:
                continue
            if o['dma'] is not None:
                last[('d', i)] = i
            else:
                last[o['eng']] = i
        prev = set(self.fence_ops)
        keep = []
        for k, i in last.items():
            if isinstance(k, tuple) and i < self._fence_pos:
                continue
            keep.append(i)
        self.fence_ops = keep
        self._fence_pos = len(self.ops)

    _fence_pos = 0

    def emit(self):
        nc = self.nc
        ops = self.ops
        signal = [False] * len(ops)
        for i, o in enumerate(ops):
            by_eng = {}
            keep = []
            for j in o['deps']:
                p = ops[j]
                if p['dma'] is not None:
                    keep.append(j)
                    continue
                if p['eng'] == o['eng'] and o['eng'] == 'pe' and o['dma'] is None:
                    continue
                by_eng[p['eng']] = max(by_eng.get(p['eng'], -1), j)
            keep.extend(by_eng.values())
            o['deps'] = keep
            for j in keep:
                signal[j] = True
        eng_sems = {e: [] for e in self.eng}
        eng_cnt = {e: SEM_CAP for e in self.eng}
        dma_sems = [nc.alloc_semaphore(f"dq{s}") for s in range(self.n_dma_sems)]
        dma_cnt = [0] * self.n_dma_sems
        token = [None] * len(ops)
        waited = {e: {} for e in self.eng}
        nwait = 0
        for i, o in enumerate(ops):
            e = o['eng']
            E = self.eng[e]
            for j in sorted(o['deps']):
                sem, val, key = token[j]
                if waited[e].get(key, 0) < val:
                    E.wait_ge(sem, val)
                    waited[e][key] = val
                    nwait += 1
            inst = o['fn']()
            if o['dma'] is not None:
                s = o['dma']
                dma_cnt[s] += 16
                inst.then_inc(dma_sems[s], 16)
                token[i] = (dma_sems[s], dma_cnt[s], ('d', s))
            elif signal[i]:
                if eng_cnt[e] >= SEM_CAP:
                    eng_sems[e].append(nc.alloc_semaphore(f"e_{e}_{len(eng_sems[e])}"))
                    eng_cnt[e] = 0
                eng_cnt[e] += 1
                sem = eng_sems[e][-1]
                inst.then_inc(sem, 1)
                token[i] = (sem, eng_cnt[e], ('e', e, len(eng_sems[e])))
        return dict(n_ops=len(ops), n_wait=nwait, n_signal=sum(signal))


class Arena:
    def __init__(self, nc, words):
        self.t = nc.alloc_sbuf_tensor("arena", [128, words], F32)
        self.words = words
        self.top = 0
        self.peak = 0

    def mark(self):
        return self.top

    def release(self, m):
        self.top = m

    def f32(self, n):
        a = self.t[:, self.top:self.top + n]
        self.top += n
        self.peak = max(self.peak, self.top)
        assert self.top <= self.words, f"arena overflow {self.top} > {self.words}"
        return a

    def bf16(self, n):
        w = (n + 1) // 2
        a = self.t[:, self.top:self.top + w].bitcast(BF16)
        self.top += w
        self.peak = max(self.peak, self.top)
        assert self.top <= self.words, f"arena overflow {self.top} > {self.words}"
        return a[:, 0:n]


def _rope_tables():
    rot, theta = 16, 500000.0
    inv = (theta ** (-np.arange(0, rot, 2, dtype=np.float32) / rot)).astype(np.float32)
    pos = np.concatenate([np.arange(NPR, dtype=np.float32),
                          np.tile(8192.0 + np.arange(4, dtype=np.float32), NSB)]).astype(np.float32)
    ang = (pos[:, None] * inv[None, :]).astype(np.float32)
    cos, sin = np.cos(ang).astype(np.float32), np.sin(ang).astype(np.float32)
    C = np.ones((128, TOK), np.float32)
    S = np.zeros((128, TOK), np.float32)
    for p in range(128):
        d = p % 64
        if d < 8:
            C[p] = cos[:, d]
            S[p] = -sin[:, d]
        elif d < 16:
            C[p] = cos[:, d - 8]
            S[p] = sin[:, d - 8]
    Pm = np.zeros((128, 128), np.float32)
    for m in range(128):
        d = m % 64
        if d < 8:
            Pm[m + 8, m] = 1.0
        elif d < 16:
            Pm[m - 8, m] = 1.0
    return C, S, Pm


def _consts():
    C, S, Pm = _rope_tables()
    idx = np.arange(128)
    Mcur = (idx[:, None] <= idx[None, :]).astype(np.float32)
    Mprev = (idx[:, None] > idx[None, :]).astype(np.float32)
    Msamp = np.zeros((64, NSB, 4), np.float32)
    for b in range(NSB):
        for t2 in range(4):
            for t in range(4):
                if t2 <= t:
                    Msamp[b * 4 + t2, b, t] = 1.0
    blk = (idx[:, None] // 32) == (idx[None, :] // 32)
    Mblk = (blk & (idx[:, None] <= idx[None, :])).astype(np.float32)
    cm = np.zeros((128, 1024), np.float32)
    cm[:, 0:128] = np.eye(128, dtype=np.float32)
    cm[:, 128:256] = 1.0
    cm[:, 256:384] = Pm
    cm[:, 384:512] = Mcur
    cm[:, 512:640] = Mprev
    cm[0:64, 640:704] = Msamp.reshape(64, 64)
    cm[:, 768:896] = Mblk
    for c in range(4):
        cm[c * 32:(c + 1) * 32, 896 + c] = 1.0
    gm = np.zeros((128, 9 * 128), np.float32)
    ii, jj = idx[:, None], idx[None, :]
    gm[:, 0:128] = (ii // 8 == jj // 8)
    for li, sz_ in enumerate((8, 16, 32, 64)):
        Ms = ((ii // (2 * sz_) == jj // (2 * sz_)) & (ii % (2 * sz_) >= sz_) & (jj % (2 * sz_) < sz_)).astype(np.float32)
        gm[:, (1 + 2 * li) * 128:(2 + 2 * li) * 128] = Ms
        gm[:, (2 + 2 * li) * 128:(3 + 2 * li) * 128] = Ms.T
    seg = np.ones((128, TOK), np.float32)
    seg[:, 0:NPR:32] = 0.0
    seg[:, NPR:TOK:4] = 0.0
    return dict(rope_c=C, rope_s=S, cmat=cm, seg=seg, gmask=gm)


V_NFFN = 0
V_NMIX = 64
V_FIN = 96
V_CONV = 104
V_LB = 232
V_GN = 264
V_HN = 265
NV = 272


def build_program(stop_after=None, debug=False):
    nc = bass.Bass("TRN2", target_bir_lowering=False, dynamic_dma_scratch_size=512)
    P = Prog(nc)

    def din(name, shape):
        return nc.dram_tensor(name, list(shape), F32, kind="ExternalInput").ap()

    def dout(name, shape):
        return nc.dram_tensor(name, list(shape), F32, kind="ExternalOutput").ap()

    xT_in = din("xT", [128, 8, TOK])
    wflat_holder = {}
    vecT_in = din("vecT", [128, NV])
    cmat_in = din("cmat", [128, 1024])
    ropec_in = din("rope_c", [128, TOK])
    ropes_in = din("rope_s", [128, TOK])
    sink_in = din("sinks", [128, 32])
    kc_in = din("kcache", [2, 128, 2, NSB, 128])
    vc_in = din("vcache", [2, 128, NSB, 256])
    gdnvec_in = din("gdnvec", [128, 32])
    gmask_in = din("gmask", [128, 9 * 128])
    seg_in = din("seg", [128, TOK])
    convs_in = din("convs", [128, 32, NSB, 3])
    sg_in = din("sg", [NSB, 16, 128, 128])
    sh_in = din("sh", [NSB, 8, 128, 128])
    convp_out = dout("convp", [128, 32, 3])
    convs_out = dout("convs_o", [128, 32, NSB, 3])
    sg_out = dout("sg_o", [NSB, 16, 128, 128])
    gp_out = dout("gp_o", [16, 128, 128])
    sh_out = dout("sh_o", [NSB, 8, 128, 128])
    hp_out = dout("hp_o", [8, 128, 128])
    yT_out = dout("yT", [128, 8, TOK])
    kp_out = dout("kp", [2, 64, 4, 128])
    vp_out = dout("vp", [2, 128, 256])
    ks_out = dout("ks", [2, NSB, 4, 64, 128])
    vs_out = dout("vs", [2, NSB, 128, 256])

    x = nc.alloc_sbuf_tensor("x", [128, 8, TOK], F32)
    h = nc.alloc_sbuf_tensor("h", [128, 8, TOK], BF16)
    NSTAGE, NWB = 2, 3
    stage = [nc.alloc_sbuf_tensor(f"stage{i}", [128, 2048], F32) for i in range(NSTAGE)]
    wb = [nc.alloc_sbuf_tensor(f"wb{i}", [128, 2048], BF16) for i in range(NWB)]
    vecT = nc.alloc_sbuf_tensor("vecT_sb", [128, NV], F32)
    cm32 = nc.alloc_sbuf_tensor("cm32", [128, 1024], F32)
    cmbf = nc.alloc_sbuf_tensor("cmbf", [128, 1024], BF16)
    esink = nc.alloc_sbuf_tensor("esink", [128, 32], F32)
    sink32 = nc.alloc_sbuf_tensor("sink32", [128, 32], F32)
    ident32 = cm32[:, 0:128]
    identb = cmbf[:, 0:128]
    onesb = cmbf[:, 128:256]
    Pmb = cmbf[:, 256:384]
    Mcurb = cmbf[:, 384:512]
    Mprevb = cmbf[:, 512:640]
    Msampb = cmbf[0:64, 640:704]
    Mblkb = cmbf[:, 768:896]
    rowmask = cmbf[:, 896:900]
    ARENA_WORDS = 22780
    A = Arena(nc, ARENA_WORDS)
    ps = [nc.alloc_psum_tensor(f"ps{i}", [128, 512], F32) for i in range(8)]
    PS = [f"ps{i}" for i in range(8)]

    class WS:
        specs = []
        off = 0
        plan_list = []
        k_issue = 0
        k_get = 0
        DEPTH = 3

    def w_request(n, fn):
        if P.plan:
            WS.plan_list.append((n, fn))
            return None
        while WS.k_issue < len(WS.plan_list) and WS.k_issue <= WS.k_get + WS.DEPTH - 1:
            nn, ff = WS.plan_list[WS.k_issue]
            off = WS.off
            WS.specs.append((off, nn, ff))
            WS.off += nn
            s = WS.k_issue % NSTAGE
            b = WS.k_issue % NWB
            wf = wflat_holder['ap']
            P.dma('sp', lambda s=s, off=off, nn=nn: nc.sync.dma_start(out=stage[s][:, 0:nn], in_=wf[:, off:off + nn]),
                  writes=[f'stage{s}'], nofence=True)
            P.op('pool', lambda s=s, b=b, nn=nn: nc.gpsimd.tensor_copy(out=wb[b][:, 0:nn], in_=stage[s][:, 0:nn]),
                 reads=[f'stage{s}'], writes=[f'wb{b}'], nofence=True)
            WS.k_issue += 1
        b = WS.k_get % NWB
        WS.k_get += 1
        return wb[b], f'wb{b}'

    def mm(out, lhsT, rhs, start, stop, reads, writes, **kw):
        P.op('pe', lambda: nc.tensor.matmul(out, lhsT, rhs, start=start, stop=stop, **kw), reads=reads, writes=writes)

    def act(out, in_, func, reads, writes, **kw):
        P.op('act', lambda: nc.scalar.activation(out=out, in_=in_, func=func, **kw), reads=reads, writes=writes)

    def vcol(base, c):
        return vecT[:, base + c:base + c + 1]

    def rmsnorm(gbase, tag, final=False):
        NO = ARENA_WORDS - 1536
        sq = [A.t[:, NO:NO + 256].bitcast(BF16), A.t[:, NO + 256:NO + 512].bitcast(BF16)]
        lnt = A.t[:, NO + 512:NO + 1024]
        rstd = A.t[:, NO + 1024:NO + 1536]
        k = 0
        for ti, (t0, tn) in enumerate(TT):
            for c in range(8):
                b = k % 2
                k += 1
                act(sq[b][:, 0:tn], x[:, c, t0:t0 + tn], AF.Square, reads=[f'x{c}.{ti}'], writes=[f'sq{b}'])
                mm(ps[6][:, 0:tn], onesb, sq[b][:, 0:tn], c == 0, c == 7, reads=[f'sq{b}'], writes=[PS[6]])
            act(lnt[:, 0:tn], ps[6][:, 0:tn], AF.Ln, reads=[PS[6]], writes=['lnt'], scale=1.0 / D, bias=EPS_AP)
            act(rstd[:, 0:tn], lnt[:, 0:tn], AF.Exp, reads=['lnt'], writes=['rstd'], scale=-0.5)
            for c in range(8):
                dst = x if final else h
                P.op('dve', lambda c=c, t0=t0, tn=tn, dst=dst: nc.vector.scalar_tensor_tensor(
                    out=dst[:, c, t0:t0 + tn], in0=x[:, c, t0:t0 + tn], scalar=vcol(gbase, c), in1=rstd[:, 0:tn],
                    op0=ALU.mult, op1=ALU.mult), reads=[f'x{c}.{ti}', 'rstd'], writes=[f'{"x" if final else "h"}{c}.{ti}'])

    def ffn(i, a):
        rmsnorm(V_NFFN + (i * 2 + a) * 8, f'ffn{i}{a}')
        m0 = A.mark()
        NSPLIT = 2
        per = NJ // NSPLIT
        actb = A.bf16(per * TOK).rearrange("p (j t) -> p j t", j=per)
        sg = [A.f32(512) for _ in range(2)]
        kk = 0
        for sp in range(NSPLIT):
            for jl in range(per):
                j = sp * per + jl

                def fn_in(inp, i=i, a=a, j=j):
                    w = inp['ffn_w_in'][i, a]
                    g = w[:, j * 128:(j + 1) * 128].reshape(8, 128, 128)
                    u = w[:, DFF + j * 128:DFF + (j + 1) * 128].reshape(8, 128, 128)
                    return np.concatenate([g, u], axis=2).transpose(1, 0, 2).reshape(128, 2048)
                got = w_request(2048, fn_in)
                if got is None:
                    continue
                wt, wkey = got
                wv = wt[:, 0:2048].rearrange("p (k n) -> p k n", k=8)
                for ti, (t0, tn) in enumerate(TT):
                    pb = (kk % 2) * 2
                    sb_ = kk % 2
                    kk += 1
                    for kc in range(8):
                        mm(ps[pb][:, 0:tn], wv[:, kc, 0:128], h[:, kc, t0:t0 + tn], kc == 0, kc == 7,
                           reads=[wkey, f'h{kc}.{ti}'], writes=[PS[pb]])
                    for kc in range(8):
                        mm(ps[pb + 1][:, 0:tn], wv[:, kc, 128:256], h[:, kc, t0:t0 + tn], kc == 0, kc == 7,
                           reads=[wkey, f'h{kc}.{ti}'], writes=[PS[pb + 1]])
                    act(sg[sb_][:, 0:tn], ps[pb][:, 0:tn], AF.Silu, reads=[PS[pb]], writes=[f'sg{sb_}'])
                    P.op('dve', lambda jl=jl, t0=t0, tn=tn, sb_=sb_, pb=pb: nc.vector.tensor_tensor(
                        out=actb[:, jl, t0:t0 + tn], in0=sg[sb_][:, 0:tn], in1=ps[pb + 1][:, 0:tn], op=ALU.mult),
                        reads=[f'sg{sb_}', PS[pb + 1]], writes=[f'act{jl}.{ti}'])
            for m in range(8):
                def fn_out(inp, i=i, a=a, sp=sp, m=m):
                    w = inp['ffn_w_out'][i, a]
                    blk = w[sp * per * 128:(sp + 1) * per * 128, m * 128:(m + 1) * 128]
                    return blk.reshape(per, 128, 128).transpose(1, 0, 2).reshape(128, per * 128)
                got = w_request(per * 128, fn_out)
                if got is None:
                    continue
                wt, wkey = got
                wv = wt[:, 0:per * 128].rearrange("p (j n) -> p j n", j=per)
                for ti, (t0, tn) in enumerate(TT):
                    pb = 4 + (kk % 2)
                    kk += 1
                    for jl in range(per):
                        mm(ps[pb][:, 0:tn], wv[:, jl, :], actb[:, jl, t0:t0 + tn], jl == 0, jl == per - 1,
                           reads=[wkey, f'act{jl}.{ti}'], writes=[PS[pb]])
                    P.op('dve', lambda m=m, t0=t0, tn=tn, pb=pb: nc.vector.scalar_tensor_tensor(
                        out=x[:, m, t0:t0 + tn], in0=ps[pb][:, 0:tn], scalar=0.5, in1=x[:, m, t0:t0 + tn],
                        op0=ALU.mult, op1=ALU.add), reads=[PS[pb], f'x{m}.{ti}'], writes=[f'x{m}.{ti}'])
        A.release(m0)

    def swa(i, j):
        rmsnorm(V_NMIX + i * 8, f'mix{i}')
        P.fence()
        m0 = A.mark()
        qT = A.bf16(8 * TOK).rearrange("p (c t) -> p c t", c=8)
        kT = A.bf16(2 * TOK).rearrange("p (c t) -> p c t", c=2)
        vtm = A.bf16(17 * 256).rearrange("p (b n) -> p b n", b=17)
        k32 = A.f32(2 * 192).rearrange("p (c t) -> p c t", c=2)
        v32 = A.f32(2 * 256).rearrange("p (b n) -> p b n", b=2)
        m1 = A.mark()
        ropec = A.f32(TOK)
        ropes = A.f32(TOK)
        P.dma('sp', lambda: nc.sync.dma_start(out=ropec, in_=ropec_in), writes=['ropec'])
        P.dma('sp', lambda: nc.sync.dma_start(out=ropes, in_=ropes_in), writes=['ropes'])
        qraw = [A.bf16(512) for _ in range(2)]
        t1 = [A.f32(512) for _ in range(2)]
        t2 = [A.f32(512) for _ in range(2)]
        win = inp_swa_in = None

        def qcols(c):
            kc, g = c // 4, c % 4
            h0 = (2 * kc) * 4 + g
            h1 = (2 * kc + 1) * 4 + g
            return np.concatenate([np.arange(h0 * 64, h0 * 64 + 64), np.arange(h1 * 64, h1 * 64 + 64)])

        def wblock(cols_list):
            def fn(inp, j=j, cols_list=cols_list):
                w = inp['swa_w_in'][j]
                cols = np.concatenate(cols_list)
                blk = w[:, cols].reshape(8, 128, len(cols))
                return blk.transpose(1, 0, 2).reshape(128, 8 * len(cols))
            return fn

        STG = 9
        SKIP = ''
        kk = 0
        groups = [[('q', 0), ('q', 1)], [('q', 2), ('q', 3)], [('q', 4), ('q', 5)], [('q', 6), ('q', 7)],
                  [('k', 0), ('k', 1)]]
        for grp in groups:
            cols_list = []
            for kind, c in grp:
                cols_list.append(qcols(c) if kind == 'q' else 1024 + np.arange(c * 128, (c + 1) * 128))
            got = w_request(2048, wblock(cols_list))
            if got is None or 'q' in SKIP:
                continue
            wt, wkey = got
            wv = wt[:, 0:2048].rearrange("p (k n) -> p k n", k=8)
            for gi, (kind, c) in enumerate(grp):
                for ti, (t0, tn) in enumerate(TT):
                    b = kk % 2
                    kk += 1
                    pa, pp = b * 2, b * 2 + 1
                    for kc in range(8):
                        mm(ps[pa][:, 0:tn], wv[:, kc, gi * 128:(gi + 1) * 128], h[:, kc, t0:t0 + tn], kc == 0, kc == 7,
                           reads=[wkey, f'h{kc}.{ti}'], writes=[PS[pa]])
                    P.op('act', lambda b=b, pa=pa, tn=tn: nc.scalar.copy(out=qraw[b][:, 0:tn], in_=ps[pa][:, 0:tn]),
                         reads=[PS[pa]], writes=[f'qraw{b}'])
                    if 'r' in SKIP:
                        continue
                    mm(ps[pp][:, 0:tn], Pmb, qraw[b][:, 0:tn], True, True, reads=[f'qraw{b}'], writes=[PS[pp]])
                    P.op('dve', lambda b=b, pa=pa, t0=t0, tn=tn: nc.vector.tensor_tensor(
                        out=t1[b][:, 0:tn], in0=ps[pa][:, 0:tn], in1=ropec[:, t0:t0 + tn], op=ALU.mult),
                        reads=[PS[pa], 'ropec'], writes=[f't1{b}'])
                    P.op('dve', lambda b=b, pp=pp, t0=t0, tn=tn: nc.vector.tensor_tensor(
                        out=t2[b][:, 0:tn], in0=ps[pp][:, 0:tn], in1=ropes[:, t0:t0 + tn], op=ALU.mult),
                        reads=[PS[pp], 'ropes'], writes=[f't2{b}'])
                    if 'p' in SKIP:
                        continue
                    dst = qT[:, c, t0:t0 + tn] if kind == 'q' else kT[:, c, t0:t0 + tn]
                    dkey = f'{kind}T{c}.{ti}'
                    P.op('pool', lambda b=b, dst=dst, tn=tn: nc.gpsimd.tensor_tensor(
                        out=dst, in0=t1[b][:, 0:tn], in1=t2[b][:, 0:tn], op=ALU.add),
                        reads=[f't1{b}', f't2{b}'], writes=[dkey])
                    if kind == 'k' and ti >= 3:
                        lo = 1920 - t0 if ti == 3 else 0
                        dlo = 0 if ti == 3 else 128
                        n = 128 if ti == 3 else 64
                        P.op('pool', lambda b=b, c=c, lo=lo, dlo=dlo, n=n: nc.gpsimd.tensor_tensor(
                            out=k32[:, c, dlo:dlo + n], in0=t1[b][:, lo:lo + n], in1=t2[b][:, lo:lo + n], op=ALU.add),
                            reads=[f't1{b}', f't2{b}'], writes=[f'k32.{c}'])
        def fn_v(inp, j=j):
            w = inp['swa_w_in'][j][:, 1280:1536].reshape(8, 128, 256)
            return w.transpose(1, 0, 2).reshape(128, 2048)
        got = w_request(2048, fn_v)
        if got is not None and 'v' not in SKIP:
            wt, wkey = got
            wv = wt[:, 0:2048].rearrange("p (k n) -> p k n", k=8)
            for blk in range(17):
                t0 = blk * 128
                tn = 128 if blk < 16 else 64
                ti = min(t0 // 512, 4)
                pa = 4 + blk % 2
                for kc in range(8):
                    mm(ps[pa][0:tn, 0:256], h[:, kc, t0:t0 + tn], wv[:, kc, :], kc == 0, kc == 7,
                       reads=[wkey, f'h{kc}.{ti}'], writes=[PS[pa]])
                P.op('act', lambda blk=blk, pa=pa, tn=tn: nc.scalar.copy(out=vtm[0:tn, blk, :], in_=ps[pa][0:tn, 0:256]),
                     reads=[PS[pa]], writes=[f'vtm{blk}'])
                if blk >= 15:
                    P.op('dve', lambda blk=blk, pa=pa, tn=tn: nc.vector.tensor_copy(out=v32[0:tn, blk - 15, :], in_=ps[pa][0:tn, 0:256]),
                         reads=[PS[pa]], writes=[f'v32.{blk - 15}'])
        P.fence()
        if STG < 2:
            A.release(m0)
            return
        for c in range(2):
            for hf in range(2):
                kvh = 2 * c + hf
                P.dma('sp', lambda c=c, hf=hf, kvh=kvh: nc.sync.dma_start(
                    out=kp_out[j, :, kvh, :], in_=k32[hf * 64:(hf + 1) * 64, c, 0:128]), reads=[f'k32.{c}'], writes=['kp_out'])
                P.dma('sp', lambda c=c, hf=hf, kvh=kvh: nc.sync.dma_start(
                    out=ks_out[j, :, kvh, :, 124:128].rearrange("b d t -> d b t"),
                    in_=k32[hf * 64:(hf + 1) * 64, c, 128:192].rearrange("p (b t) -> p b t", t=4)),
                    reads=[f'k32.{c}'], writes=['ks_out'])
                P.dma('sp', lambda c=c, hf=hf, kvh=kvh: nc.sync.dma_start(
                    out=ks_out[j, :, kvh, :, 0:124].rearrange("b d t -> d b t"),
                    in_=kc_in[j, hf * 64:(hf + 1) * 64, c, :, 4:128]), writes=['ks_out'])
        P.dma('sp', lambda: nc.sync.dma_start(out=vp_out[j], in_=v32[:, 0, :]), reads=['v32.0'], writes=['vp_out'])
        for b in range(NSB):
            P.dma('sp', lambda b=b: nc.sync.dma_start(out=vs_out[j, b, 124:128, :], in_=v32[b * 4:(b + 1) * 4, 1, :]),
                  reads=['v32.1'], writes=['vs_out'])
        P.dma('sp', lambda: nc.sync.dma_start(out=vs_out[j, :, 0:124, :].rearrange("b k n -> k b n"),
                                              in_=vc_in[j, 4:128, :, :]), writes=['vs_out'])
        A.release(m1)
        if STG < 3:
            A.release(m0)
            return
        pT = [A.bf16(1024) for _ in range(2)]
        lnd = [A.f32(512) for _ in range(2)]
        rden = [A.f32(512) for _ in range(2)]
        scale = 64 ** -0.5
        it = 0
        for qb in range(16):
            ti = qb // 4
            for kvh in range(4):
                kc, hf = kvh // 2, kvh % 2
                lo, hi = hf * 64, (hf + 1) * 64
                b = it % 2
                it += 1
                pS = [ps[b * 2], ps[b * 2 + 1]]
                pO, pD = ps[4 + b], ps[6 + b]
                kbs = [qb - 1, qb] if qb > 0 else [qb]
                for kb in kbs:
                    slot = 0 if kb == qb - 1 else 1
                    for g in range(4):
                        c = kc * 4 + g
                        mm(pS[slot][:, g * 128:(g + 1) * 128], kT[lo:hi, kc, kb * 128:(kb + 1) * 128],
                           qT[lo:hi, c, qb * 128:(qb + 1) * 128], True, True,
                           reads=[f'kT{kc}.{kb // 4}', f'qT{c}.{ti}'], writes=[PS[b * 2 + slot]])
                for kb in kbs:
                    slot = 0 if kb == qb - 1 else 1
                    act(pT[b][:, slot * 512:(slot + 1) * 512], pS[slot][:, :], AF.Exp,
                        reads=[PS[b * 2 + slot]], writes=[f'pT{b}.{slot}'], scale=scale)
                    M = Mprevb if slot == 0 else Mcurb
                    P.op('pool', lambda b=b, slot=slot, M=M: nc.gpsimd.tensor_tensor(
                        out=pT[b][:, slot * 512:(slot + 1) * 512].rearrange("p (g n) -> p g n", g=4),
                        in0=pT[b][:, slot * 512:(slot + 1) * 512].rearrange("p (g n) -> p g n", g=4),
                        in1=M.unsqueeze(1).broadcast_to([128, 4, 128]), op=ALU.mult),
                        reads=[f'pT{b}.{slot}'], writes=[f'pT{b}.{slot}'])
                for g in range(4):
                    hq = kvh * 4 + g
                    for n_, kb in enumerate(kbs):
                        slot = 0 if kb == qb - 1 else 1
                        mm(pO[lo:hi, g * 128:(g + 1) * 128], vtm[:, kb, kvh * 64:(kvh + 1) * 64],
                           pT[b][:, slot * 512 + g * 128: slot * 512 + (g + 1) * 128], n_ == 0, n_ == len(kbs) - 1,
                           reads=[f'vtm{kb}', f'pT{b}.{slot}'], writes=[PS[4 + b]])
                    for n_, kb in enumerate(kbs):
                        slot = 0 if kb == qb - 1 else 1
                        mm(pD[lo:hi, g * 128:(g + 1) * 128], onesb[:, 0:64],
                           pT[b][:, slot * 512 + g * 128: slot * 512 + (g + 1) * 128], n_ == 0, n_ == len(kbs) - 1,
                           reads=[f'pT{b}.{slot}'], writes=[PS[6 + b]])
                    act(lnd[b][lo:hi, g * 128:(g + 1) * 128], pD[lo:hi, g * 128:(g + 1) * 128], AF.Ln,
                        reads=[PS[6 + b], 'esink'], writes=[f'lnd{b}'], bias=esink[lo:hi, j * 16 + hq:j * 16 + hq + 1])
                act(rden[b][lo:hi, :], lnd[b][lo:hi, :], AF.Exp, reads=[f'lnd{b}'], writes=[f'rden{b}'], scale=-1.0)
                P.op('dve', lambda b=b, lo=lo, hi=hi, kc=kc, qb=qb, pO=pO: nc.vector.tensor_tensor(
                    out=h[lo:hi, kc * 4:kc * 4 + 4, qb * 128:(qb + 1) * 128],
                    in0=pO[lo:hi, :].rearrange("p (g n) -> p g n", g=4),
                    in1=rden[b][lo:hi, :].rearrange("p (g n) -> p g n", g=4), op=ALU.mult),
                    reads=[PS[4 + b], f'rden{b}'], writes=[f'h{kc * 4 + g_}.{ti}' for g_ in range(4)])
        P.fence()
        A.release(m1)
        if STG < 4:
            A.release(m0)
            return
        kcb = A.bf16(2 * NSB * 128).rearrange("p (c b k) -> p c b k", c=2, b=NSB)
        vcb = A.bf16(NSB * 256).rearrange("p (b n) -> p b n", b=NSB)
        for c in range(2):
            s = c % NSTAGE
            P.dma('sp', lambda c=c, s=s: nc.sync.dma_start(out=stage[s][:, 0:2048], in_=kc_in[j, :, c, :, :].rearrange("p b k -> p (b k)")),
                  writes=[f'stage{s}'])
            P.op('pool', lambda c=c, s=s: nc.gpsimd.tensor_copy(out=kcb[:, c, :, :].rearrange("p b k -> p (b k)"), in_=stage[s][:, 0:2048]),
                 reads=[f'stage{s}'], writes=['kcb'])
        for hb in range(2):
            s = hb % NSTAGE
            P.dma('sp', lambda hb=hb, s=s: nc.sync.dma_start(out=stage[s][:, 0:2048], in_=vc_in[j, :, hb * 8:(hb + 1) * 8, :].rearrange("p b n -> p (b n)")),
                  writes=[f'stage{s}'])
            P.op('pool', lambda hb=hb, s=s: nc.gpsimd.tensor_copy(out=vcb[:, hb * 8:(hb + 1) * 8, :].rearrange("p b n -> p (b n)"), in_=stage[s][:, 0:2048]),
                 reads=[f'stage{s}'], writes=['vcb'])
        pTc = A.bf16(1024)
        pTn = A.bf16(1024)
        lnd2 = A.f32(1024)
        rden2 = A.f32(1024)
        pSc = [ps[0], ps[1]]
        pSn = [ps[2], ps[3]]
        S0 = NPR
        for b_ in range(NSB):
            for kvh in range(4):
                kc, hf = kvh // 2, kvh % 2
                lo, hi = hf * 64, (hf + 1) * 64
                col = (b_ * 4 + kvh) * 16
                bank, cin = col // 512, col % 512
                for g in range(4):
                    c = kc * 4 + g
                    mm(pSc[bank][:, cin + g * 4:cin + g * 4 + 4], kcb[lo:hi, kc, b_, :],
                       qT[lo:hi, c, S0 + b_ * 4:S0 + b_ * 4 + 4], True, True,
                       reads=['kcb', f'qT{c}.4'], writes=[PS[bank]])
                    mm(pSn[bank][0:64, cin + g * 4:cin + g * 4 + 4], kT[lo:hi, kc, S0:S0 + 64],
                       qT[lo:hi, c, S0 + b_ * 4:S0 + b_ * 4 + 4], True, True,
                       reads=[f'kT{kc}.4', f'qT{c}.4'], writes=[PS[2 + bank]])
        for bank in range(2):
            act(pTc[:, bank * 512:(bank + 1) * 512], pSc[bank][:, :], AF.Exp, reads=[PS[bank]], writes=['pTc'], scale=scale)
            act(pTn[0:64, bank * 512:(bank + 1) * 512], pSn[bank][0:64, :], AF.Exp, reads=[PS[2 + bank]], writes=['pTn'], scale=scale)
        P.op('pool', lambda: nc.gpsimd.tensor_tensor(
            out=pTc.rearrange("p (a t) -> p a t", t=4), in0=pTc.rearrange("p (a t) -> p a t", t=4),
            in1=Mprevb[:, 0:4].unsqueeze(1).broadcast_to([128, 256, 4]), op=ALU.mult), reads=['pTc'], writes=['pTc'])
        P.op('pool', lambda: nc.gpsimd.tensor_tensor(
            out=pTn[0:64, :].rearrange("p (b a t) -> p b a t", b=NSB, t=4),
            in0=pTn[0:64, :].rearrange("p (b a t) -> p b a t", b=NSB, t=4),
            in1=Msampb.rearrange("p (b t) -> p b t", t=4).unsqueeze(2).broadcast_to([64, NSB, 16, 4]), op=ALU.mult),
            reads=['pTn'], writes=['pTn'])
        pO2, pD2 = ps[4], ps[5]
        for b_ in range(NSB):
            for kvh in range(4):
                kc, hf = kvh // 2, kvh % 2
                lo, hi = hf * 64, (hf + 1) * 64
                col = (b_ * 4 + kvh) * 16
                oc = (b_ * 2 + kc) * 16
                mm(pO2[lo:hi, oc:oc + 16], vcb[:, b_, kvh * 64:(kvh + 1) * 64], pTc[:, col:col + 16], True, False,
                   reads=['vcb', 'pTc'], writes=[PS[4]])
                mm(pO2[lo:hi, oc:oc + 16], vtm[0:64, 16, kvh * 64:(kvh + 1) * 64], pTn[0:64, col:col + 16], False, True,
                   reads=['vtm16', 'pTn'], writes=[PS[4]])
                mm(pD2[lo:hi, oc:oc + 16], onesb[:, 0:64], pTc[:, col:col + 16], True, False, reads=['pTc'], writes=[PS[5]])
                mm(pD2[lo:hi, oc:oc + 16], onesb[0:64, 0:64], pTn[0:64, col:col + 16], False, True, reads=['pTn'], writes=[PS[5]])
        for kvh in range(4):
            kc, hf = kvh // 2, kvh % 2
            lo, hi = hf * 64, (hf + 1) * 64
            for g in range(4):
                hq = kvh * 4 + g
                act(lnd2[lo:hi, 0:512].rearrange("p (b k g t) -> p b k g t", b=NSB, k=2, g=4)[:, :, kc, g, :],
                    pD2[lo:hi, :].rearrange("p (b k g t) -> p b k g t", b=NSB, k=2, g=4)[:, :, kc, g, :], AF.Ln,
                    reads=[PS[5], 'esink'], writes=['lnd2'], bias=esink[lo:hi, j * 16 + hq:j * 16 + hq + 1])
        act(rden2[:, 0:512], lnd2[:, 0:512], AF.Exp, reads=['lnd2'], writes=['rden2'], scale=-1.0)
        for kc in range(2):
            P.op('dve', lambda kc=kc: nc.vector.tensor_tensor(
                out=h[:, kc * 4:kc * 4 + 4, S0:S0 + 64].rearrange("p g (b t) -> p b g t", t=4),
                in0=pO2[:, :].rearrange("p (b k g t) -> p b k g t", b=NSB, k=2, g=4)[:, :, kc, :, :],
                in1=rden2[:, 0:512].rearrange("p (b k g t) -> p b k g t", b=NSB, k=2, g=4)[:, :, kc, :, :], op=ALU.mult),
                reads=[PS[4], 'rden2'], writes=[f'h{kc * 4 + g_}.4' for g_ in range(4)])
        P.fence()
        for mp in range(4):
            def fn_o(inp, j=j, mp=mp):
                w = inp['swa_w_out'][j]
                rows = np.concatenate([qcols(c) for c in range(8)])
                blk = w[rows][:, mp * 256:(mp + 1) * 256].reshape(8, 128, 256)
                return blk.transpose(1, 0, 2).reshape(128, 2048)
            got = w_request(2048, fn_o)
            if got is None:
                continue
            wt, wkey = got
            wv = wt[:, 0:2048].rearrange("p (k n) -> p k n", k=8)
            for mi in range(2):
                m = mp * 2 + mi
                for ti, (t0, tn) in enumerate(TT):
                    pb = (mi * 5 + ti) % 2
                    for kc in range(8):
                        mm(ps[pb][:, 0:tn], wv[:, kc, mi * 128:(mi + 1) * 128], h[:, kc, t0:t0 + tn], kc == 0, kc == 7,
                           reads=[wkey, f'h{kc}.{ti}'], writes=[PS[pb]])
                    P.op('dve', lambda m=m, t0=t0, tn=tn, pb=pb: nc.vector.tensor_tensor(
                        out=x[:, m, t0:t0 + tn], in0=ps[pb][:, 0:tn], in1=x[:, m, t0:t0 + tn], op=ALU.add),
                        reads=[PS[pb], f'x{m}.{ti}'], writes=[f'x{m}.{ti}'])
        A.release(m0)
        P.fence()

    def gated_norm_out(o_ps, C, cols, ncol_key, gcol, siluz, og, ogkey, tmpk):
        o32, sqb, lnr, rn, t3 = tmpk
        P.op('act', lambda: nc.scalar.copy(out=o32[:, 0:C], in_=o_ps), reads=[ncol_key], writes=['gn_o32'])
        act(sqb[:, 0:C], o32[:, 0:C], AF.Square, reads=['gn_o32'], writes=['gn_sq'])
        mm(ps[7][:, 0:C], onesb, sqb[:, 0:C], True, True, reads=['gn_sq'], writes=[PS[7]])
        act(lnr[:, 0:C], ps[7][:, 0:C], AF.Ln, reads=[PS[7]], writes=['gn_ln'], scale=1.0 / 128, bias=EPS_AP)
        act(rn[:, 0:C], lnr[:, 0:C], AF.Exp, reads=['gn_ln'], writes=['gn_rn'], scale=-0.5)
        P.op('dve', lambda: nc.vector.scalar_tensor_tensor(out=t3[:, 0:C], in0=o32[:, 0:C], scalar=gcol, in1=rn[:, 0:C],
                                                           op0=ALU.mult, op1=ALU.mult), reads=['gn_o32', 'gn_rn'], writes=['gn_t3'])
        P.op('pool', lambda: nc.gpsimd.tensor_tensor(out=og[:, cols], in0=t3[:, 0:C], in1=siluz[:, cols], op=ALU.mult),
             reads=['gn_t3', 'siluz'], writes=[ogkey])

    def out_proj_acc(og_list, wfn_rows, ogkeys):
        nk = len(og_list)
        mper = min(8, 2048 // (nk * 128))
        for m0_ in range(0, 8, mper):
            def fn(inp, m0_=m0_):
                rows = wfn_rows(inp)
                blk = rows[:, m0_ * 128:(m0_ + mper) * 128].reshape(nk, 128, mper * 128)
                return blk.transpose(1, 0, 2).reshape(128, nk * mper * 128)
            got = w_request(nk * mper * 128, fn)
            if got is None:
                continue
            wt, wkey = got
            wv = wt[:, 0:nk * mper * 128].rearrange("p (k n) -> p k n", k=nk)
            for mi in range(mper):
                m = m0_ + mi
                for ti, (t0, tn) in enumerate(TT):
                    pb = (m * 5 + ti) % 2
                    for kc in range(nk):
                        mm(ps[pb][:, 0:tn], wv[:, kc, mi * 128:(mi + 1) * 128], og_list[kc][:, t0:t0 + tn], kc == 0, kc == nk - 1,
                           reads=[wkey, ogkeys[kc]], writes=[PS[pb]])
                    P.op('dve', lambda m=m, t0=t0, tn=tn, pb=pb: nc.vector.tensor_tensor(
                        out=x[:, m, t0:t0 + tn], in0=ps[pb][:, 0:tn], in1=x[:, m, t0:t0 + tn], op=ALU.add),
                        reads=[PS[pb], f'x{m}.{ti}'], writes=[f'x{m}.{ti}'])

    def proj_fm(wv, wkey, col0, dst_fn):
        for ti, (t0, tn) in enumerate(TT):
            pb = ti % 2
            for kc in range(8):
                mm(ps[pb][:, 0:tn], wv[:, kc, col0:col0 + 128], h[:, kc, t0:t0 + tn], kc == 0, kc == 7,
                   reads=[wkey, f'h{kc}.{ti}'], writes=[PS[pb]])
            dst_fn(ti, t0, tn, ps[pb][:, 0:tn], PS[pb])

    def gdn(i, j):
        rmsnorm(V_NMIX + i * 8, f'mix{i}')
        P.fence()
        m0 = A.mark()
        cm32 = A.f32(1024)
        P.dma('sp', lambda: nc.sync.dma_start(out=cm32, in_=cmat_in), writes=['cm32'])
        ident32, ones32 = cm32[:, 0:128], cm32[:, 128:256]
        Mcur32, Mprev32 = cm32[:, 384:512], cm32[:, 512:640]
        gmaskb = A.bf16(9 * 128)
        gv = A.f32(32)
        P.dma('sp', lambda: nc.sync.dma_start(out=gv, in_=gdnvec_in), writes=['gv'])
        nega = A.f32(16)
        act(nega, gv[:, 16:32], AF.Exp, reads=['gv'], writes=['nega'])
        P.op('dve', lambda: nc.vector.tensor_scalar(out=nega, in0=nega, scalar1=-1.0, scalar2=None, op0=ALU.mult), reads=['nega'], writes=['nega'])
        bg = A.f32(16 * 32).rearrange("p (b n) -> p b n", b=16)
        bgs = A.f32(16 * 32).rearrange("p (b n) -> p b n", b=16)

        def fn_ba(inp):
            w = inp['gdn_w_in'][0][:, 6144:6176].reshape(8, 128, 32)
            return w.transpose(1, 0, 2).reshape(128, 256)
        got = w_request(256, fn_ba)
        if got is not None:
            wt, wkey = got
            wv = wt[:, 0:256].rearrange("p (k n) -> p k n", k=8)
            for blk in range(16):
                for kc in range(8):
                    mm(ps[0][:, blk * 32:(blk + 1) * 32], h[:, kc, blk * 128:(blk + 1) * 128], wv[:, kc, :], kc == 0, kc == 7,
                       reads=[wkey, f'h{kc}.{blk // 4}'], writes=[PS[0]])
            for b_ in range(NSB):
                for kc in range(8):
                    mm(ps[1][0:4, b_ * 32:(b_ + 1) * 32], h[:, kc, NPR + b_ * 4:NPR + b_ * 4 + 4], wv[:, kc, :], kc == 0, kc == 7,
                       reads=[wkey, f'h{kc}.4'], writes=[PS[1]])
            for (src, dst, np_, key, pk) in ((ps[0], bg, 128, 'bg', PS[0]), (ps[1], bgs, 4, 'bgs', PS[1])):
                sv = src[0:np_, :].rearrange("p (b n) -> p b n", b=16)
                dv = dst[0:np_]
                act(dv[:, :, 0:16], sv[:, :, 0:16], AF.Sigmoid, reads=[pk], writes=[key])
                P.op('dve', lambda sv=sv, dv=dv, np_=np_: nc.vector.tensor_tensor(
                    out=dv[:, :, 16:32], in0=sv[:, :, 16:32], in1=gv[0:np_, 0:16].unsqueeze(1).broadcast_to([np_, 16, 16]), op=ALU.add),
                    reads=[pk, 'gv'], writes=[key])
                act(dv[:, :, 16:32], dv[:, :, 16:32], AF.Exp, reads=[key], writes=[key])
                act(dv[:, :, 16:32], dv[:, :, 16:32], AF.Ln, reads=[key], writes=[key], bias=ONE_AP[0:np_])
                P.op('dve', lambda dv=dv, np_=np_: nc.vector.tensor_tensor(
                    out=dv[:, :, 16:32], in0=dv[:, :, 16:32], in1=nega[0:np_].unsqueeze(1).broadcast_to([np_, 16, 16]), op=ALU.mult),
                    reads=[key, 'nega'], writes=[key])
        m1 = A.mark()
        gmask = A.f32(9 * 128)
        P.dma('sp', lambda: nc.sync.dma_start(out=gmask, in_=gmask_in), writes=['gmask'])
        P.op('pool', lambda: nc.gpsimd.tensor_copy(out=gmaskb, in_=gmask), reads=['gmask'], writes=['gmb'])
        P.fence()
        for kh in range(8):
            A.release(m1)
            gdn_group(i, kh, cm32, bg, bgs, gmaskb)
        A.release(m0)
        P.fence()

    def gdn_group(i, kh, cm32, bg, bgs, gmaskb):
        ident32, ones32 = cm32[:, 0:128], cm32[:, 128:256]
        Mcur32, Mprev32 = cm32[:, 384:512], cm32[:, 512:640]
        heads = [2 * kh, 2 * kh + 1]
        qT = A.bf16(TOK)
        kT = A.bf16(TOK)
        vTs = [A.bf16(64) for _ in range(2)]
        sz = [A.bf16(TOK) for _ in range(2)]
        og = [A.bf16(TOK) for _ in range(2)]
        Ktm = A.bf16(16 * 128).rearrange("p (b n) -> p b n", b=16)
        Vtm = [A.bf16(16 * 128).rearrange("p (b n) -> p b n", b=16) for _ in range(2)]
        Kts = A.bf16(2 * 128).rearrange("p (b n) -> p b n", b=2)
        Vts = [A.bf16(2 * 128).rearrange("p (b n) -> p b n", b=2) for _ in range(2)]
        m2 = A.mark()
        vT = [A.bf16(TOK) for _ in range(2)]
        pre2 = [A.f32(3 + NPR) for _ in range(2)]
        pres2 = [A.f32(NSB * 7).rearrange("p (b t) -> p b t", t=7) for _ in range(2)]
        pcnt = [0]
        acc = A.f32(TOK)
        sqb = A.bf16(512)
        lnr = A.f32(512)

        def wcols(cc):
            def fn(inp, cc=cc):
                w = inp['gdn_w_in'][0]
                cols = np.concatenate([np.arange(c * 128, (c + 1) * 128) for c in cc])
                blk = w[:, cols].reshape(8, 128, 256)
                return blk.transpose(1, 0, 2).reshape(128, 2048)
            return fn
        chunks = [('q', kh), ('k', 8 + kh), ('v0', 16 + 2 * kh), ('v1', 17 + 2 * kh)]
        blocks = [[chunks[0], chunks[1]], [chunks[2], chunks[3]]]
        for blk_ in blocks:
            got = w_request(2048, wcols([c for _, c in blk_]))
            if got is None:
                continue
            wt, wkey = got
            wv = wt[:, 0:2048].rearrange("p (k n) -> p k n", k=8)
            for gi, (nm, cc) in enumerate(blk_):
                pi_ = pcnt[0] % 2
                pcnt[0] += 1
                pre, pres = pre2[pi_], pres2[pi_]
                kpre, kpres = f'pre{pi_}', f'pres{pi_}'
                P.op('dve', lambda pre=pre: nc.vector.memset(pre[:, 0:3], 0.0), writes=[kpre])
                P.dma('sp', lambda cc=cc, pres=pres: nc.sync.dma_start(out=pres[:, :, 0:3], in_=convs_in[:, cc, :, :]), writes=[kpres])

                def dst(ti, t0, tn, pap, pk, pre=pre, pres=pres, kpre=kpre, kpres=kpres):
                    if ti < 4:
                        P.op('act', lambda: nc.scalar.copy(out=pre[:, 3 + t0:3 + t0 + tn], in_=pap), reads=[pk], writes=[kpre])
                    else:
                        P.op('act', lambda: nc.scalar.copy(out=pres[:, :, 3:7], in_=pap.rearrange("p (b t) -> p b t", t=4)),
                             reads=[pk], writes=[kpres])
                proj_fm(wv, wkey, gi * 128, dst)
                P.dma('sp', lambda cc=cc, pre=pre: nc.sync.dma_start(out=convp_out[:, cc, :], in_=pre[:, NPR:NPR + 3]), reads=[kpre], writes=['convp_out'])
                P.dma('sp', lambda cc=cc, pres=pres: nc.sync.dma_start(out=convs_out[:, cc, :, :], in_=pres[:, :, 4:7]), reads=[kpres], writes=['convs_out'])
                wc = [vecT[:, V_CONV + t * 32 + cc:V_CONV + t * 32 + cc + 1] for t in range(4)]
                for (src3, dst_, n3) in ((None, acc[:, 0:NPR], NPR), ('s', acc[:, NPR:TOK].rearrange("p (b t) -> p b t", t=4), 4)):
                    def sl(t):
                        return pre[:, t:t + NPR] if src3 is None else pres[:, :, t:t + 4]
                    rk = [kpre] if src3 is None else [kpres]
                    ak = 'accp' if src3 is None else 'accs'
                    P.op('dve', lambda dst_=dst_, s3=sl(3), wc=wc: nc.vector.tensor_scalar(out=dst_, in0=s3, scalar1=wc[3], scalar2=None, op0=ALU.mult),
                         reads=rk, writes=[ak])
                    for t in range(3):
                        P.op('dve', lambda dst_=dst_, st=sl(t), t=t, wc=wc: nc.vector.scalar_tensor_tensor(
                            out=dst_, in0=st, scalar=wc[t], in1=dst_, op0=ALU.mult, op1=ALU.add), reads=rk + [ak], writes=[ak])
                if nm in ('v0', 'v1'):
                    a_ = int(nm[1])
                    act(vT[a_], acc, AF.Silu, reads=['accp', 'accs'], writes=[f'vT{a_}'])
                else:
                    act(acc, acc, AF.Silu, reads=['accp', 'accs'], writes=['accp', 'accs'])
                    dstT = qT if nm == 'q' else kT
                    sc_ = (128 ** -0.5) if nm == 'q' else 1.0
                    for ti, (t0, tn) in enumerate(TT):
                        act(sqb[:, 0:tn], acc[:, t0:t0 + tn], AF.Square, reads=['accp', 'accs'], writes=['l2sq'])
                        mm(ps[6][:, 0:tn], onesb, sqb[:, 0:tn], True, True, reads=['l2sq'], writes=[PS[6]])
                        act(lnr[:, 0:tn], ps[6][:, 0:tn], AF.Ln, reads=[PS[6]], writes=['l2ln'], bias=EPS_AP)
                        act(lnr[:, 0:tn], lnr[:, 0:tn], AF.Exp, reads=['l2ln'], writes=['l2ln'], scale=-0.5)
                        P.op('dve', lambda t0=t0, tn=tn, dstT=dstT, sc_=sc_: nc.vector.scalar_tensor_tensor(
                            out=dstT[:, t0:t0 + tn], in0=acc[:, t0:t0 + tn], scalar=sc_, in1=lnr[:, 0:tn], op0=ALU.mult, op1=ALU.mult),
                            reads=['accp', 'accs', 'l2ln'], writes=[f'{nm}T'])
        got = w_request(2048, (lambda inp, kh=kh: inp['gdn_w_in'][0][:, 4096 + 2 * kh * 128:4096 + (2 * kh + 2) * 128]
                               .reshape(8, 128, 256).transpose(1, 0, 2).reshape(128, 2048)))
        if got is not None:
            wt, wkey = got
            wv = wt[:, 0:2048].rearrange("p (k n) -> p k n", k=8)
            for a_ in range(2):
                def dstz(ti, t0, tn, pap, pk, a_=a_):
                    act(sz[a_][:, t0:t0 + tn], pap, AF.Silu, reads=[pk], writes=['siluz'])
                proj_fm(wv, wkey, a_ * 128, dstz)
        tps = ps[5][:, 0:64].bitcast(BF16)
        for (srcT, dtm, dts, key) in ((kT, Ktm, Kts, 'Ktm'), (vT[0], Vtm[0], Vts[0], 'Vtm0'), (vT[1], Vtm[1], Vts[1], 'Vtm1')):
            skey = 'kT' if srcT is kT else ('vT0' if srcT is vT[0] else 'vT1')
            for blk in range(16):
                P.op('pe', lambda srcT=srcT, blk=blk: nc.tensor.transpose(tps, srcT[:, blk * 128:(blk + 1) * 128], identb),
                     reads=[skey], writes=[PS[5]])
                P.op('act', lambda dtm=dtm, blk=blk: nc.scalar.copy(out=dtm[:, blk, :], in_=tps), reads=[PS[5]], writes=[key])
        for a_ in range(2):
            P.op('dve', lambda a_=a_: nc.vector.tensor_copy(out=vTs[a_], in_=vT[a_][:, NPR:TOK]), reads=[f'vT{a_}'], writes=[f'vTs{a_}'])
        P.fence()
        A.release(m2)
        S32 = [A.f32(128) for _ in range(2)]
        Sbf = [A.bf16(128) for _ in range(2)]
        tmpk = (A.f32(128), A.bf16(128), A.f32(128), A.f32(128), A.f32(128))

        def dbl(fn):
            return [fn() for _ in range(2)]

        def quad(fn):
            return [fn() for _ in range(4)]
        R2 = dbl(lambda: A.f32(256).rearrange("p (h n) -> p h n", h=2))
        D4 = dbl(lambda: A.f32(512).rearrange("p (m n) -> p m n", m=4))
        gsm = dbl(lambda: A.f32(16).rearrange("p (h n) -> p h n", h=2))
        egl = quad(lambda: A.f32(2))
        NA4 = dbl(lambda: A.bf16(512).rearrange("p (m n) -> p m n", m=4))
        N04 = dbl(lambda: A.bf16(512).rearrange("p (m n) -> p m n", m=4))
        N24 = dbl(lambda: A.bf16(512).rearrange("p (m n) -> p m n", m=4))
        T4 = dbl(lambda: A.bf16(512).rearrange("p (m n) -> p m n", m=4))
        Y4 = dbl(lambda: A.bf16(512).rearrange("p (m n) -> p m n", m=4))
        att2 = quad(lambda: A.bf16(256).rearrange("p (h n) -> p h n", h=2))
        bV2 = quad(lambda: A.bf16(256).rearrange("p (h n) -> p h n", h=2))
        TTh = quad(lambda: A.bf16(256).rearrange("p (h n) -> p h n", h=2))
        K22 = dbl(lambda: A.bf16(256).rearrange("p (h n) -> p h n", h=2))
        Kh2 = quad(lambda: A.bf16(256).rearrange("p (h n) -> p h n", h=2))
        nW2 = quad(lambda: A.bf16(256).rearrange("p (h n) -> p h n", h=2))
        Vn2 = quad(lambda: A.bf16(256).rearrange("p (h n) -> p h n", h=2))
        DG2 = dbl(lambda: A.f32(256).rearrange("p (h n) -> p h n", h=2))
        qt2 = quad(lambda: A.bf16(256).rearrange("p (h n) -> p h n", h=2))
        gmb = gmaskb
        for a_ in range(2):
            P.op('dve', lambda a_=a_: nc.vector.memset(S32[a_], 0.0), writes=[f'S32{a_}'])
            P.op('pool', lambda a_=a_: nc.gpsimd.memset(Sbf[a_], 0.0), writes=[f'Sbf{a_}'])

        def GMb(k):
            return gmb[:, k * 128:(k + 1) * 128]
        ctx = {}

        def phaseA(ck):
                samp = ck >= 16
                C = 4 if samp else 128
                pb = ck % 2
                hb = ck % 4
                sfx = f'.{pb}'
                hfx = f'.h{hb}'
                if samp:
                    b_ = ck - 16
                    cols = slice(NPR + b_ * 4, NPR + b_ * 4 + 4)
                    sb2 = b_ % 2
                    Kt, Vt_ = Kts[0:4, sb2, :], [Vts[0][0:4, sb2, :], Vts[1][0:4, sb2, :]]
                    bgv = bgs[0:4, b_, :]
                    kkey, vkeys, bgk = f'Ktms{sb2}', [f'Vtm0s{sb2}', f'Vtm1s{sb2}'], 'bgs'
                    tps7 = ps[7][:, 64:128].bitcast(BF16)
                    scol = slice(b_ * 4, b_ * 4 + 4)
                    for (srcT, dts, skey, dkey) in ((kT[:, cols], Kt, 'kT', kkey), (vTs[0][:, scol], Vt_[0], 'vTs0', vkeys[0]), (vTs[1][:, scol], Vt_[1], 'vTs1', vkeys[1])):
                        P.op('pe', lambda srcT=srcT, tps7=tps7: nc.tensor.transpose(tps7[0:4, :], srcT, identb), reads=[skey], writes=[PS[7]])
                        P.op('act', lambda dts=dts, tps7=tps7: nc.scalar.copy(out=dts, in_=tps7[0:4, :]), reads=[PS[7]], writes=[dkey])
                else:
                    cols = slice(ck * 128, (ck + 1) * 128)
                    Kt, Vt_ = Ktm[:, ck, :], [Vtm[0][:, ck, :], Vtm[1][:, ck, :]]
                    bgv = bg[:, ck, :]
                    kkey, vkeys, bgk = 'Ktm', ['Vtm0', 'Vtm1'], 'bg'
                h0 = heads[0]
                gcols = bgv[:, 16 + h0:18 + h0]
                bcols = bgv[:, h0:h0 + 2]
                r2, d4, gs, eg = R2[pb], D4[pb], gsm[pb], egl[hb]
                na4, n04, n24, t4, y4 = NA4[pb], N04[pb], N24[pb], T4[pb], Y4[pb]
                mm(ps[0][0:C, 0:C], kT[:, cols], kT[:, cols], True, True, reads=['kT'], writes=[PS[0]])
                mm(ps[0][0:C, 128:128 + C], kT[:, cols], qT[:, cols], True, True, reads=['kT', 'qT'], writes=[PS[0]])
                for a_ in range(2):
                    P.op('dve', lambda a_=a_, C=C, r2=r2, gcols=gcols: nc.vector.tensor_scalar(out=r2[0:C, a_, 0:C], in0=Mprev32[0:C, 0:C], scalar1=gcols[:, a_:a_ + 1], scalar2=None, op0=ALU.mult),
                         reads=[bgk, 'cm32'], writes=['R2' + sfx])
                for a_ in range(2):
                    mm(ps[1][0:C, a_ * 128:a_ * 128 + C], Mcur32[0:C, 0:C], r2[0:C, a_, 0:C], True, True, reads=['R2' + sfx, 'cm32'], writes=[PS[1]])
                    mm(ps[1][0:C, 256 + a_ * 128:256 + a_ * 128 + C], r2[0:C, a_, 0:C], Mcur32[0:C, 0:C], True, True, reads=['R2' + sfx, 'cm32'], writes=[PS[1]])
                mm(ps[0][0:C, 256:258], Mcur32[0:C, 0:C], gcols, True, True, reads=[bgk, 'cm32'], writes=[PS[0]])
                mm(ps[0][0:C, 258:260], ones32[0:C, 0:C], gcols, True, True, reads=[bgk, 'cm32'], writes=[PS[0]])
                mm(ps[0][:, 260:262], ones32[0:C, :], gcols, True, True, reads=[bgk, 'cm32'], writes=[PS[0]])
                if samp:
                    for m_ in range(4):
                        act(d4[0:C, m_, 0:C], ps[1][0:C, m_ * 128:m_ * 128 + C], AF.Exp, reads=[PS[1]], writes=['D4' + sfx])
                else:
                    act(d4[:, :, :], ps[1][:, :].rearrange("p (m n) -> p m n", m=4), AF.Exp, reads=[PS[1]], writes=['D4' + sfx])
                P.op('act', lambda gs=gs, C=C: nc.scalar.copy(out=gs[0:C, :, 0], in_=ps[0][0:C, 256:258]), reads=[PS[0]], writes=['gs' + sfx])
                act(gs[0:C, :, 1], ps[0][0:C, 256:258], AF.Exp, reads=[PS[0]], writes=['gs' + sfx])
                P.op('dve', lambda gs=gs, C=C: nc.vector.tensor_tensor(out=gs[0:C, :, 2], in0=ps[0][0:C, 258:260], in1=gs[0:C, :, 0], op=ALU.subtract),
                     reads=[PS[0], 'gs' + sfx], writes=['gs' + sfx])
                act(gs[0:C, :, 2], gs[0:C, :, 2], AF.Exp, reads=['gs' + sfx], writes=['gs' + sfx])
                act(eg, ps[0][:, 260:262], AF.Exp, reads=[PS[0]], writes=['egl' + hfx])
                P.op('dve', lambda gs=gs, C=C, bcols=bcols: nc.vector.tensor_scalar(out=gs[0:C, :, 3], in0=bcols, scalar1=-1.0, scalar2=None, op0=ALU.mult),
                     reads=[bgk], writes=['gs' + sfx])
                P.op('dve', lambda gs=gs, C=C, bcols=bcols: nc.vector.tensor_tensor(out=gs[0:C, :, 4], in0=bcols, in1=gs[0:C, :, 1], op=ALU.mult),
                     reads=[bgk, 'gs' + sfx], writes=['gs' + sfx])
                P.op('pool', lambda C=C, d4=d4: nc.gpsimd.tensor_tensor(out=d4[0:C, 0:2, 0:C], in0=d4[0:C, 0:2, 0:C], in1=Mprev32[0:C, 0:C].unsqueeze(1).broadcast_to([C, 2, C]), op=ALU.mult),
                     reads=['D4' + sfx, 'cm32'], writes=['D4' + sfx])
                P.op('pool', lambda C=C, d4=d4: nc.gpsimd.tensor_tensor(out=d4[0:C, 2:4, 0:C], in0=d4[0:C, 2:4, 0:C], in1=Mcur32[0:C, 0:C].unsqueeze(1).broadcast_to([C, 2, C]), op=ALU.mult),
                     reads=['D4' + sfx, 'cm32'], writes=['D4' + sfx])
                for a_ in range(2):
                    P.op('dve', lambda a_=a_, gs=gs, C=C, d4=d4, na4=na4: nc.vector.scalar_tensor_tensor(
                        out=na4[0:C, a_, 0:C], in0=ps[0][0:C, 0:C], scalar=gs[0:C, a_, 3:4], in1=d4[0:C, a_, 0:C], op0=ALU.mult, op1=ALU.mult),
                        reads=[PS[0], 'gs' + sfx, 'D4' + sfx], writes=['NA4' + sfx])
                    P.op('dve', lambda a_=a_, C=C, d4=d4, at=att2[hb]: nc.vector.tensor_tensor(out=at[0:C, a_, 0:C], in0=ps[0][0:C, 128:128 + C], in1=d4[0:C, 2 + a_, 0:C], op=ALU.mult),
                         reads=[PS[0], 'D4' + sfx], writes=['att2' + hfx])
                yield
                pA, pAk, pB, pBk = ps[2 + 2 * pb], PS[2 + 2 * pb], ps[3 + 2 * pb], PS[3 + 2 * pb]
                pAb = pA[:, 0:128].bitcast(BF16).rearrange("p (h n) -> p h n", h=2)
                for a_ in range(2):
                    P.op('pe', lambda a_=a_, C=C, na4=na4, pAb=pAb: nc.tensor.transpose(pAb[0:C, a_, 0:C], na4[0:C, a_, 0:C], identb[0:C, 0:C]), reads=['NA4' + sfx], writes=[pAk])
                P.op('act', lambda C=C, na4=na4, pAb=pAb: nc.scalar.copy(out=na4[0:C, 2:4, 0:C], in_=pAb[0:C, :, 0:C]), reads=[pAk], writes=['NA4' + sfx])
                P.op('dve', lambda C=C, na4=na4, n04=n04: nc.vector.tensor_tensor(out=n04[0:C, :, 0:C], in0=na4[0:C, :, 0:C], in1=GMb(0)[0:C, 0:C].unsqueeze(1).broadcast_to([C, 4, C]), op=ALU.mult),
                     reads=['NA4' + sfx, 'gmb'], writes=['N04' + sfx])
                P.op('dve', lambda C=C, n04=n04, t4=t4: nc.vector.tensor_tensor(out=t4[0:C, :, 0:C], in0=n04[0:C, :, 0:C], in1=identb[0:C, 0:C].unsqueeze(1).broadcast_to([C, 4, C]), op=ALU.add),
                     reads=['N04' + sfx], writes=['T4' + sfx])
                yield
                for step in range(1 if samp else 2):
                    src, sk = (n04, 'N04' + sfx) if step == 0 else (n24, 'N24' + sfx)
                    dst, dk_ = (n24, 'N24' + sfx) if step == 0 else (n04, 'N04' + sfx)
                    for a_ in range(2):
                        mm(pA[0:C, a_ * 128:a_ * 128 + C], src[0:C, 2 + a_, 0:C], src[0:C, a_, 0:C], True, True, reads=[sk], writes=[pAk])
                        mm(pA[0:C, (2 + a_) * 128:(2 + a_) * 128 + C], src[0:C, a_, 0:C], src[0:C, 2 + a_, 0:C], True, True, reads=[sk], writes=[pAk])
                    yield
                    P.op('act', lambda C=C, dst=dst: nc.scalar.copy(out=dst[0:C, :, 0:C], in_=pA[0:C, :].rearrange("p (m n) -> p m n", m=4)[:, :, 0:C]), reads=[pAk], writes=[dk_])
                    for a_ in range(2):
                        mm(pB[0:C, a_ * 128:a_ * 128 + C], dst[0:C, 2 + a_, 0:C], t4[0:C, a_, 0:C], True, True, reads=[dk_, 'T4' + sfx], writes=[pBk])
                        mm(pB[0:C, (2 + a_) * 128:(2 + a_) * 128 + C], dst[0:C, a_, 0:C], t4[0:C, 2 + a_, 0:C], True, True, reads=[dk_, 'T4' + sfx], writes=[pBk])
                    yield
                    P.op('dve', lambda C=C, t4=t4: nc.vector.tensor_tensor(out=t4[0:C, :, 0:C], in0=pB[0:C, :].rearrange("p (m n) -> p m n", m=4)[:, :, 0:C], in1=t4[0:C, :, 0:C], op=ALU.add),
                         reads=[pBk, 'T4' + sfx], writes=['T4' + sfx])
                if not samp:
                    for li in range(4):
                        lastl = li == 3
                        Ms = GMb(1 + 2 * li)
                        for a_ in range(2):
                            mm(pA[:, a_ * 128:(a_ + 1) * 128], na4[:, 2 + a_, :], t4[:, a_, :], True, True, reads=['NA4' + sfx, 'T4' + sfx], writes=[pAk])
                        yield
                        P.op('dve', lambda y4=y4, Ms=Ms, pA=pA: nc.vector.tensor_tensor(out=y4[:, 0:2, :], in0=pA[:, 0:256].rearrange("p (m n) -> p m n", m=2),
                                                                                 in1=Ms.unsqueeze(1).broadcast_to([128, 2, 128]), op=ALU.mult),
                             reads=[pAk, 'gmb'], writes=['Y4' + sfx])
                        for a_ in range(2):
                            if not lastl:
                                mm(pB[:, a_ * 128:(a_ + 1) * 128], t4[:, 2 + a_, :], y4[:, a_, :], True, False, reads=['T4' + sfx, 'Y4' + sfx], writes=[pBk])
                                mm(pB[:, a_ * 128:(a_ + 1) * 128], identb, t4[:, a_, :], False, True, reads=['T4' + sfx], writes=[pBk])
                            mm(pB[:, (2 + a_) * 128:(3 + a_) * 128], y4[:, a_, :], t4[:, 2 + a_, :], True, False, reads=['T4' + sfx, 'Y4' + sfx], writes=[pBk])
                            mm(pB[:, (2 + a_) * 128:(3 + a_) * 128], identb, t4[:, 2 + a_, :], False, True, reads=['T4' + sfx], writes=[pBk])
                        yield
                        if lastl:
                            P.op('act', lambda tth=TTh[hb], pB=pB: nc.scalar.copy(out=tth[:, :, :], in_=pB[:, 256:512].rearrange("p (m n) -> p m n", m=2)), reads=[pBk], writes=['TTh' + hfx])
                        else:
                            P.op('act', lambda t4=t4, pB=pB: nc.scalar.copy(out=t4[:, :, :], in_=pB[:, :].rearrange("p (m n) -> p m n", m=4)), reads=[pBk], writes=['T4' + sfx])
                if samp:
                    P.op('act', lambda C=C, t4=t4, tth=TTh[hb]: nc.scalar.copy(out=tth[0:C, :, 0:C], in_=t4[0:C, 2:4, 0:C]), reads=['T4' + sfx], writes=['TTh' + hfx])
                yield
                pw, pwk = ps[1], PS[1]
                for a_ in range(2):
                    P.op('act', lambda a_=a_, C=C, Vt_=Vt_, bcols=bcols, t_=bV2[hb]: nc.scalar.activation(out=t_[0:C, a_, :], in_=Vt_[a_], func=AF.Identity, scale=bcols[:, a_:a_ + 1]),
                         reads=[vkeys[a_], bgk], writes=['bV2' + hfx])
                    P.op('act', lambda a_=a_, C=C, Kt=Kt, gs=gs, t_=K22[pb]: nc.scalar.activation(out=t_[0:C, a_, :], in_=Kt, func=AF.Identity, scale=gs[0:C, a_, 4:5]),
                         reads=[kkey, 'gs' + sfx], writes=['K22' + sfx])
                    P.op('act', lambda a_=a_, C=C, Kt=Kt, gs=gs, t_=Kh2[hb]: nc.scalar.activation(out=t_[0:C, a_, :], in_=Kt, func=AF.Identity, scale=gs[0:C, a_, 2:3]),
                         reads=[kkey, 'gs' + sfx], writes=['Kh2' + hfx])
                    P.op('dve', lambda a_=a_, C=C, gs=gs, t_=DG2[pb]: nc.vector.tensor_scalar(out=t_[0:C, a_, 0:C], in0=ident32[0:C, 0:C], scalar1=gs[0:C, a_, 1:2], scalar2=None, op0=ALU.mult),
                         reads=['gs' + sfx, 'cm32'], writes=['DG2' + sfx])
                for a_ in range(2):
                    mm(pw[:, a_ * 128:a_ * 128 + C], K22[pb][0:C, a_, :], TTh[hb][0:C, a_, 0:C], True, True, reads=['K22' + sfx, 'TTh' + hfx], writes=[pwk])
                    mm(pw[:, 256 + a_ * 128:256 + a_ * 128 + C], ones32[0:C, :], DG2[pb][0:C, a_, 0:C], True, True, reads=['DG2' + sfx, 'cm32'], writes=[pwk])
                act(nW2[hb][:, :, 0:C], pw[:, 0:256].rearrange("p (h n) -> p h n", h=2)[:, :, 0:C], AF.Identity, reads=[pwk], writes=['nW2' + hfx], scale=-1.0)
                P.op('dve', lambda C=C, cols=cols, t_=qt2[hb]: nc.vector.tensor_tensor(out=t_[:, :, 0:C], in0=qT[:, cols].unsqueeze(1).broadcast_to([128, 2, C]),
                                                                                 in1=pw[:, 256:512].rearrange("p (h n) -> p h n", h=2)[:, :, 0:C], op=ALU.mult),
                     reads=['qT', pwk], writes=['qt2' + hfx])
                ctx[ck] = dict(C=C, pb=pb, hb=hb, hfx=hfx, sfx=sfx, samp=samp, cols=cols, t4=t4, eg=eg, b_=(ck - 16 if samp else None))
                yield

        def phaseB(ck):
                c_ = ctx[ck]
                C, pb, sfx, samp, cols, t4, eg, b_ = c_['C'], c_['pb'], c_['sfx'], c_['samp'], c_['cols'], c_['t4'], c_['eg'], c_['b_']
                hb, hfx = c_['hb'], c_['hfx']
                pw, pwk, po, pok = ps[6], PS[6], ps[6], PS[6]
                if samp:
                    for a_ in range(2):
                        hd = heads[a_]
                        P.dma('sp', lambda a_=a_, b_=b_, hd=hd: nc.sync.dma_start(out=S32[a_], in_=sg_in[b_, hd]), writes=[f'S32{a_}'])
                        P.op('act', lambda a_=a_: nc.scalar.copy(out=Sbf[a_], in_=S32[a_]), reads=[f'S32{a_}'], writes=[f'Sbf{a_}'])
                for a_ in range(2):
                    hd = heads[a_]
                    mm(pw[0:C, a_ * 128:128 + a_ * 128], TTh[hb][0:C, a_, 0:C], bV2[hb][0:C, a_, :], True, False, reads=['TTh' + hfx, 'bV2' + hfx], writes=[pwk])
                    mm(pw[0:C, a_ * 128:128 + a_ * 128], nW2[hb][:, a_, 0:C], Sbf[a_], False, True, reads=['nW2' + hfx, f'Sbf{a_}'], writes=[pwk])
                yield
                P.op('act', lambda C=C, t_=Vn2[hb]: nc.scalar.copy(out=t_[0:C, :, :], in_=pw[0:C, 0:256].rearrange("p (h n) -> p h n", h=2)), reads=[pwk], writes=['Vn2' + hfx])
                for a_ in range(2):
                    hd = heads[a_]
                    mm(po[:, 256 + a_ * 128:256 + a_ * 128 + C], Sbf[a_], qt2[hb][:, a_, 0:C], True, False, reads=[f'Sbf{a_}', 'qt2' + hfx], writes=[pok])
                    mm(po[:, 256 + a_ * 128:256 + a_ * 128 + C], Vn2[hb][0:C, a_, :], att2[hb][0:C, a_, 0:C], False, True, reads=['Vn2' + hfx, 'att2' + hfx], writes=[pok])
                    mm(ps[7][:, 256 + a_ * 128:384 + a_ * 128], Kh2[hb][0:C, a_, :], Vn2[hb][0:C, a_, :], True, True, reads=['Kh2' + hfx, 'Vn2' + hfx], writes=[PS[7]])
                for a_ in range(2):
                    hd = heads[a_]
                    if not samp:
                        P.op('dve', lambda a_=a_, eg=eg: nc.vector.scalar_tensor_tensor(out=Sbf[a_], in0=S32[a_], scalar=eg[:, a_:a_ + 1], in1=ps[7][:, 256 + a_ * 128:384 + a_ * 128], op0=ALU.mult, op1=ALU.add),
                             reads=[f'S32{a_}', 'egl' + hfx, PS[7]], writes=[f'Sbf{a_}'])
                    P.op('dve', lambda a_=a_, eg=eg: nc.vector.scalar_tensor_tensor(out=S32[a_], in0=S32[a_], scalar=eg[:, a_:a_ + 1], in1=ps[7][:, 256 + a_ * 128:384 + a_ * 128],
                                                                                 op0=ALU.mult, op1=ALU.add),
                         reads=[f'S32{a_}', 'egl' + hfx, PS[7]], writes=[f'S32{a_}'])
                    if samp:
                        P.dma('sp', lambda a_=a_, b_=b_, hd=hd: nc.sync.dma_start(out=sg_out[b_, hd], in_=S32[a_]), reads=[f'S32{a_}'], writes=['sg_out'])
                    else:
                        pass
                        if ck == 15:
                            P.dma('sp', lambda a_=a_, hd=hd: nc.sync.dma_start(out=gp_out[hd], in_=S32[a_]), reads=[f'S32{a_}'], writes=['gp_out'])
                yield
                for a_ in range(2):
                    gated_norm_out(po[:, 256 + a_ * 128:256 + a_ * 128 + C], C, cols, pok, vecT[:, V_GN:V_GN + 1], sz[a_], og[a_], f'og{a_}', tmpk)
                yield

        def seqB(k0):
            for kk_ in (k0, k0 + 1):
                for _ in phaseB(kk_):
                    yield

        def run_zip(gens):
            alive = [True] * len(gens)
            while any(alive):
                for gi_, g_ in enumerate(gens):
                    if alive[gi_]:
                        try:
                            next(g_)
                        except StopIteration:
                            alive[gi_] = False
        run_zip([phaseA(0), phaseA(1)])
        for k0 in range(2, 32, 2):
            if k0 == 16:
                run_zip([seqB(14)])
                run_zip([phaseA(16), phaseA(17)])
            else:
                run_zip([phaseA(k0), phaseA(k0 + 1), seqB(k0 - 2)])
        run_zip([seqB(30)])
        P.fence()
        out_proj_acc(og, lambda inp, kh=kh: inp['gdn_w_out'][0][2 * kh * 128:(2 * kh + 2) * 128], ['og0', 'og1'])

    def hgrn(i, j):
        rmsnorm(V_NMIX + i * 8, f'mix{i}')
        P.fence()
        m0 = A.mark()
        seg = A.f32(TOK)
        P.dma('sp', lambda: nc.sync.dma_start(out=seg, in_=seg_in), writes=['seg'])
        rowmask32 = A.f32(4)
        P.dma('sp', lambda: nc.sync.dma_start(out=rowmask32, in_=cmat_in[:, 896:900]), writes=['rowmask32'])
        assert i == 2
        lbe = A.f32(32).rearrange("p (l c) -> p l c", l=4)
        lbs = A.f32(8)
        lb = A.f32(8)
        oml = A.f32(8)
        act(lbe, vecT[:, V_LB:V_LB + 32].rearrange("p (l c) -> p l c", l=4), AF.Exp, reads=['vecT'], writes=['lbe'])
        P.op('dve', lambda: nc.vector.tensor_tensor(out=lbs, in0=lbe[:, 0, :], in1=lbe[:, 1, :], op=ALU.add), reads=['lbe'], writes=['lbs'])
        P.op('dve', lambda: nc.vector.tensor_tensor(out=lbs, in0=lbs, in1=lbe[:, 2, :], op=ALU.add), reads=['lbe', 'lbs'], writes=['lbs'])
        P.op('dve', lambda: nc.vector.tensor_tensor(out=lbs, in0=lbs, in1=lbe[:, 3, :], op=ALU.add), reads=['lbe', 'lbs'], writes=['lbs'])
        P.op('dve', lambda: nc.vector.reciprocal(out=lbs, in_=lbs), reads=['lbs'], writes=['lbs'])
        P.op('dve', lambda: nc.vector.tensor_tensor(out=lb, in0=lbe[:, 1, :], in1=lbe[:, 2, :], op=ALU.add), reads=['lbe'], writes=['lb'])
        P.op('dve', lambda: nc.vector.tensor_tensor(out=lb, in0=lb, in1=lbs, op=ALU.mult), reads=['lb', 'lbs'], writes=['lb'])
        P.op('dve', lambda: nc.vector.tensor_scalar(out=oml, in0=lb, scalar1=-1.0, scalar2=1.0, op0=ALU.mult, op1=ALU.add), reads=['lb'], writes=['oml'])
        m1 = A.mark()
        for hd in range(8):
            A.release(m1)
            hgrn_head(i, hd, seg, lb, oml, rowmask32)
        A.release(m0)
        P.fence()

    def hgrn_head(i, hd, seg, lb, oml, rowmask32):
        q32 = A.f32(TOK)
        k32_ = A.f32(TOK)
        bc = A.f32(TOK)
        ex = A.f32(TOK)
        qtT = A.bf16(TOK)
        ktT = A.bf16(TOK)
        khT = A.bf16(TOK)
        sz = A.bf16(TOK)
        og = A.bf16(TOK)
        Vtm = A.bf16(16 * 128).rearrange("p (b n) -> p b n", b=16)
        Vts = A.bf16(16 * 128).rearrange("p (b n) -> p b n", b=16)
        Khm = A.bf16(16 * 128).rearrange("p (b n) -> p b n", b=16)
        Khs = A.bf16(16 * 128).rearrange("p (b n) -> p b n", b=16)
        ebl = A.f32(80)
        lbc, omc = lb[:, hd:hd + 1], oml[:, hd:hd + 1]

        def wblk(c0, c1):
            def fn(inp, c0=c0, c1=c1):
                w = inp['hgrn_w_in'][0]
                cols = np.concatenate([np.arange(c0 * 128, (c0 + 1) * 128), np.arange(c1 * 128, (c1 + 1) * 128)])
                return w[:, cols].reshape(8, 128, 256).transpose(1, 0, 2).reshape(128, 2048)
            return fn
        got = w_request(2048, wblk(hd, 8 + hd))
        if got is not None:
            wt, wkey = got
            wv = wt[:, 0:2048].rearrange("p (k n) -> p k n", k=8)
            proj_fm(wv, wkey, 0, lambda ti, t0, tn, pap, pk: act(q32[:, t0:t0 + tn], pap, AF.Silu, reads=[pk], writes=['q32']))

            def dfz(ti, t0, tn, pap, pk):
                act(k32_[:, t0:t0 + tn], pap, AF.Sigmoid, reads=[pk], writes=['k32_'])
            proj_fm(wv, wkey, 128, dfz)
        P.op('dve', lambda: nc.vector.tensor_scalar(out=k32_, in0=k32_, scalar1=omc, scalar2=lbc, op0=ALU.mult, op1=ALU.add),
             reads=['k32_', 'lb', 'oml'], writes=['k32_'])
        act(bc, k32_, AF.Ln, reads=['k32_'], writes=['bc'])
        P.op('dve', lambda: nc.vector.tensor_scalar(out=k32_, in0=k32_, scalar1=-1.0, scalar2=1.0, op0=ALU.mult, op1=ALU.add),
             reads=['k32_'], writes=['k32_'])
        P.op('dve', lambda: nc.vector.tensor_tensor_scan(out=bc, data0=seg, data1=bc, initial=0.0, op0=ALU.mult, op1=ALU.add),
             reads=['bc', 'seg'], writes=['bc'])
        act(ex, bc, AF.Exp, reads=['bc'], writes=['ex'])
        P.op('dve', lambda: nc.vector.tensor_tensor(out=qtT, in0=q32, in1=ex, op=ALU.mult), reads=['q32', 'ex'], writes=['qtT'])
        act(ex, bc, AF.Exp, reads=['bc', 'qtT'], writes=['ex'], scale=-1.0)
        P.op('dve', lambda: nc.vector.tensor_tensor(out=ktT, in0=k32_, in1=ex, op=ALU.mult), reads=['k32_', 'ex'], writes=['ktT'])
        blp = bc[:, 0:NPR].rearrange("p (c t) -> p c t", t=32)
        bls = bc[:, NPR:TOK].rearrange("p (c t) -> p c t", t=4)
        P.op('dve', lambda: nc.vector.tensor_tensor(out=ex[:, 0:NPR].rearrange("p (c t) -> p c t", t=32), in0=blp[:, :, 31:32].broadcast_to([128, 64, 32]),
                                                   in1=blp, op=ALU.subtract), reads=['bc', 'ktT'], writes=['ex'])
        P.op('dve', lambda: nc.vector.tensor_tensor(out=ex[:, NPR:TOK].rearrange("p (c t) -> p c t", t=4), in0=bls[:, :, 3:4].broadcast_to([128, 16, 4]),
                                                   in1=bls, op=ALU.subtract), reads=['bc', 'ktT'], writes=['ex'])
        act(ex, ex, AF.Exp, reads=['ex'], writes=['ex'])
        P.op('dve', lambda: nc.vector.tensor_tensor(out=khT, in0=k32_, in1=ex, op=ALU.mult), reads=['k32_', 'ex'], writes=['khT'])
        act(ebl[:, 0:64].unsqueeze(2), blp[:, :, 31:32], AF.Exp, reads=['bc'], writes=['ebl'])
        act(ebl[:, 64:80].unsqueeze(2), bls[:, :, 3:4], AF.Exp, reads=['bc'], writes=['ebl'])
        got = w_request(2048, wblk(16 + hd, 24 + hd))
        if got is not None:
            wt, wkey = got
            wv = wt[:, 0:2048].rearrange("p (k n) -> p k n", k=8)
            for blk in range(16):
                pb = blk % 2
                for kc in range(8):
                    mm(ps[pb][:, 0:128], h[:, kc, blk * 128:(blk + 1) * 128], wv[:, kc, 0:128], kc == 0, kc == 7,
                       reads=[wkey, f'h{kc}.{blk // 4}'], writes=[PS[pb]])
                P.op('act', lambda blk=blk, pb=pb: nc.scalar.copy(out=Vtm[:, blk, :], in_=ps[pb][:, 0:128]), reads=[PS[pb]], writes=['Vtm'])
            for b_ in range(NSB):
                pb = b_ % 2
                for kc in range(8):
                    mm(ps[pb][0:4, 0:128], h[:, kc, NPR + b_ * 4:NPR + b_ * 4 + 4], wv[:, kc, 0:128], kc == 0, kc == 7,
                       reads=[wkey, f'h{kc}.4'], writes=[PS[pb]])
                P.op('act', lambda b_=b_, pb=pb: nc.scalar.copy(out=Vts[0:4, b_, :], in_=ps[pb][0:4, 0:128]), reads=[PS[pb]], writes=['Vts'])
            proj_fm(wv, wkey, 128, lambda ti, t0, tn, pap, pk: act(sz[:, t0:t0 + tn], pap, AF.Silu, reads=[pk], writes=['siluz']))
        tps = ps[5][:, 0:64].bitcast(BF16)
        for blk in range(16):
            P.op('pe', lambda blk=blk: nc.tensor.transpose(tps, khT[:, blk * 128:(blk + 1) * 128], identb), reads=['khT'], writes=[PS[5]])
            P.op('act', lambda blk=blk: nc.scalar.copy(out=Khm[:, blk, :], in_=tps), reads=[PS[5]], writes=['Khm'])
        for b_ in range(NSB):
            P.op('pe', lambda b_=b_: nc.tensor.transpose(tps[0:4, :], khT[:, NPR + b_ * 4:NPR + b_ * 4 + 4], identb), reads=['khT'], writes=[PS[5]])
            P.op('act', lambda b_=b_: nc.scalar.copy(out=Khs[0:4, b_, :], in_=tps[0:4, :]), reads=[PS[5]], writes=['Khs'])
        S32 = A.f32(128)
        Sbf = A.bf16(128)
        attT = [A.bf16(128) for _ in range(2)]
        Kmk = [A.bf16(128) for _ in range(4)]
        tmpk = (A.f32(128), A.bf16(128), A.f32(128), A.f32(128), A.f32(128))
        P.op('dve', lambda: nc.vector.memset(S32, 0.0), writes=['S32'])
        P.op('pool', lambda: nc.gpsimd.memset(Sbf, 0.0), writes=['Sbf'])
        gcol = vecT[:, V_HN:V_HN + 1]
        for blk in range(16):
            b2 = blk % 2
            cols = slice(blk * 128, (blk + 1) * 128)
            mm(ps[2 + b2][:, 0:128], ktT[:, cols], qtT[:, cols], True, True, reads=['ktT', 'qtT'], writes=[PS[2 + b2]])
            P.op('dve', lambda b2=b2: nc.vector.tensor_tensor(out=attT[b2], in0=ps[2 + b2][:, 0:128], in1=Mblkb, op=ALU.mult),
                 reads=[PS[2 + b2]], writes=[f'attT{b2}'])
            po = ps[6]
            mm(po[:, 0:128], Vtm[:, blk, :], attT[b2], True, False, reads=['Vtm', f'attT{b2}'], writes=[PS[6]])
            for sc in range(4):
                P.op('act', lambda blk=blk, sc=sc: nc.scalar.activation(out=Kmk[sc], in_=Khm[:, blk, :], func=AF.Identity, scale=rowmask32[:, sc:sc + 1]),
                     reads=['Khm', 'rowmask32'], writes=[f'Kmk{sc}'])
            for sc in range(4):
                c32 = slice(blk * 128 + sc * 32, blk * 128 + sc * 32 + 32)
                mm(po[:, sc * 32:(sc + 1) * 32], Sbf, qtT[:, c32], False, sc == 3, reads=['Sbf', 'qtT'], writes=[PS[6]])
                mm(ps[4][:, 0:128], Kmk[sc], Vtm[:, blk, :], True, True, reads=[f'Kmk{sc}', 'Vtm'], writes=[PS[4]])
                P.op('dve', lambda blk=blk, sc=sc: nc.vector.scalar_tensor_tensor(out=Sbf, in0=S32, scalar=ebl[:, blk * 4 + sc:blk * 4 + sc + 1], in1=ps[4][:, 0:128],
                                                                              op0=ALU.mult, op1=ALU.add), reads=['S32', 'ebl', PS[4]], writes=['Sbf'])
                P.op('dve', lambda blk=blk, sc=sc: nc.vector.scalar_tensor_tensor(out=S32, in0=S32, scalar=ebl[:, blk * 4 + sc:blk * 4 + sc + 1], in1=ps[4][:, 0:128],
                                                                              op0=ALU.mult, op1=ALU.add), reads=['S32', 'ebl', PS[4]], writes=['S32'])
            gated_norm_out(po[:, 0:128], 128, cols, PS[6], gcol, sz, og, 'og', tmpk)
        P.dma('sp', lambda: nc.sync.dma_start(out=hp_out[hd], in_=S32), reads=['S32'], writes=['hp_out'])
        for b_ in range(NSB):
            cols = slice(NPR + b_ * 4, NPR + b_ * 4 + 4)
            P.dma('sp', lambda b_=b_: nc.sync.dma_start(out=S32, in_=sh_in[b_, hd]), writes=['S32'])
            P.op('act', lambda: nc.scalar.copy(out=Sbf, in_=S32), reads=['S32'], writes=['Sbf'])
            b2 = b_ % 2
            mm(ps[2 + b2][0:4, 0:4], ktT[:, cols], qtT[:, cols], True, True, reads=['ktT', 'qtT'], writes=[PS[2 + b2]])
            P.op('dve', lambda b2=b2: nc.vector.tensor_tensor(out=attT[b2][0:4, 0:4], in0=ps[2 + b2][0:4, 0:4], in1=Mcurb[0:4, 0:4], op=ALU.mult),
                 reads=[PS[2 + b2]], writes=[f'attT{b2}'])
            po = ps[6]
            mm(po[:, 0:4], Vts[0:4, b_, :], attT[b2][0:4, 0:4], True, False, reads=['Vts', f'attT{b2}'], writes=[PS[6]])
            mm(po[:, 0:4], Sbf, qtT[:, cols], False, True, reads=['Sbf', 'qtT'], writes=[PS[6]])
            mm(ps[4][:, 0:128], Khs[0:4, b_, :], Vts[0:4, b_, :], True, True, reads=['Khs', 'Vts'], writes=[PS[4]])
            P.op('dve', lambda b_=b_: nc.vector.scalar_tensor_tensor(out=S32, in0=S32, scalar=ebl[:, 64 + b_:65 + b_], in1=ps[4][:, 0:128],
                                                                 op0=ALU.mult, op1=ALU.add), reads=['S32', 'ebl', PS[4]], writes=['S32'])
            P.dma('sp', lambda b_=b_: nc.sync.dma_start(out=sh_out[b_, hd], in_=S32), reads=['S32'], writes=['sh_out'])
            gated_norm_out(po[:, 0:4], 4, cols, PS[6], gcol, sz, og, 'og', tmpk)
        out_proj_acc([og], lambda inp, hd=hd: inp['hgrn_w_out'][0][hd * 128:(hd + 1) * 128], ['og'])

    eps_t = nc.alloc_sbuf_tensor("eps_t", [128, 2], F32)
    EPS_AP = eps_t[:, 0:1]
    ONE_AP = eps_t[:, 1:2]

    def body():
        P.op('dve', lambda: nc.vector.memset(eps_t[:, 0:1], EPS), writes=['eps'])
        P.op('dve', lambda: nc.vector.memset(eps_t[:, 1:2], 1.0), writes=['eps'])
        P.dma('sp', lambda: nc.sync.dma_start(out=vecT[:], in_=vecT_in), writes=['vecT'])
        P.dma('sp', lambda: nc.sync.dma_start(out=cm32[:], in_=cmat_in), writes=['cm32'])
        P.op('pool', lambda: nc.gpsimd.tensor_copy(out=cmbf[:], in_=cm32[:]), reads=['cm32'], writes=['cmbf'])
        P.dma('sp', lambda: nc.sync.dma_start(out=sink32[:], in_=sink_in), writes=['sink32'])
        act(esink[:], sink32[:], AF.Exp, reads=['sink32'], writes=['esink'])
        for c in range(8):
            P.dma('sp', lambda c=c: nc.sync.dma_start(out=x[:, c, :], in_=xT_in[:, c, :]), writes=[f'x{c}.{ti}' for ti in range(5)])
        P.fence()
        for i in range(DEPTH):
            kind, j = i % 3, i // 3
            ffn(i, 0)
            if stop_after == (i, 'a'):
                break
            if kind == 0:
                swa(i, j)
            elif kind == 1:
                gdn(i, j)
            else:
                hgrn(i, j)
            if stop_after == (i, 'm'):
                break
            ffn(i, 1)
            if stop_after == (i, 'b'):
                break
        if stop_after is None:
            rmsnorm(V_FIN, 'final', final=True)
        P.fence()
        for c in range(8):
            P.dma('sp', lambda c=c: nc.sync.dma_start(out=yT_out[:, c, :], in_=x[:, c, :]), reads=[f'x{c}.{ti}' for ti in range(5)], writes=['yT_out'])
        P.op('sp', lambda: None, reads=['yT_out', 'kp_out', 'vp_out', 'ks_out', 'vs_out', 'convp_out', 'convs_out',
                                        'sg_out', 'gp_out', 'sh_out', 'hp_out'])

    P.plan = True
    body()
    P.plan = False
    total = sum(n for n, _ in WS.plan_list)
    wflat_holder['ap'] = din("wflat", [128, total])
    A.top = 0
    body()
    stats = P.emit()
    stats['arena_peak'] = A.peak
    stats['wtotal'] = total
    return nc, WS.specs, stats


def pack_weights(specs, inputs, total):
    wflat = np.empty((128, total), np.float32)
    for off, n, fn in specs:
        wflat[:, off:off + n] = fn(inputs)
    return wflat


def host_inputs(inputs, specs, total):
    cst = _consts()
    wflat = pack_weights(specs, inputs, total)
    vec = np.concatenate([
        inputs['norm_ffn'].reshape(64, 128), inputs['norm_mix'].reshape(32, 128),
        inputs['final_norm'].reshape(8, 128), inputs['gdn_conv_w'].reshape(128, 128),
        inputs['hgrn_lb_logits'].reshape(32, 128), inputs['gdn_norm'].reshape(1, 128),
        inputs['hgrn_norm'].reshape(1, 128), np.zeros((6, 128), np.float32)], axis=0)
    vecT = np.ascontiguousarray(vec.T)
    sinks = np.ascontiguousarray(np.broadcast_to(inputs['swa_sinks'].reshape(1, 32), (128, 32))).astype(np.float32)
    gdnvec = np.ascontiguousarray(np.broadcast_to(
        np.concatenate([inputs['gdn_dt_bias'].reshape(16), inputs['gdn_a_log'].reshape(16)])[None, :], (128, 32))).astype(np.float32)
    maps = []
    for c in range(NCORE):
        xp = inputs['x_prompt'][c]
        xs = inputs['x_sample'][c * NSB:(c + 1) * NSB].reshape(NSM, D)
        xa = np.concatenate([xp, xs], axis=0)
        xT = np.ascontiguousarray(xa.T.reshape(8, 128, TOK).transpose(1, 0, 2))
        ck = inputs['cache_swa_k'][:, c * NSB:(c + 1) * NSB]
        kcache = ck.reshape(2, NSB, 128, 2, 2, 64).transpose(0, 4, 5, 3, 1, 2).reshape(2, 128, 2, NSB, 128)
        cv = inputs['cache_swa_v'][:, c * NSB:(c + 1) * NSB]
        vcache = cv.transpose(0, 2, 1, 3, 4).reshape(2, 128, NSB, 256)
        cs = inputs['state_gdn_conv'][0, c * NSB:(c + 1) * NSB]
        convs = np.ascontiguousarray(cs.reshape(NSB, 3, 32, 128).transpose(3, 2, 0, 1))
        sg = np.ascontiguousarray(inputs['state_gdn'][0, c * NSB:(c + 1) * NSB])
        sh = np.ascontiguousarray(inputs['state_hgrn'][0, c * NSB:(c + 1) * NSB])
        maps.append(dict(gdnvec=gdnvec, gmask=cst['gmask'], seg=cst['seg'], convs=convs, sg=sg, sh=sh, xT=xT, wflat=wflat, vecT=vecT, cmat=cst['cmat'], rope_c=cst['rope_c'], rope_s=cst['rope_s'],
                         sinks=sinks, kcache=np.ascontiguousarray(kcache), vcache=np.ascontiguousarray(vcache)))
    return maps


_CACHE = {}


def kernel(**inputs):
    inputs = {k: np.asarray(v) for k, v in inputs.items()}
    if 'prog' not in _CACHE:
        _CACHE['prog'] = build_program()
    nc, specs, stats = _CACHE['prog']
    maps = host_inputs(inputs, specs, stats['wtotal'])
    res = run_bass_kernel_spmd(nc, maps, core_ids=list(range(NCORE)))
    R = res.results
    B = NCORE
    y_prompt = np.empty((B, NPR, D), np.float32)
    y_sample = np.empty((B * NSB, 4, D), np.float32)
    pk = np.empty((2, B, 128, 4, 64), np.float32)
    pv = np.empty((2, B, 128, 4, 64), np.float32)
    pg = np.empty((1, B, 16, 128, 128), np.float32)
    pc = np.empty((1, B, 3, 4096), np.float32)
    ph = np.empty((1, B, 8, 128, 128), np.float32)
    sk = np.empty((2, B * NSB, 128, 4, 64), np.float32)
    sv = np.empty((2, B * NSB, 128, 4, 64), np.float32)
    sg = np.empty((1, B * NSB, 16, 128, 128), np.float32)
    sc = np.empty((1, B * NSB, 3, 4096), np.float32)
    sh = np.empty((1, B * NSB, 8, 128, 128), np.float32)
    for c in range(B):
        r = R[c]
        y = np.asarray(r['yT']).transpose(2, 1, 0).reshape(TOK, D)
        y_prompt[c] = y[:NPR]
        y_sample[c * NSB:(c + 1) * NSB] = y[NPR:].reshape(NSB, 4, D)
        bs = slice(c * NSB, (c + 1) * NSB)
        for j in range(2):
            pk[j, c] = np.asarray(r['kp'])[j].transpose(2, 1, 0)
            pv[j, c] = np.asarray(r['vp'])[j].reshape(128, 4, 64)
            sk[j, bs] = np.asarray(r['ks'])[j].transpose(0, 3, 1, 2)
            sv[j, bs] = np.asarray(r['vs'])[j].reshape(NSB, 128, 4, 64)
        pg[0, c] = np.asarray(r['gp_o'])
        ph[0, c] = np.asarray(r['hp_o'])
        sg[0, bs] = np.asarray(r['sg_o'])
        sh[0, bs] = np.asarray(r['sh_o'])
        pc[0, c] = np.asarray(r['convp']).transpose(2, 1, 0).reshape(3, 4096)
        sc[0, bs] = np.asarray(r['convs_o']).transpose(2, 3, 1, 0).reshape(NSB, 3, 4096)
    return (y_prompt, y_sample, pk, pv, pg, pc, ph, sk, sv, sg, sc, sh)
```

```python
import numpy as np
import concourse.bass as bass
import concourse.mybir as mybir
from concourse.bass_utils import run_bass_kernel_spmd

F32 = mybir.dt.float32
BF16 = mybir.dt.bfloat16
AF = mybir.ActivationFunctionType
ALU = mybir.AluOpType
AX = mybir.AxisListType
SEM_CAP = 30000

NCORE = 8
D = 1024
NPR = 2048
NSB = 16
NSM = NSB * 4
TOK = NPR + NSM
TT = [(0, 512), (512, 512), (1024, 512), (1536, 512), (2048, 64)]
DFF = 2816
NJ = DFF // 128
EPS = 1e-6
DEPTH = 4


class Prog:
    def __init__(self, nc, n_dma_sems=24):
        self.nc = nc
        self.eng = {'pe': nc.tensor, 'dve': nc.vector, 'act': nc.scalar,
                    'pool': nc.gpsimd, 'sp': nc.sync}
        self.ops = []
        self.last_w = {}
        self.readers = {}
        self.n_dma_sems = n_dma_sems
        self.dma_rr = 0
        self.dma_last = [None] * n_dma_sems
        self.fence_ops = []
        self.plan = False

    def _deps(self, reads, writes, nofence):
        deps = set()
        for r in reads:
            if r in self.last_w:
                deps.add(self.last_w[r])
        for w in writes:
            if w in self.last_w:
                deps.add(self.last_w[w])
            deps.update(self.readers.get(w, ()))
        if not nofence:
            deps.update(self.fence_ops)
        return deps

    def _commit(self, idx, reads, writes):
        for r in reads:
            self.readers.setdefault(r, []).append(idx)
        for w in writes:
            self.last_w[w] = idx
            self.readers[w] = []

    def op(self, eng, fn, reads=(), writes=(), nofence=False):
        if self.plan:
            return
        psr = [r for r in reads if r.startswith('ps')]
        if psr:
            reads = [r for r in reads if not r.startswith('ps')]
            writes = list(writes) + psr
        deps = self._deps(reads, writes, nofence)
        idx = len(self.ops)
        self.ops.append(dict(eng=eng, fn=fn, deps=deps, dma=None, nofence=nofence))
        self._commit(idx, reads, writes)

    def dma(self, eng, fn, reads=(), writes=(), nofence=False):
        if self.plan:
            return
        deps = self._deps(reads, writes, nofence)
        s = self.dma_rr
        self.dma_rr = (self.dma_rr + 1) % self.n_dma_sems
        if self.dma_last[s] is not None:
            deps.add(self.dma_last[s])
        idx = len(self.ops)
        self.dma_last[s] = idx
        self.ops.append(dict(eng=eng, fn=fn, deps=deps, dma=s, nofence=nofence))
        self._commit(idx, reads, writes)

    def fence(self):
        if self.plan:
            return
        last = {}
        for i, o in enumerate(self.ops):
            if o['nofence']:
                continue
            if o['dma'] is not None:
                last[('d', i)] = i
            else:
                last[o['eng']] = i
        prev = set(self.fence_ops)
        keep = []
        for k, i in last.items():
            if isinstance(k, tuple) and i < self._fence_pos:
                continue
            keep.append(i)
        self.fence_ops = keep
        self._fence_pos = len(self.ops)

    _fence_pos = 0

    def emit(self):
        nc = self.nc
        ops = self.ops
        signal = [False] * len(ops)
        for i, o in enumerate(ops):
            by_eng = {}
            keep = []
            for j in o['deps']:
                p = ops[j]
                if p['dma'] is not None:
                    keep.append(j)
                    continue
                if p['eng'] == o['eng'] and o['eng'] == 'pe' and o['dma'] is None:
                    continue
                by_eng[p['eng']] = max(by_eng.get(p['eng'], -1), j)
            keep.extend(by_eng.values())
            o['deps'] = keep
            for j in keep:
                signal[j] = True
        eng_sems = {e: [] for e in self.eng}
        eng_cnt = {e: SEM_CAP for e in self.eng}
        dma_sems = [nc.alloc_semaphore(f"dq{s}") for s in range(self.n_dma_sems)]
        dma_cnt = [0] * self.n_dma_sems
        token = [None] * len(ops)
        waited = {e: {} for e in self.eng}
        nwait = 0
        for i, o in enumerate(ops):
            e = o['eng']
            E = self.eng[e]
            for j in sorted(o['deps']):
                sem, val, key = token[j]
                if waited[e].get(key, 0) < val:
                    E.wait_ge(sem, val)
                    waited[e][key] = val
                    nwait += 1
            inst = o['fn']()
            if o['dma'] is not None:
                s = o['dma']
                dma_cnt[s] += 16
                inst.then_inc(dma_sems[s], 16)
                token[i] = (dma_sems[s], dma_cnt[s], ('d', s))
            elif signal[i]:
                if eng_cnt[e] >= SEM_CAP:
                    eng_sems[e].append(nc.alloc_semaphore(f"e_{e}_{len(eng_sems[e])}"))
                    eng_cnt[e] = 0
                eng_cnt[e] += 1
                sem = eng_sems[e][-1]
                inst.then_inc(sem, 1)
                token[i] = (sem, eng_cnt[e], ('e', e, len(eng_sems[e])))
        return dict(n_ops=len(ops), n_wait=nwait, n_signal=sum(signal))


class Arena:
    def __init__(self, nc, words):
        self.t = nc.alloc_sbuf_tensor("arena", [128, words], F32)
        self.words = words
        self.top = 0
        self.peak = 0

    def mark(self):
        return self.top

    def release(self, m):
        self.top = m

    def f32(self, n):
        a = self.t[:, self.top:self.top + n]
        self.top += n
        self.peak = max(self.peak, self.top)
        assert self.top <= self.words, f"arena overflow {self.top} > {self.words}"
        return a

    def bf16(self, n):
        w = (n + 1) // 2
        a = self.t[:, self.top:self.top + w].bitcast(BF16)
        self.top += w
        self.peak = max(self.peak, self.top)
        assert self.top <= self.words, f"arena overflow {self.top} > {self.words}"
        return a[:, 0:n]


def _rope_tables():
    rot, theta = 16, 500000.0
    inv = (theta ** (-np.arange(0, rot, 2, dtype=np.float32) / rot)).astype(np.float32)
    pos = np.concatenate([np.arange(NPR, dtype=np.float32),
                          np.tile(8192.0 + np.arange(4, dtype=np.float32), NSB)]).astype(np.float32)
    ang = (pos[:, None] * inv[None, :]).astype(np.float32)
    cos, sin = np.cos(ang).astype(np.float32), np.sin(ang).astype(np.float32)
    C = np.ones((128, TOK), np.float32)
    S = np.zeros((128, TOK), np.float32)
    for p in range(128):
        d = p % 64
        if d < 8:
            C[p] = cos[:, d]
            S[p] = -sin[:, d]
        elif d < 16:
            C[p] = cos[:, d - 8]
            S[p] = sin[:, d - 8]
    Pm = np.zeros((128, 128), np.float32)
    for m in range(128):
        d = m % 64
        if d < 8:
            Pm[m + 8, m] = 1.0
        elif d < 16:
            Pm[m - 8, m] = 1.0
    return C, S, Pm


def _consts():
    C, S, Pm = _rope_tables()
    idx = np.arange(128)
    Mcur = (idx[:, None] <= idx[None, :]).astype(np.float32)
    Mprev = (idx[:, None] > idx[None, :]).astype(np.float32)
    Msamp = np.zeros((64, NSB, 4), np.float32)
    for b in range(NSB):
        for t2 in range(4):
            for t in range(4):
                if t2 <= t:
                    Msamp[b * 4 + t2, b, t] = 1.0
    blk = (idx[:, None] // 32) == (idx[None, :] // 32)
    Mblk = (blk & (idx[:, None] <= idx[None, :])).astype(np.float32)
    cm = np.zeros((128, 1024), np.float32)
    cm[:, 0:128] = np.eye(128, dtype=np.float32)
    cm[:, 128:256] = 1.0
    cm[:, 256:384] = Pm
    cm[:, 384:512] = Mcur
    cm[:, 512:640] = Mprev
    cm[0:64, 640:704] = Msamp.reshape(64, 64)
    cm[:, 768:896] = Mblk
    for c in range(4):
        cm[c * 32:(c + 1) * 32, 896 + c] = 1.0
    gm = np.zeros((128, 9 * 128), np.float32)
    ii, jj = idx[:, None], idx[None, :]
    gm[:, 0:128] = (ii // 8 == jj // 8)
    for li, sz_ in enumerate((8, 16, 32, 64)):
        Ms = ((ii // (2 * sz_) == jj // (2 * sz_)) & (ii % (2 * sz_) >= sz_) & (jj % (2 * sz_) < sz_)).astype(np.float32)
        gm[:, (1 + 2 * li) * 128:(2 + 2 * li) * 128] = Ms
        gm[:, (2 + 2 * li) * 128:(3 + 2 * li) * 128] = Ms.T
    seg = np.ones((128, TOK), np.float32)
    seg[:, 0:NPR:32] = 0.0
    seg[:, NPR:TOK:4] = 0.0
    return dict(rope_c=C, rope_s=S, cmat=cm, seg=seg, gmask=gm)


V_NFFN = 0
V_NMIX = 64
V_FIN = 96
V_CONV = 104
V_LB = 232
V_GN = 264
V_HN = 265
NV = 272


def build_program(stop_after=None, debug=False):
    nc = bass.Bass("TRN2", target_bir_lowering=False, dynamic_dma_scratch_size=512)
    P = Prog(nc)

    def din(name, shape):
        return nc.dram_tensor(name, list(shape), F32, kind="ExternalInput").ap()

    def dout(name, shape):
        return nc.dram_tensor(name, list(shape), F32, kind="ExternalOutput").ap()

    xT_in = din("xT", [128, 8, TOK])
    wflat_holder = {}
    vecT_in = din("vecT", [128, NV])
    cmat_in = din("cmat", [128, 1024])
    ropec_in = din("rope_c", [128, TOK])
    ropes_in = din("rope_s", [128, TOK])
    sink_in = din("sinks", [128, 32])
    kc_in = din("kcache", [2, 128, 2, NSB, 128])
    vc_in = din("vcache", [2, 128, NSB, 256])
    gdnvec_in = din("gdnvec", [128, 32])
    gmask_in = din("gmask", [128, 9 * 128])
    seg_in = din("seg", [128, TOK])
    convs_in = din("convs", [128, 32, NSB, 3])
    sg_in = din("sg", [NSB, 16, 128, 128])
    sh_in = din("sh", [NSB, 8, 128, 128])
    convp_out = dout("convp", [128, 32, 3])
    convs_out = dout("convs_o", [128, 32, NSB, 3])
    sg_out = dout("sg_o", [NSB, 16, 128, 128])
    gp_out = dout("gp_o", [16, 128, 128])
    sh_out = dout("sh_o", [NSB, 8, 128, 128])
    hp_out = dout("hp_o", [8, 128, 128])
    yT_out = dout("yT", [128, 8, TOK])
    kp_out = dout("kp", [2, 64, 4, 128])
    vp_out = dout("vp", [2, 128, 256])
    ks_out = dout("ks", [2, NSB, 4, 64, 128])
    vs_out = dout("vs", [2, NSB, 128, 256])

    x = nc.alloc_sbuf_tensor("x", [128, 8, TOK], F32)
    h = nc.alloc_sbuf_tensor("h", [128, 8, TOK], BF16)
    NSTAGE, NWB = 2, 3
    stage = [nc.alloc_sbuf_tensor(f"stage{i}", [128, 2048], F32) for i in range(NSTAGE)]
    wb = [nc.alloc_sbuf_tensor(f"wb{i}", [128, 2048], BF16) for i in range(NWB)]
    vecT = nc.alloc_sbuf_tensor("vecT_sb", [128, NV], F32)
    cm32 = nc.alloc_sbuf_tensor("cm32", [128, 1024], F32)
    cmbf = nc.alloc_sbuf_tensor("cmbf", [128, 1024], BF16)
    esink = nc.alloc_sbuf_tensor("esink", [128, 32], F32)
    sink32 = nc.alloc_sbuf_tensor("sink32", [128, 32], F32)
    ident32 = cm32[:, 0:128]
    identb = cmbf[:, 0:128]
    onesb = cmbf[:, 128:256]
    Pmb = cmbf[:, 256:384]
    Mcurb = cmbf[:, 384:512]
    Mprevb = cmbf[:, 512:640]
    Msampb = cmbf[0:64, 640:704]
    Mblkb = cmbf[:, 768:896]
    rowmask = cmbf[:, 896:900]
    ARENA_WORDS = 22780
    A = Arena(nc, ARENA_WORDS)
    ps = [nc.alloc_psum_tensor(f"ps{i}", [128, 512], F32) for i in range(8)]
    PS = [f"ps{i}" for i in range(8)]

    class WS:
        specs = []
        off = 0
        plan_list = []
        k_issue = 0
        k_get = 0
        DEPTH = 3

    def w_request(n, fn):
        if P.plan:
            WS.plan_list.append((n, fn))
            return None
        while WS.k_issue < len(WS.plan_list) and WS.k_issue <= WS.k_get + WS.DEPTH - 1:
            nn, ff = WS.plan_list[WS.k_issue]
            off = WS.off
            WS.specs.append((off, nn, ff))
            WS.off += nn
            s = WS.k_issue % NSTAGE
            b = WS.k_issue % NWB
            wf = wflat_holder['ap']
            P.dma('sp', lambda s=s, off=off, nn=nn: nc.sync.dma_start(out=stage[s][:, 0:nn], in_=wf[:, off:off + nn]),
                  writes=[f'stage{s}'], nofence=True)
            P.op('pool', lambda s=s, b=b, nn=nn: nc.gpsimd.tensor_copy(out=wb[b][:, 0:nn], in_=stage[s][:, 0:nn]),
                 reads=[f'stage{s}'], writes=[f'wb{b}'], nofence=True)
            WS.k_issue += 1
        b = WS.k_get % NWB
        WS.k_get += 1
        return wb[b], f'wb{b}'

    def mm(out, lhsT, rhs, start, stop, reads, writes, **kw):
        P.op('pe', lambda: nc.tensor.matmul(out, lhsT, rhs, start=start, stop=stop, **kw), reads=reads, writes=writes)

    def act(out, in_, func, reads, writes, **kw):
        P.op('act', lambda: nc.scalar.activation(out=out, in_=in_, func=func, **kw), reads=reads, writes=writes)

    def vcol(base, c):
        return vecT[:, base + c:base + c + 1]

    def rmsnorm(gbase, tag, final=False):
        NO = ARENA_WORDS - 1536
        sq = [A.t[:, NO:NO + 256].bitcast(BF16), A.t[:, NO + 256:NO + 512].bitcast(BF16)]
        lnt = A.t[:, NO + 512:NO + 1024]
        rstd = A.t[:, NO + 1024:NO + 1536]
        k = 0
        for ti, (t0, tn) in enumerate(TT):
            for c in range(8):
                b = k % 2
                k += 1
                act(sq[b][:, 0:tn], x[:, c, t0:t0 + tn], AF.Square, reads=[f'x{c}.{ti}'], writes=[f'sq{b}'])
                mm(ps[6][:, 0:tn], onesb, sq[b][:, 0:tn], c == 0, c == 7, reads=[f'sq{b}'], writes=[PS[6]])
            act(lnt[:, 0:tn], ps[6][:, 0:tn], AF.Ln, reads=[PS[6]], writes=['lnt'], scale=1.0 / D, bias=EPS_AP)
            act(rstd[:, 0:tn], lnt[:, 0:tn], AF.Exp, reads=['lnt'], writes=['rstd'], scale=-0.5)
            for c in range(8):
                dst = x if final else h
                P.op('dve', lambda c=c, t0=t0, tn=tn, dst=dst: nc.vector.scalar_tensor_tensor(
                    out=dst[:, c, t0:t0 + tn], in0=x[:, c, t0:t0 + tn], scalar=vcol(gbase, c), in1=rstd[:, 0:tn],
                    op0=ALU.mult, op1=ALU.mult), reads=[f'x{c}.{ti}', 'rstd'], writes=[f'{"x" if final else "h"}{c}.{ti}'])

    def ffn(i, a):
        rmsnorm(V_NFFN + (i * 2 + a) * 8, f'ffn{i}{a}')
        m0 = A.mark()
        NSPLIT = 2
        per = NJ // NSPLIT
        actb = A.bf16(per * TOK).rearrange("p (j t) -> p j t", j=per)
        sg = [A.f32(512) for _ in range(2)]
        kk = 0
        for sp in range(NSPLIT):
            for jl in range(per):
                j = sp * per + jl

                def fn_in(inp, i=i, a=a, j=j):
                    w = inp['ffn_w_in'][i, a]
                    g = w[:, j * 128:(j + 1) * 128].reshape(8, 128, 128)
                    u = w[:, DFF + j * 128:DFF + (j + 1) * 128].reshape(8, 128, 128)
                    return np.concatenate([g, u], axis=2).transpose(1, 0, 2).reshape(128, 2048)
                got = w_request(2048, fn_in)
                if got is None:
                    continue
                wt, wkey = got
                wv = wt[:, 0:2048].rearrange("p (k n) -> p k n", k=8)
                for ti, (t0, tn) in enumerate(TT):
                    pb = (kk % 2) * 2
                    sb_ = kk % 2
                    kk += 1
                    for kc in range(8):
                        mm(ps[pb][:, 0:tn], wv[:, kc, 0:128], h[:, kc, t0:t0 + tn], kc == 0, kc == 7,
                           reads=[wkey, f'h{kc}.{ti}'], writes=[PS[pb]])
                    for kc in range(8):
                        mm(ps[pb + 1][:, 0:tn], wv[:, kc, 128:256], h[:, kc, t0:t0 + tn], kc == 0, kc == 7,
                           reads=[wkey, f'h{kc}.{ti}'], writes=[PS[pb + 1]])
                    act(sg[sb_][:, 0:tn], ps[pb][:, 0:tn], AF.Silu, reads=[PS[pb]], writes=[f'sg{sb_}'])
                    P.op('dve', lambda jl=jl, t0=t0, tn=tn, sb_=sb_, pb=pb: nc.vector.tensor_tensor(
                        out=actb[:, jl, t0:t0 + tn], in0=sg[sb_][:, 0:tn], in1=ps[pb + 1][:, 0:tn], op=ALU.mult),
                        reads=[f'sg{sb_}', PS[pb + 1]], writes=[f'act{jl}.{ti}'])
            for m in range(8):
                def fn_out(inp, i=i, a=a, sp=sp, m=m):
                    w = inp['ffn_w_out'][i, a]
                    blk = w[sp * per * 128:(sp + 1) * per * 128, m * 128:(m + 1) * 128]
                    return blk.reshape(per, 128, 128).transpose(1, 0, 2).reshape(128, per * 128)
                got = w_request(per * 128, fn_out)
                if got is None:
                    continue
                wt, wkey = got
                wv = wt[:, 0:per * 128].rearrange("p (j n) -> p j n", j=per)
                for ti, (t0, tn) in enumerate(TT):
                    pb = 4 + (kk % 2)
                    kk += 1
                    for jl in range(per):
                        mm(ps[pb][:, 0:tn], wv[:, jl, :], actb[:, jl, t0:t0 + tn], jl == 0, jl == per - 1,
                           reads=[wkey, f'act{jl}.{ti}'], writes=[PS[pb]])
                    P.op('dve', lambda m=m, t0=t0, tn=tn, pb=pb: nc.vector.scalar_tensor_tensor(
                        out=x[:, m, t0:t0 + tn], in0=ps[pb][:, 0:tn], scalar=0.5, in1=x[:, m, t0:t0 + tn],
                        op0=ALU.mult, op1=ALU.add), reads=[PS[pb], f'x{m}.{ti}'], writes=[f'x{m}.{ti}'])
        A.release(m0)

    def swa(i, j):
        rmsnorm(V_NMIX + i * 8, f'mix{i}')
        P.fence()
        m0 = A.mark()
        qT = A.bf16(8 * TOK).rearrange("p (c t) -> p c t", c=8)
        kT = A.bf16(2 * TOK).rearrange("p (c t) -> p c t", c=2)
        vtm = A.bf16(17 * 256).rearrange("p (b n) -> p b n", b=17)
        k32 = A.f32(2 * 192).rearrange("p (c t) -> p c t", c=2)
        v32 = A.f32(2 * 256).rearrange("p (b n) -> p b n", b=2)
        m1 = A.mark()
        ropec = A.f32(TOK)
        ropes = A.f32(TOK)
        P.dma('sp', lambda: nc.sync.dma_start(out=ropec, in_=ropec_in), writes=['ropec'])
        P.dma('sp', lambda: nc.sync.dma_start(out=ropes, in_=ropes_in), writes=['ropes'])
        qraw = [A.bf16(512) for _ in range(2)]
        t1 = [A.f32(512) for _ in range(2)]
        t2 = [A.f32(512) for _ in range(2)]
        win = inp_swa_in = None

        def qcols(c):
            kc, g = c // 4, c % 4
            h0 = (2 * kc) * 4 + g
            h1 = (2 * kc + 1) * 4 + g
            return np.concatenate([np.arange(h0 * 64, h0 * 64 + 64), np.arange(h1 * 64, h1 * 64 + 64)])

        def wblock(cols_list):
            def fn(inp, j=j, cols_list=cols_list):
                w = inp['swa_w_in'][j]
                cols = np.concatenate(cols_list)
                blk = w[:, cols].reshape(8, 128, len(cols))
                return blk.transpose(1, 0, 2).reshape(128, 8 * len(cols))
            return fn

        STG = 9
        SKIP = ''
        kk = 0
        groups = [[('q', 0), ('q', 1)], [('q', 2), ('q', 3)], [('q', 4), ('q', 5)], [('q', 6), ('q', 7)],
                  [('k', 0), ('k', 1)]]
        for grp in groups:
            cols_list = []
            for kind, c in grp:
                cols_list.append(qcols(c) if kind == 'q' else 1024 + np.arange(c * 128, (c + 1) * 128))
            got = w_request(2048, wblock(cols_list))
            if got is None or 'q' in SKIP:
                continue
            wt, wkey = got
            wv = wt[:, 0:2048].rearrange("p (k n) -> p k n", k=8)
            for gi, (kind, c) in enumerate(grp):
                for ti, (t0, tn) in enumerate(TT):
                    b = kk % 2
                    kk += 1
                    pa, pp = b * 2, b * 2 + 1
                    for kc in range(8):
                        mm(ps[pa][:, 0:tn], wv[:, kc, gi * 128:(gi + 1) * 128], h[:, kc, t0:t0 + tn], kc == 0, kc == 7,
                           reads=[wkey, f'h{kc}.{ti}'], writes=[PS[pa]])
                    P.op('act', lambda b=b, pa=pa, tn=tn: nc.scalar.copy(out=qraw[b][:, 0:tn], in_=ps[pa][:, 0:tn]),
                         reads=[PS[pa]], writes=[f'qraw{b}'])
                    if 'r' in SKIP:
                        continue
                    mm(ps[pp][:, 0:tn], Pmb, qraw[b][:, 0:tn], True, True, reads=[f'qraw{b}'], writes=[PS[pp]])
                    P.op('dve', lambda b=b, pa=pa, t0=t0, tn=tn: nc.vector.tensor_tensor(
                        out=t1[b][:, 0:tn], in0=ps[pa][:, 0:tn], in1=ropec[:, t0:t0 + tn], op=ALU.mult),
                        reads=[PS[pa], 'ropec'], writes=[f't1{b}'])
                    P.op('dve', lambda b=b, pp=pp, t0=t0, tn=tn: nc.vector.tensor_tensor(
                        out=t2[b][:, 0:tn], in0=ps[pp][:, 0:tn], in1=ropes[:, t0:t0 + tn], op=ALU.mult),
                        reads=[PS[pp], 'ropes'], writes=[f't2{b}'])
                    if 'p' in SKIP:
                        continue
                    dst = qT[:, c, t0:t0 + tn] if kind == 'q' else kT[:, c, t0:t0 + tn]
                    dkey = f'{kind}T{c}.{ti}'
                    P.op('pool', lambda b=b, dst=dst, tn=tn: nc.gpsimd.tensor_tensor(
                        out=dst, in0=t1[b][:, 0:tn], in1=t2[b][:, 0:tn], op=ALU.add),
                        reads=[f't1{b}', f't2{b}'], writes=[dkey])
                    if kind == 'k' and ti >= 3:
                        lo = 1920 - t0 if ti == 3 else 0
                        dlo = 0 if ti == 3 else 128
                        n = 128 if ti == 3 else 64
                        P.op('pool', lambda b=b, c=c, lo=lo, dlo=dlo, n=n: nc.gpsimd.tensor_tensor(
                            out=k32[:, c, dlo:dlo + n], in0=t1[b][:, lo:lo + n], in1=t2[b][:, lo:lo + n], op=ALU.add),
                            reads=[f't1{b}', f't2{b}'], writes=[f'k32.{c}'])
        def fn_v(inp, j=j):
            w = inp['swa_w_in'][j][:, 1280:1536].reshape(8, 128, 256)
            return w.transpose(1, 0, 2).reshape(128, 2048)
        got = w_request(2048, fn_v)
        if got is not None and 'v' not in SKIP:
            wt, wkey = got
            wv = wt[:, 0:2048].rearrange("p (k n) -> p k n", k=8)
            for blk in range(17):
                t0 = blk * 128
                tn = 128 if blk < 16 else 64
                ti = min(t0 // 512, 4)
                pa = 4 + blk % 2
                for kc in range(8):
                    mm(ps[pa][0:tn, 0:256], h[:, kc, t0:t0 + tn], wv[:, kc, :], kc == 0, kc == 7,
                       reads=[wkey, f'h{kc}.{ti}'], writes=[PS[pa]])
                P.op('act', lambda blk=blk, pa=pa, tn=tn: nc.scalar.copy(out=vtm[0:tn, blk, :], in_=ps[pa][0:tn, 0:256]),
                     reads=[PS[pa]], writes=[f'vtm{blk}'])
                if blk >= 15:
                    P.op('dve', lambda blk=blk, pa=pa, tn=tn: nc.vector.tensor_copy(out=v32[0:tn, blk - 15, :], in_=ps[pa][0:tn, 0:256]),
                         reads=[PS[pa]], writes=[f'v32.{blk - 15}'])
        P.fence()
        if STG < 2:
            A.release(m0)
            return
        for c in range(2):
            for hf in range(2):
                kvh = 2 * c + hf
                P.dma('sp', lambda c=c, hf=hf, kvh=kvh: nc.sync.dma_start(
                    out=kp_out[j, :, kvh, :], in_=k32[hf * 64:(hf + 1) * 64, c, 0:128]), reads=[f'k32.{c}'], writes=['kp_out'])
                P.dma('sp', lambda c=c, hf=hf, kvh=kvh: nc.sync.dma_start(
                    out=ks_out[j, :, kvh, :, 124:128].rearrange("b d t -> d b t"),
                    in_=k32[hf * 64:(hf + 1) * 64, c, 128:192].rearrange("p (b t) -> p b t", t=4)),
                    reads=[f'k32.{c}'], writes=['ks_out'])
                P.dma('sp', lambda c=c, hf=hf, kvh=kvh: nc.sync.dma_start(
                    out=ks_out[j, :, kvh, :, 0:124].rearrange("b d t -> d b t"),
                    in_=kc_in[j, hf * 64:(hf + 1) * 64, c, :, 4:128]), writes=['ks_out'])
        P.dma('sp', lambda: nc.sync.dma_start(out=vp_out[j], in_=v32[:, 0, :]), reads=['v32.0'], writes=['vp_out'])
        for b in range(NSB):
            P.dma('sp', lambda b=b: nc.sync.dma_start(out=vs_out[j, b, 124:128, :], in_=v32[b * 4:(b + 1) * 4, 1, :]),
                  reads=['v32.1'], writes=['vs_out'])
        P.dma('sp', lambda: nc.sync.dma_start(out=vs_out[j, :, 0:124, :].rearrange("b k n -> k b n"),
                                              in_=vc_in[j, 4:128, :, :]), writes=['vs_out'])
        A.release(m1)
        if STG < 3:
            A.release(m0)
            return
        pT = [A.bf16(1024) for _ in range(2)]
        lnd = [A.f32(512) for _ in range(2)]
        rden = [A.f32(512) for _ in range(2)]
        scale = 64 ** -0.5
        it = 0
        for qb in range(16):
            ti = qb // 4
            for kvh in range(4):
                kc, hf = kvh // 2, kvh % 2
                lo, hi = hf * 64, (hf + 1) * 64
                b = it % 2
                it += 1
                pS = [ps[b * 2], ps[b * 2 + 1]]
                pO, pD = ps[4 + b], ps[6 + b]
                kbs = [qb - 1, qb] if qb > 0 else [qb]
                for kb in kbs:
                    slot = 0 if kb == qb - 1 else 1
                    for g in range(4):
                        c = kc * 4 + g
                        mm(pS[slot][:, g * 128:(g + 1) * 128], kT[lo:hi, kc, kb * 128:(kb + 1) * 128],
                           qT[lo:hi, c, qb * 128:(qb + 1) * 128], True, True,
                           reads=[f'kT{kc}.{kb // 4}', f'qT{c}.{ti}'], writes=[PS[b * 2 + slot]])
                for kb in kbs:
                    slot = 0 if kb == qb - 1 else 1
                    act(pT[b][:, slot * 512:(slot + 1) * 512], pS[slot][:, :], AF.Exp,
                        reads=[PS[b * 2 + slot]], writes=[f'pT{b}.{slot}'], scale=scale)
                    M = Mprevb if slot == 0 else Mcurb
                    P.op('pool', lambda b=b, slot=slot, M=M: nc.gpsimd.tensor_tensor(
                        out=pT[b][:, slot * 512:(slot + 1) * 512].rearrange("p (g n) -> p g n", g=4),
                        in0=pT[b][:, slot * 512:(slot + 1) * 512].rearrange("p (g n) -> p g n", g=4),
                        in1=M.unsqueeze(1).broadcast_to([128, 4, 128]), op=ALU.mult),
                        reads=[f'pT{b}.{slot}'], writes=[f'pT{b}.{slot}'])
                for g in range(4):
                    hq = kvh * 4 + g
                    for n_, kb in enumerate(kbs):
                        slot = 0 if kb == qb - 1 else 1
                        mm(pO[lo:hi, g * 128:(g + 1) * 128], vtm[:, kb, kvh * 64:(kvh + 1) * 64],
                           pT[b][:, slot * 512 + g * 128: slot * 512 + (g + 1) * 128], n_ == 0, n_ == len(kbs) - 1,
                           reads=[f'vtm{kb}', f'pT{b}.{slot}'], writes=[PS[4 + b]])
                    for n_, kb in enumerate(kbs):
                        slot = 0 if kb == qb - 1 else 1
                        mm(pD[lo:hi, g * 128:(g + 1) * 128], onesb[:, 0:64],
                           pT[b][:, slot * 512 + g * 128: slot * 512 + (g + 1) * 128], n_ == 0, n_ == len(kbs) - 1,
                           reads=[f'pT{b}.{slot}'], writes=[PS[6 + b]])
                    act(lnd[b][lo:hi, g * 128:(g + 1) * 128], pD[lo:hi, g * 128:(g + 1) * 128], AF.Ln,
                        reads=[PS[6 + b], 'esink'], writes=[f'lnd{b}'], bias=esink[lo:hi, j * 16 + hq:j * 16 + hq + 1])
                act(rden[b][lo:hi, :], lnd[b][lo:hi, :], AF.Exp, reads=[f'lnd{b}'], writes=[f'rden{b}'], scale=-1.0)
                P.op('dve', lambda b=b, lo=lo, hi=hi, kc=kc, qb=qb, pO=pO: nc.vector.tensor_tensor(
                    out=h[lo:hi, kc * 4:kc * 4 + 4, qb * 128:(qb + 1) * 128],
                    in0=pO[lo:hi, :].rearrange("p (g n) -> p g n", g=4),
                    in1=rden[b][lo:hi, :].rearrange("p (g n) -> p g n", g=4), op=ALU.mult),
                    reads=[PS[4 + b], f'rden{b}'], writes=[f'h{kc * 4 + g_}.{ti}' for g_ in range(4)])
        P.fence()
        A.release(m1)
        if STG < 4:
            A.release(m0)
            return
        kcb = A.bf16(2 * NSB * 128).rearrange("p (c b k) -> p c b k", c=2, b=NSB)
        vcb = A.bf16(NSB * 256).rearrange("p (b n) -> p b n", b=NSB)
        for c in range(2):
            s = c % NSTAGE
            P.dma('sp', lambda c=c, s=s: nc.sync.dma_start(out=stage[s][:, 0:2048], in_=kc_in[j, :, c, :, :].rearrange("p b k -> p (b k)")),
                  writes=[f'stage{s}'])
            P.op('pool', lambda c=c, s=s: nc.gpsimd.tensor_copy(out=kcb[:, c, :, :].rearrange("p b k -> p (b k)"), in_=stage[s][:, 0:2048]),
                 reads=[f'stage{s}'], writes=['kcb'])
        for hb in range(2):
            s = hb % NSTAGE
            P.dma('sp', lambda hb=hb, s=s: nc.sync.dma_start(out=stage[s][:, 0:2048], in_=vc_in[j, :, hb * 8:(hb + 1) * 8, :].rearrange("p b n -> p (b n)")),
                  writes=[f'stage{s}'])
            P.op('pool', lambda hb=hb, s=s: nc.gpsimd.tensor_copy(out=vcb[:, hb * 8:(hb + 1) * 8, :].rearrange("p b n -> p (b n)"), in_=stage[s][:, 0:2048]),
                 reads=[f'stage{s}'], writes=['vcb'])
        pTc = A.bf16(1024)
        pTn = A.bf16(1024)
        lnd2 = A.f32(1024)
        rden2 = A.f32(1024)
        pSc = [ps[0], ps[1]]
        pSn = [ps[2], ps[3]]
        S0 = NPR
        for b_ in range(NSB):
            for kvh in range(4):
                kc, hf = kvh // 2, kvh % 2
                lo, hi = hf * 64, (hf + 1) * 64
                col = (b_ * 4 + kvh) * 16
                bank, cin = col // 512, col % 512
                for g in range(4):
                    c = kc * 4 + g
                    mm(pSc[bank][:, cin + g * 4:cin + g * 4 + 4], kcb[lo:hi, kc, b_, :],
                       qT[lo:hi, c, S0 + b_ * 4:S0 + b_ * 4 + 4], True, True,
                       reads=['kcb', f'qT{c}.4'], writes=[PS[bank]])
                    mm(pSn[bank][0:64, cin + g * 4:cin + g * 4 + 4], kT[lo:hi, kc, S0:S0 + 64],
                       qT[lo:hi, c, S0 + b_ * 4:S0 + b_ * 4 + 4], True, True,
                       reads=[f'kT{kc}.4', f'qT{c}.4'], writes=[PS[2 + bank]])
        for bank in range(2):
            act(pTc[:, bank * 512:(bank + 1) * 512], pSc[bank][:, :], AF.Exp, reads=[PS[bank]], writes=['pTc'], scale=scale)
            act(pTn[0:64, bank * 512:(bank + 1) * 512], pSn[bank][0:64, :], AF.Exp, reads=[PS[2 + bank]], writes=['pTn'], scale=scale)
        P.op('pool', lambda: nc.gpsimd.tensor_tensor(
            out=pTc.rearrange("p (a t) -> p a t", t=4), in0=pTc.rearrange("p (a t) -> p a t", t=4),
            in1=Mprevb[:, 0:4].unsqueeze(1).broadcast_to([128, 256, 4]), op=ALU.mult), reads=['pTc'], writes=['pTc'])
        P.op('pool', lambda: nc.gpsimd.tensor_tensor(
            out=pTn[0:64, :].rearrange("p (b a t) -> p b a t", b=NSB, t=4),
            in0=pTn[0:64, :].rearrange("p (b a t) -> p b a t", b=NSB, t=4),
            in1=Msampb.rearrange("p (b t) -> p b t", t=4).unsqueeze(2).broadcast_to([64, NSB, 16, 4]), op=ALU.mult),
            reads=['pTn'], writes=['pTn'])
        pO2, pD2 = ps[4], ps[5]
        for b_ in range(NSB):
            for kvh in range(4):
                kc, hf = kvh // 2, kvh % 2
                lo, hi = hf * 64, (hf + 1) * 64
                col = (b_ * 4 + kvh) * 16
                oc = (b_ * 2 + kc) * 16
                mm(pO2[lo:hi, oc:oc + 16], vcb[:, b_, kvh * 64:(kvh + 1) * 64], pTc[:, col:col + 16], True, False,
                   reads=['vcb', 'pTc'], writes=[PS[4]])
                mm(pO2[lo:hi, oc:oc + 16], vtm[0:64, 16, kvh * 64:(kvh + 1) * 64], pTn[0:64, col:col + 16], False, True,
                   reads=['vtm16', 'pTn'], writes=[PS[4]])
                mm(pD2[lo:hi, oc:oc + 16], onesb[:, 0:64], pTc[:, col:col + 16], True, False, reads=['pTc'], writes=[PS[5]])
                mm(pD2[lo:hi, oc:oc + 16], onesb[0:64, 0:64], pTn[0:64, col:col + 16], False, True, reads=['pTn'], writes=[PS[5]])
        for kvh in range(4):
            kc, hf = kvh // 2, kvh % 2
            lo, hi = hf * 64, (hf + 1) * 64
            for g in range(4):
                hq = kvh * 4 + g
                act(lnd2[lo:hi, 0:512].rearrange("p (b k g t) -> p b k g t", b=NSB, k=2, g=4)[:, :, kc, g, :],
                    pD2[lo:hi, :].rearrange("p (b k g t) -> p b k g t", b=NSB, k=2, g=4)[:, :, kc, g, :], AF.Ln,
                    reads=[PS[5], 'esink'], writes=['lnd2'], bias=esink[lo:hi, j * 16 + hq:j * 16 + hq + 1])
        act(rden2[:, 0:512], lnd2[:, 0:512], AF.Exp, reads=['lnd2'], writes=['rden2'], scale=-1.0)
        for kc in range(2):
            P.op('dve', lambda kc=kc: nc.vector.tensor_tensor(
                out=h[:, kc * 4:kc * 4 + 4, S0:S0 + 64].rearrange("p g (b t) -> p b g t", t=4),
                in0=pO2[:, :].rearrange("p (b k g t) -> p b k g t", b=NSB, k=2, g=4)[:, :, kc, :, :],
                in1=rden2[:, 0:512].rearrange("p (b k g t) -> p b k g t", b=NSB, k=2, g=4)[:, :, kc, :, :], op=ALU.mult),
                reads=[PS[4], 'rden2'], writes=[f'h{kc * 4 + g_}.4' for g_ in range(4)])
        P.fence()
        for mp in range(4):
            def fn_o(inp, j=j, mp=mp):
                w = inp['swa_w_out'][j]
                rows = np.concatenate([qcols(c) for c in range(8)])
                blk = w[rows][:, mp * 256:(mp + 1) * 256].reshape(8, 128, 256)
                return blk.transpose(1, 0, 2).reshape(128, 2048)
            got = w_request(2048, fn_o)
            if got is None:
                continue
            wt, wkey = got
            wv = wt[:, 0:2048].rearrange("p (k n) -> p k n", k=8)
            for mi in range(2):
                m = mp * 2 + mi
                for ti, (t0, tn) in enumerate(TT):
                    pb = (mi * 5 + ti) % 2
                    for kc in range(8):
                        mm(ps[pb][:, 0:tn], wv[:, kc, mi * 128:(mi + 1) * 128], h[:, kc, t0:t0 + tn], kc == 0, kc == 7,
                           reads=[wkey, f'h{kc}.{ti}'], writes=[PS[pb]])
                    P.op('dve', lambda m=m, t0=t0, tn=tn, pb=pb: nc.vector.tensor_tensor(
                        out=x[:, m, t0:t0 + tn], in0=ps[pb][:, 0:tn], in1=x[:, m, t0:t0 + tn], op=ALU.add),
                        reads=[PS[pb], f'x{m}.{ti}'], writes=[f'x{m}.{ti}'])
        A.release(m0)
        P.fence()

    def gated_norm_out(o_ps, C, cols, ncol_key, gcol, siluz, og, ogkey, tmpk):
        o32, sqb, lnr, rn, t3 = tmpk
        P.op('act', lambda: nc.scalar.copy(out=o32[:, 0:C], in_=o_ps), reads=[ncol_key], writes=['gn_o32'])
        act(sqb[:, 0:C], o32[:, 0:C], AF.Square, reads=['gn_o32'], writes=['gn_sq'])
        mm(ps[7][:, 0:C], onesb, sqb[:, 0:C], True, True, reads=['gn_sq'], writes=[PS[7]])
        act(lnr[:, 0:C], ps[7][:, 0:C], AF.Ln, reads=[PS[7]], writes=['gn_ln'], scale=1.0 / 128, bias=EPS_AP)
        act(rn[:, 0:C], lnr[:, 0:C], AF.Exp, reads=['gn_ln'], writes=['gn_rn'], scale=-0.5)
        P.op('dve', lambda: nc.vector.scalar_tensor_tensor(out=t3[:, 0:C], in0=o32[:, 0:C], scalar=gcol, in1=rn[:, 0:C],
                                                           op0=ALU.mult, op1=ALU.mult), reads=['gn_o32', 'gn_rn'], writes=['gn_t3'])
        P.op('pool', lambda: nc.gpsimd.tensor_tensor(out=og[:, cols], in0=t3[:, 0:C], in1=siluz[:, cols], op=ALU.mult),
             reads=['gn_t3', 'siluz'], writes=[ogkey])

    def out_proj_acc(og_list, wfn_rows, ogkeys):
        nk = len(og_list)
        mper = min(8, 2048 // (nk * 128))
        for m0_ in range(0, 8, mper):
            def fn(inp, m0_=m0_):
                rows = wfn_rows(inp)
                blk = rows[:, m0_ * 128:(m0_ + mper) * 128].reshape(nk, 128, mper * 128)
                return blk.transpose(1, 0, 2).reshape(128, nk * mper * 128)
            got = w_request(nk * mper * 128, fn)
            if got is None:
                continue
            wt, wkey = got
            wv = wt[:, 0:nk * mper * 128].rearrange("p (k n) -> p k n", k=nk)
            for mi in range(mper):
                m = m0_ + mi
                for ti, (t0, tn) in enumerate(TT):
                    pb = (m * 5 + ti) % 2
                    for kc in range(nk):
                        mm(ps[pb][:, 0:tn], wv[:, kc, mi * 128:(mi + 1) * 128], og_list[kc][:, t0:t0 + tn], kc == 0, kc == nk - 1,
                           reads=[wkey, ogkeys[kc]], writes=[PS[pb]])
                    P.op('dve', lambda m=m, t0=t0, tn=tn, pb=pb: nc.vector.tensor_tensor(
                        out=x[:, m, t0:t0 + tn], in0=ps[pb][:, 0:tn], in1=x[:, m, t0:t0 + tn], op=ALU.add),
                        reads=[PS[pb], f'x{m}.{ti}'], writes=[f'x{m}.{ti}'])

    def proj_fm(wv, wkey, col0, dst_fn):
        for ti, (t0, tn) in enumerate(TT):
            pb = ti % 2
            for kc in range(8):
                mm(ps[pb][:, 0:tn], wv[:, kc, col0:col0 + 128], h[:, kc, t0:t0 + tn], kc == 0, kc == 7,
                   reads=[wkey, f'h{kc}.{ti}'], writes=[PS[pb]])
            dst_fn(ti, t0, tn, ps[pb][:, 0:tn], PS[pb])

    def gdn(i, j):
        rmsnorm(V_NMIX + i * 8, f'mix{i}')
        P.fence()
        m0 = A.mark()
        cm32 = A.f32(1024)
        P.dma('sp', lambda: nc.sync.dma_start(out=cm32, in_=cmat_in), writes=['cm32'])
        ident32, ones32 = cm32[:, 0:128], cm32[:, 128:256]
        Mcur32, Mprev32 = cm32[:, 384:512], cm32[:, 512:640]
        gmaskb = A.bf16(9 * 128)
        gv = A.f32(32)
        P.dma('sp', lambda: nc.sync.dma_start(out=gv, in_=gdnvec_in), writes=['gv'])
        nega = A.f32(16)
        act(nega, gv[:, 16:32], AF.Exp, reads=['gv'], writes=['nega'])
        P.op('dve', lambda: nc.vector.tensor_scalar(out=nega, in0=nega, scalar1=-1.0, scalar2=None, op0=ALU.mult), reads=['nega'], writes=['nega'])
        bg = A.f32(16 * 32).rearrange("p (b n) -> p b n", b=16)
        bgs = A.f32(16 * 32).rearrange("p (b n) -> p b n", b=16)

        def fn_ba(inp):
            w = inp['gdn_w_in'][0][:, 6144:6176].reshape(8, 128, 32)
            return w.transpose(1, 0, 2).reshape(128, 256)
        got = w_request(256, fn_ba)
        if got is not None:
            wt, wkey = got
            wv = wt[:, 0:256].rearrange("p (k n) -> p k n", k=8)
            for blk in range(16):
                for kc in range(8):
                    mm(ps[0][:, blk * 32:(blk + 1) * 32], h[:, kc, blk * 128:(blk + 1) * 128], wv[:, kc, :], kc == 0, kc == 7,
                       reads=[wkey, f'h{kc}.{blk // 4}'], writes=[PS[0]])
            for b_ in range(NSB):
                for kc in range(8):
                    mm(ps[1][0:4, b_ * 32:(b_ + 1) * 32], h[:, kc, NPR + b_ * 4:NPR + b_ * 4 + 4], wv[:, kc, :], kc == 0, kc == 7,
                       reads=[wkey, f'h{kc}.4'], writes=[PS[1]])
            for (src, dst, np_, key, pk) in ((ps[0], bg, 128, 'bg', PS[0]), (ps[1], bgs, 4, 'bgs', PS[1])):
                sv = src[0:np_, :].rearrange("p (b n) -> p b n", b=16)
                dv = dst[0:np_]
                act(dv[:, :, 0:16], sv[:, :, 0:16], AF.Sigmoid, reads=[pk], writes=[key])
                P.op('dve', lambda sv=sv, dv=dv, np_=np_: nc.vector.tensor_tensor(
                    out=dv[:, :, 16:32], in0=sv[:, :, 16:32], in1=gv[0:np_, 0:16].unsqueeze(1).broadcast_to([np_, 16, 16]), op=ALU.add),
                    reads=[pk, 'gv'], writes=[key])
                act(dv[:, :, 16:32], dv[:, :, 16:32], AF.Exp, reads=[key], writes=[key])
                act(dv[:, :, 16:32], dv[:, :, 16:32], AF.Ln, reads=[key], writes=[key], bias=ONE_AP[0:np_])
                P.op('dve', lambda dv=dv, np_=np_: nc.vector.tensor_tensor(
                    out=dv[:, :, 16:32], in0=dv[:, :, 16:32], in1=nega[0:np_].unsqueeze(1).broadcast_to([np_, 16, 16]), op=ALU.mult),
                    reads=[key, 'nega'], writes=[key])
        m1 = A.mark()
        gmask = A.f32(9 * 128)
        P.dma('sp', lambda: nc.sync.dma_start(out=gmask, in_=gmask_in), writes=['gmask'])
        P.op('pool', lambda: nc.gpsimd.tensor_copy(out=gmaskb, in_=gmask), reads=['gmask'], writes=['gmb'])
        P.fence()
        for kh in range(8):
            A.release(m1)
            gdn_group(i, kh, cm32, bg, bgs, gmaskb)
        A.release(m0)
        P.fence()

    def gdn_group(i, kh, cm32, bg, bgs, gmaskb):
        ident32, ones32 = cm32[:, 0:128], cm32[:, 128:256]
        Mcur32, Mprev32 = cm32[:, 384:512], cm32[:, 512:640]
        heads = [2 * kh, 2 * kh + 1]
        qT = A.bf16(TOK)
        kT = A.bf16(TOK)
        vTs = [A.bf16(64) for _ in range(2)]
        sz = [A.bf16(TOK) for _ in range(2)]
        og = [A.bf16(TOK) for _ in range(2)]
        Ktm = A.bf16(16 * 128).rearrange("p (b n) -> p b n", b=16)
        Vtm = [A.bf16(16 * 128).rearrange("p (b n) -> p b n", b=16) for _ in range(2)]
        Kts = A.bf16(2 * 128).rearrange("p (b n) -> p b n", b=2)
        Vts = [A.bf16(2 * 128).rearrange("p (b n) -> p b n", b=2) for _ in range(2)]
        m2 = A.mark()
        vT = [A.bf16(TOK) for _ in range(2)]
        pre = A.f32(3 + NPR)
        pres = A.f32(NSB * 7).rearrange("p (b t) -> p b t", t=7)
        acc = A.f32(TOK)
        sqb = A.bf16(512)
        lnr = A.f32(512)

        def wcols(cc):
            def fn(inp, cc=cc):
                w = inp['gdn_w_in'][0]
                cols = np.concatenate([np.arange(c * 128, (c + 1) * 128) for c in cc])
                blk = w[:, cols].reshape(8, 128, 256)
                return blk.transpose(1, 0, 2).reshape(128, 2048)
            return fn
        chunks = [('q', kh), ('k', 8 + kh), ('v0', 16 + 2 * kh), ('v1', 17 + 2 * kh)]
        blocks = [[chunks[0], chunks[1]], [chunks[2], chunks[3]]]
        for blk_ in blocks:
            got = w_request(2048, wcols([c for _, c in blk_]))
            if got is None:
                continue
            wt, wkey = got
            wv = wt[:, 0:2048].rearrange("p (k n) -> p k n", k=8)
            for gi, (nm, cc) in enumerate(blk_):
                P.op('dve', lambda: nc.vector.memset(pre[:, 0:3], 0.0), writes=['pre'])
                P.dma('sp', lambda cc=cc: nc.sync.dma_start(out=pres[:, :, 0:3], in_=convs_in[:, cc, :, :]), writes=['pres'])

                def dst(ti, t0, tn, pap, pk):
                    if ti < 4:
                        P.op('act', lambda: nc.scalar.copy(out=pre[:, 3 + t0:3 + t0 + tn], in_=pap), reads=[pk], writes=['pre'])
                    else:
                        P.op('act', lambda: nc.scalar.copy(out=pres[:, :, 3:7], in_=pap.rearrange("p (b t) -> p b t", t=4)),
                             reads=[pk], writes=['pres'])
                proj_fm(wv, wkey, gi * 128, dst)
                P.dma('sp', lambda cc=cc: nc.sync.dma_start(out=convp_out[:, cc, :], in_=pre[:, NPR:NPR + 3]), reads=['pre'], writes=['convp_out'])
                P.dma('sp', lambda cc=cc: nc.sync.dma_start(out=convs_out[:, cc, :, :], in_=pres[:, :, 4:7]), reads=['pres'], writes=['convs_out'])
                wc = [vecT[:, V_CONV + t * 32 + cc:V_CONV + t * 32 + cc + 1] for t in range(4)]
                for (src3, dst_, n3) in ((None, acc[:, 0:NPR], NPR), ('s', acc[:, NPR:TOK].rearrange("p (b t) -> p b t", t=4), 4)):
                    def sl(t):
                        return pre[:, t:t + NPR] if src3 is None else pres[:, :, t:t + 4]
                    rk = ['pre'] if src3 is None else ['pres']
                    ak = 'accp' if src3 is None else 'accs'
                    P.op('dve', lambda dst_=dst_, s3=sl(3), wc=wc: nc.vector.tensor_scalar(out=dst_, in0=s3, scalar1=wc[3], scalar2=None, op0=ALU.mult),
                         reads=rk, writes=[ak])
                    for t in range(3):
                        P.op('dve', lambda dst_=dst_, st=sl(t), t=t, wc=wc: nc.vector.scalar_tensor_tensor(
                            out=dst_, in0=st, scalar=wc[t], in1=dst_, op0=ALU.mult, op1=ALU.add), reads=rk + [ak], writes=[ak])
                if nm in ('v0', 'v1'):
                    a_ = int(nm[1])
                    act(vT[a_], acc, AF.Silu, reads=['accp', 'accs'], writes=[f'vT{a_}'])
                else:
                    act(acc, acc, AF.Silu, reads=['accp', 'accs'], writes=['accp', 'accs'])
                    dstT = qT if nm == 'q' else kT
                    sc_ = (128 ** -0.5) if nm == 'q' else 1.0
                    for ti, (t0, tn) in enumerate(TT):
                        act(sqb[:, 0:tn], acc[:, t0:t0 + tn], AF.Square, reads=['accp', 'accs'], writes=['l2sq'])
                        mm(ps[6][:, 0:tn], onesb, sqb[:, 0:tn], True, True, reads=['l2sq'], writes=[PS[6]])
                        act(lnr[:, 0:tn], ps[6][:, 0:tn], AF.Ln, reads=[PS[6]], writes=['l2ln'], bias=EPS_AP)
                        act(lnr[:, 0:tn], lnr[:, 0:tn], AF.Exp, reads=['l2ln'], writes=['l2ln'], scale=-0.5)
                        P.op('dve', lambda t0=t0, tn=tn, dstT=dstT, sc_=sc_: nc.vector.scalar_tensor_tensor(
                            out=dstT[:, t0:t0 + tn], in0=acc[:, t0:t0 + tn], scalar=sc_, in1=lnr[:, 0:tn], op0=ALU.mult, op1=ALU.mult),
                            reads=['accp', 'accs', 'l2ln'], writes=[f'{nm}T'])
        got = w_request(2048, (lambda inp, kh=kh: inp['gdn_w_in'][0][:, 4096 + 2 * kh * 128:4096 + (2 * kh + 2) * 128]
                               .reshape(8, 128, 256).transpose(1, 0, 2).reshape(128, 2048)))
        if got is not None:
            wt, wkey = got
            wv = wt[:, 0:2048].rearrange("p (k n) -> p k n", k=8)
            for a_ in range(2):
                def dstz(ti, t0, tn, pap, pk, a_=a_):
                    act(sz[a_][:, t0:t0 + tn], pap, AF.Silu, reads=[pk], writes=['siluz'])
                proj_fm(wv, wkey, a_ * 128, dstz)
        tps = ps[5][:, 0:64].bitcast(BF16)
        for (srcT, dtm, dts, key) in ((kT, Ktm, Kts, 'Ktm'), (vT[0], Vtm[0], Vts[0], 'Vtm0'), (vT[1], Vtm[1], Vts[1], 'Vtm1')):
            skey = 'kT' if srcT is kT else ('vT0' if srcT is vT[0] else 'vT1')
            for blk in range(16):
                P.op('pe', lambda srcT=srcT, blk=blk: nc.tensor.transpose(tps, srcT[:, blk * 128:(blk + 1) * 128], identb),
                     reads=[skey], writes=[PS[5]])
                P.op('act', lambda dtm=dtm, blk=blk: nc.scalar.copy(out=dtm[:, blk, :], in_=tps), reads=[PS[5]], writes=[key])
        for a_ in range(2):
            P.op('dve', lambda a_=a_: nc.vector.tensor_copy(out=vTs[a_], in_=vT[a_][:, NPR:TOK]), reads=[f'vT{a_}'], writes=[f'vTs{a_}'])
        P.fence()
        A.release(m2)
        S32 = [A.f32(128) for _ in range(2)]
        Sbf = [A.bf16(128) for _ in range(2)]
        tmpk2 = (A.f32(256), A.bf16(256), A.f32(256), A.f32(256), A.f32(256))
        tmpk = tuple(t_[:, 0:128] for t_ in tmpk2)

        def dbl(fn):
            return [fn() for _ in range(2)]

        def quad(fn):
            return [fn() for _ in range(4)]
        R2 = dbl(lambda: A.f32(256).rearrange("p (h n) -> p h n", h=2))
        D4 = dbl(lambda: A.f32(512).rearrange("p (m n) -> p m n", m=4))
        gsm = dbl(lambda: A.f32(16).rearrange("p (h n) -> p h n", h=2))
        egl = quad(lambda: A.f32(2))
        NA4 = dbl(lambda: A.bf16(512).rearrange("p (m n) -> p m n", m=4))
        N04 = dbl(lambda: A.bf16(512).rearrange("p (m n) -> p m n", m=4))
        N24 = dbl(lambda: A.bf16(512).rearrange("p (m n) -> p m n", m=4))
        T4 = dbl(lambda: A.bf16(512).rearrange("p (m n) -> p m n", m=4))
        Y4 = dbl(lambda: A.bf16(512).rearrange("p (m n) -> p m n", m=4))
        att2 = quad(lambda: A.bf16(256).rearrange("p (h n) -> p h n", h=2))
        bV2 = quad(lambda: A.bf16(256).rearrange("p (h n) -> p h n", h=2))
        TTh = quad(lambda: A.bf16(256).rearrange("p (h n) -> p h n", h=2))
        K22 = dbl(lambda: A.bf16(256).rearrange("p (h n) -> p h n", h=2))
        Kh2 = quad(lambda: A.bf16(256).rearrange("p (h n) -> p h n", h=2))
        nW2 = quad(lambda: A.bf16(256).rearrange("p (h n) -> p h n", h=2))
        Vn2 = quad(lambda: A.bf16(256).rearrange("p (h n) -> p h n", h=2))
        DG2 = dbl(lambda: A.f32(256).rearrange("p (h n) -> p h n", h=2))
        qt2 = quad(lambda: A.bf16(256).rearrange("p (h n) -> p h n", h=2))
        gmb = gmaskb
        for a_ in range(2):
            P.op('dve', lambda a_=a_: nc.vector.memset(S32[a_], 0.0), writes=[f'S32{a_}'])
            P.op('pool', lambda a_=a_: nc.gpsimd.memset(Sbf[a_], 0.0), writes=[f'Sbf{a_}'])

        def GMb(k):
            return gmb[:, k * 128:(k + 1) * 128]
        ctx = {}

        def phaseA(ck):
                samp = ck >= 16
                C = 4 if samp else 128
                pb = ck % 2
                hb = ck % 4
                sfx = f'.{pb}'
                hfx = f'.h{hb}'
                if samp:
                    b_ = ck - 16
                    cols = slice(NPR + b_ * 4, NPR + b_ * 4 + 4)
                    sb2 = b_ % 2
                    Kt, Vt_ = Kts[0:4, sb2, :], [Vts[0][0:4, sb2, :], Vts[1][0:4, sb2, :]]
                    bgv = bgs[0:4, b_, :]
                    kkey, vkeys, bgk = f'Ktms{sb2}', [f'Vtm0s{sb2}', f'Vtm1s{sb2}'], 'bgs'
                    tps7 = ps[7][:, 64:128].bitcast(BF16)
                    scol = slice(b_ * 4, b_ * 4 + 4)
                    for (srcT, dts, skey, dkey) in ((kT[:, cols], Kt, 'kT', kkey), (vTs[0][:, scol], Vt_[0], 'vTs0', vkeys[0]), (vTs[1][:, scol], Vt_[1], 'vTs1', vkeys[1])):
                        P.op('pe', lambda srcT=srcT, tps7=tps7: nc.tensor.transpose(tps7[0:4, :], srcT, identb), reads=[skey], writes=[PS[7]])
                        P.op('act', lambda dts=dts, tps7=tps7: nc.scalar.copy(out=dts, in_=tps7[0:4, :]), reads=[PS[7]], writes=[dkey])
                else:
                    cols = slice(ck * 128, (ck + 1) * 128)
                    Kt, Vt_ = Ktm[:, ck, :], [Vtm[0][:, ck, :], Vtm[1][:, ck, :]]
                    bgv = bg[:, ck, :]
                    kkey, vkeys, bgk = 'Ktm', ['Vtm0', 'Vtm1'], 'bg'
                h0 = heads[0]
                gcols = bgv[:, 16 + h0:18 + h0]
                bcols = bgv[:, h0:h0 + 2]
                r2, d4, gs, eg = R2[pb], D4[pb], gsm[pb], egl[hb]
                na4, n04, n24, t4, y4 = NA4[pb], N04[pb], N24[pb], T4[pb], Y4[pb]
                mm(ps[0][0:C, 0:C], kT[:, cols], kT[:, cols], True, True, reads=['kT'], writes=[PS[0]])
                mm(ps[0][0:C, 128:128 + C], kT[:, cols], qT[:, cols], True, True, reads=['kT', 'qT'], writes=[PS[0]])
                for a_ in range(2):
                    P.op('dve', lambda a_=a_, C=C, r2=r2, gcols=gcols: nc.vector.tensor_scalar(out=r2[0:C, a_, 0:C], in0=Mprev32[0:C, 0:C], scalar1=gcols[:, a_:a_ + 1], scalar2=None, op0=ALU.mult),
                         reads=[bgk, 'cm32'], writes=['R2' + sfx])
                for a_ in range(2):
                    mm(ps[1][0:C, a_ * 128:a_ * 128 + C], Mcur32[0:C, 0:C], r2[0:C, a_, 0:C], True, True, reads=['R2' + sfx, 'cm32'], writes=[PS[1]])
                    mm(ps[1][0:C, 256 + a_ * 128:256 + a_ * 128 + C], r2[0:C, a_, 0:C], Mcur32[0:C, 0:C], True, True, reads=['R2' + sfx, 'cm32'], writes=[PS[1]])
                mm(ps[0][0:C, 256:258], Mcur32[0:C, 0:C], gcols, True, True, reads=[bgk, 'cm32'], writes=[PS[0]])
                mm(ps[0][0:C, 258:260], ones32[0:C, 0:C], gcols, True, True, reads=[bgk, 'cm32'], writes=[PS[0]])
                mm(ps[0][:, 260:262], ones32[0:C, :], gcols, True, True, reads=[bgk, 'cm32'], writes=[PS[0]])
                if samp:
                    for m_ in range(4):
                        act(d4[0:C, m_, 0:C], ps[1][0:C, m_ * 128:m_ * 128 + C], AF.Exp, reads=[PS[1]], writes=['D4' + sfx])
                else:
                    act(d4[:, :, :], ps[1][:, :].rearrange("p (m n) -> p m n", m=4), AF.Exp, reads=[PS[1]], writes=['D4' + sfx])
                P.op('act', lambda gs=gs, C=C: nc.scalar.copy(out=gs[0:C, :, 0], in_=ps[0][0:C, 256:258]), reads=[PS[0]], writes=['gs' + sfx])
                act(gs[0:C, :, 1], ps[0][0:C, 256:258], AF.Exp, reads=[PS[0]], writes=['gs' + sfx])
                P.op('dve', lambda gs=gs, C=C: nc.vector.tensor_tensor(out=gs[0:C, :, 2], in0=ps[0][0:C, 258:260], in1=gs[0:C, :, 0], op=ALU.subtract),
                     reads=[PS[0], 'gs' + sfx], writes=['gs' + sfx])
                act(gs[0:C, :, 2], gs[0:C, :, 2], AF.Exp, reads=['gs' + sfx], writes=['gs' + sfx])
                act(eg, ps[0][:, 260:262], AF.Exp, reads=[PS[0]], writes=['egl' + hfx])
                P.op('dve', lambda gs=gs, C=C, bcols=bcols: nc.vector.tensor_scalar(out=gs[0:C, :, 3], in0=bcols, scalar1=-1.0, scalar2=None, op0=ALU.mult),
                     reads=[bgk], writes=['gs' + sfx])
                P.op('dve', lambda gs=gs, C=C, bcols=bcols: nc.vector.tensor_tensor(out=gs[0:C, :, 4], in0=bcols, in1=gs[0:C, :, 1], op=ALU.mult),
                     reads=[bgk, 'gs' + sfx], writes=['gs' + sfx])
                P.op('pool', lambda C=C, d4=d4: nc.gpsimd.tensor_tensor(out=d4[0:C, 0:2, 0:C], in0=d4[0:C, 0:2, 0:C], in1=Mprev32[0:C, 0:C].unsqueeze(1).broadcast_to([C, 2, C]), op=ALU.mult),
                     reads=['D4' + sfx, 'cm32'], writes=['D4' + sfx])
                P.op('pool', lambda C=C, d4=d4: nc.gpsimd.tensor_tensor(out=d4[0:C, 2:4, 0:C], in0=d4[0:C, 2:4, 0:C], in1=Mcur32[0:C, 0:C].unsqueeze(1).broadcast_to([C, 2, C]), op=ALU.mult),
                     reads=['D4' + sfx, 'cm32'], writes=['D4' + sfx])
                for a_ in range(2):
                    P.op('dve', lambda a_=a_, gs=gs, C=C, d4=d4, na4=na4: nc.vector.scalar_tensor_tensor(
                        out=na4[0:C, a_, 0:C], in0=ps[0][0:C, 0:C], scalar=gs[0:C, a_, 3:4], in1=d4[0:C, a_, 0:C], op0=ALU.mult, op1=ALU.mult),
                        reads=[PS[0], 'gs' + sfx, 'D4' + sfx], writes=['NA4' + sfx])
                    P.op('dve', lambda a_=a_, C=C, d4=d4, at=att2[hb]: nc.vector.tensor_tensor(out=at[0:C, a_, 0:C], in0=ps[0][0:C, 128:128 + C], in1=d4[0:C, 2 + a_, 0:C], op=ALU.mult),
                         reads=[PS[0], 'D4' + sfx], writes=['att2' + hfx])
                yield
                pA, pAk, pB, pBk = ps[2 + 2 * pb], PS[2 + 2 * pb], ps[3 + 2 * pb], PS[3 + 2 * pb]
                pAb = pA[:, 0:128].bitcast(BF16).rearrange("p (h n) -> p h n", h=2)
                for a_ in range(2):
                    P.op('pe', lambda a_=a_, C=C, na4=na4, pAb=pAb: nc.tensor.transpose(pAb[0:C, a_, 0:C], na4[0:C, a_, 0:C], identb[0:C, 0:C]), reads=['NA4' + sfx], writes=[pAk])
                P.op('act', lambda C=C, na4=na4, pAb=pAb: nc.scalar.copy(out=na4[0:C, 2:4, 0:C], in_=pAb[0:C, :, 0:C]), reads=[pAk], writes=['NA4' + sfx])
                P.op('dve', lambda C=C, na4=na4, n04=n04: nc.vector.tensor_tensor(out=n04[0:C, :, 0:C], in0=na4[0:C, :, 0:C], in1=GMb(0)[0:C, 0:C].unsqueeze(1).broadcast_to([C, 4, C]), op=ALU.mult),
                     reads=['NA4' + sfx, 'gmb'], writes=['N04' + sfx])
                P.op('dve', lambda C=C, n04=n04, t4=t4: nc.vector.tensor_tensor(out=t4[0:C, :, 0:C], in0=n04[0:C, :, 0:C], in1=identb[0:C, 0:C].unsqueeze(1).broadcast_to([C, 4, C]), op=ALU.add),
                     reads=['N04' + sfx], writes=['T4' + sfx])
                yield
                for step in range(1 if samp else 2):
                    src, sk = (n04, 'N04' + sfx) if step == 0 else (n24, 'N24' + sfx)
                    dst, dk_ = (n24, 'N24' + sfx) if step == 0 else (n04, 'N04' + sfx)
                    for a_ in range(2):
                        mm(pA[0:C, a_ * 128:a_ * 128 + C], src[0:C, 2 + a_, 0:C], src[0:C, a_, 0:C], True, True, reads=[sk], writes=[pAk])
                        mm(pA[0:C, (2 + a_) * 128:(2 + a_) * 128 + C], src[0:C, a_, 0:C], src[0:C, 2 + a_, 0:C], True, True, reads=[sk], writes=[pAk])
                    yield
                    P.op('act', lambda C=C, dst=dst: nc.scalar.copy(out=dst[0:C, :, 0:C], in_=pA[0:C, :].rearrange("p (m n) -> p m n", m=4)[:, :, 0:C]), reads=[pAk], writes=[dk_])
                    for a_ in range(2):
                        mm(pB[0:C, a_ * 128:a_ * 128 + C], dst[0:C, 2 + a_, 0:C], t4[0:C, a_, 0:C], True, True, reads=[dk_, 'T4' + sfx], writes=[pBk])
                        mm(pB[0:C, (2 + a_) * 128:(2 + a_) * 128 + C], dst[0:C, a_, 0:C], t4[0:C, 2 + a_, 0:C], True, True, reads=[dk_, 'T4' + sfx], writes=[pBk])
                    yield
                    P.op('dve', lambda C=C, t4=t4: nc.vector.tensor_tensor(out=t4[0:C, :, 0:C], in0=pB[0:C, :].rearrange("p (m n) -> p m n", m=4)[:, :, 0:C], in1=t4[0:C, :, 0:C], op=ALU.add),
                         reads=[pBk, 'T4' + sfx], writes=['T4' + sfx])
                if not samp:
                    for li in range(4):
                        lastl = li == 3
                        Ms = GMb(1 + 2 * li)
                        for a_ in range(2):
                            mm(pA[:, a_ * 128:(a_ + 1) * 128], na4[:, 2 + a_, :], t4[:, a_, :], True, True, reads=['NA4' + sfx, 'T4' + sfx], writes=[pAk])
                        yield
                        P.op('dve', lambda y4=y4, Ms=Ms, pA=pA: nc.vector.tensor_tensor(out=y4[:, 0:2, :], in0=pA[:, 0:256].rearrange("p (m n) -> p m n", m=2),
                                                                                 in1=Ms.unsqueeze(1).broadcast_to([128, 2, 128]), op=ALU.mult),
                             reads=[pAk, 'gmb'], writes=['Y4' + sfx])
                        for a_ in range(2):
                            if not lastl:
                                mm(pB[:, a_ * 128:(a_ + 1) * 128], t4[:, 2 + a_, :], y4[:, a_, :], True, False, reads=['T4' + sfx, 'Y4' + sfx], writes=[pBk])
                                mm(pB[:, a_ * 128:(a_ + 1) * 128], identb, t4[:, a_, :], False, True, reads=['T4' + sfx], writes=[pBk])
                            mm(pB[:, (2 + a_) * 128:(3 + a_) * 128], y4[:, a_, :], t4[:, 2 + a_, :], True, False, reads=['T4' + sfx, 'Y4' + sfx], writes=[pBk])
                            mm(pB[:, (2 + a_) * 128:(3 + a_) * 128], identb, t4[:, 2 + a_, :], False, True, reads=['T4' + sfx], writes=[pBk])
                        yield
                        if lastl:
                            P.op('act', lambda tth=TTh[hb], pB=pB: nc.scalar.copy(out=tth[:, :, :], in_=pB[:, 256:512].rearrange("p (m n) -> p m n", m=2)), reads=[pBk], writes=['TTh' + hfx])
                        else:
                            P.op('act', lambda t4=t4, pB=pB: nc.scalar.copy(out=t4[:, :, :], in_=pB[:, :].rearrange("p (m n) -> p m n", m=4)), reads=[pBk], writes=['T4' + sfx])
                if samp:
                    P.op('act', lambda C=C, t4=t4, tth=TTh[hb]: nc.scalar.copy(out=tth[0:C, :, 0:C], in_=t4[0:C, 2:4, 0:C]), reads=['T4' + sfx], writes=['TTh' + hfx])
                yield
                pw, pwk = ps[1], PS[1]
                for a_ in range(2):
                    P.op('pool', lambda a_=a_, C=C, Vt_=Vt_, bcols=bcols, t_=bV2[hb]: nc.gpsimd.tensor_scalar(out=t_[0:C, a_, :], in0=Vt_[a_], scalar1=bcols[:, a_:a_ + 1], scalar2=1.0, op0=ALU.mult, op1=ALU.mult),
                         reads=[vkeys[a_], bgk], writes=['bV2' + hfx])
                    P.op('pool', lambda a_=a_, C=C, Kt=Kt, gs=gs, t_=K22[pb]: nc.gpsimd.tensor_scalar(out=t_[0:C, a_, :], in0=Kt, scalar1=gs[0:C, a_, 4:5], scalar2=1.0, op0=ALU.mult, op1=ALU.mult),
                         reads=[kkey, 'gs' + sfx], writes=['K22' + sfx])
                    P.op('pool', lambda a_=a_, C=C, Kt=Kt, gs=gs, t_=Kh2[hb]: nc.gpsimd.tensor_scalar(out=t_[0:C, a_, :], in0=Kt, scalar1=gs[0:C, a_, 2:3], scalar2=1.0, op0=ALU.mult, op1=ALU.mult),
                         reads=[kkey, 'gs' + sfx], writes=['Kh2' + hfx])
                    P.op('dve', lambda a_=a_, C=C, gs=gs, t_=DG2[pb]: nc.vector.tensor_scalar(out=t_[0:C, a_, 0:C], in0=ident32[0:C, 0:C], scalar1=gs[0:C, a_, 1:2], scalar2=None, op0=ALU.mult),
                         reads=['gs' + sfx, 'cm32'], writes=['DG2' + sfx])
                for a_ in range(2):
                    mm(pw[:, a_ * 128:a_ * 128 + C], K22[pb][0:C, a_, :], TTh[hb][0:C, a_, 0:C], True, True, reads=['K22' + sfx, 'TTh' + hfx], writes=[pwk])
                    mm(pw[:, 256 + a_ * 128:256 + a_ * 128 + C], ones32[0:C, :], DG2[pb][0:C, a_, 0:C], True, True, reads=['DG2' + sfx, 'cm32'], writes=[pwk])
                act(nW2[hb][:, :, 0:C], pw[:, 0:256].rearrange("p (h n) -> p h n", h=2)[:, :, 0:C], AF.Identity, reads=[pwk], writes=['nW2' + hfx], scale=-1.0)
                P.op('dve', lambda C=C, cols=cols, t_=qt2[hb]: nc.vector.tensor_tensor(out=t_[:, :, 0:C], in0=qT[:, cols].unsqueeze(1).broadcast_to([128, 2, C]),
                                                                                 in1=pw[:, 256:512].rearrange("p (h n) -> p h n", h=2)[:, :, 0:C], op=ALU.mult),
                     reads=['qT', pwk], writes=['qt2' + hfx])
                ctx[ck] = dict(C=C, pb=pb, hb=hb, hfx=hfx, sfx=sfx, samp=samp, cols=cols, t4=t4, eg=eg, b_=(ck - 16 if samp else None))
                yield

        def phaseB(ck):
                c_ = ctx[ck]
                C, pb, sfx, samp, cols, t4, eg, b_ = c_['C'], c_['pb'], c_['sfx'], c_['samp'], c_['cols'], c_['t4'], c_['eg'], c_['b_']
                hb, hfx = c_['hb'], c_['hfx']
                pw, pwk, po, pok = ps[6], PS[6], ps[6], PS[6]
                if samp:
                    for a_ in range(2):
                        hd = heads[a_]
                        P.dma('sp', lambda a_=a_, b_=b_, hd=hd: nc.sync.dma_start(out=S32[a_], in_=sg_in[b_, hd]), writes=[f'S32{a_}'])
                        P.op('act', lambda a_=a_: nc.scalar.copy(out=Sbf[a_], in_=S32[a_]), reads=[f'S32{a_}'], writes=[f'Sbf{a_}'])
                for a_ in range(2):
                    hd = heads[a_]
                    mm(pw[0:C, a_ * 128:128 + a_ * 128], TTh[hb][0:C, a_, 0:C], bV2[hb][0:C, a_, :], True, False, reads=['TTh' + hfx, 'bV2' + hfx], writes=[pwk])
                    mm(pw[0:C, a_ * 128:128 + a_ * 128], nW2[hb][:, a_, 0:C], Sbf[a_], False, True, reads=['nW2' + hfx, f'Sbf{a_}'], writes=[pwk])
                yield
                P.op('act', lambda C=C, t_=Vn2[hb]: nc.scalar.copy(out=t_[0:C, :, :], in_=pw[0:C, 0:256].rearrange("p (h n) -> p h n", h=2)), reads=[pwk], writes=['Vn2' + hfx])
                for a_ in range(2):
                    hd = heads[a_]
                    mm(po[:, 256 + a_ * 128:256 + a_ * 128 + C], Sbf[a_], qt2[hb][:, a_, 0:C], True, False, reads=[f'Sbf{a_}', 'qt2' + hfx], writes=[pok])
                    mm(po[:, 256 + a_ * 128:256 + a_ * 128 + C], Vn2[hb][0:C, a_, :], att2[hb][0:C, a_, 0:C], False, True, reads=['Vn2' + hfx, 'att2' + hfx], writes=[pok])
                    mm(ps[7][:, 256 + a_ * 128:384 + a_ * 128], Kh2[hb][0:C, a_, :], Vn2[hb][0:C, a_, :], True, True, reads=['Kh2' + hfx, 'Vn2' + hfx], writes=[PS[7]])
                for a_ in range(2):
                    hd = heads[a_]
                    if not samp:
                        P.op('dve', lambda a_=a_, eg=eg: nc.vector.scalar_tensor_tensor(out=Sbf[a_], in0=S32[a_], scalar=eg[:, a_:a_ + 1], in1=ps[7][:, 256 + a_ * 128:384 + a_ * 128], op0=ALU.mult, op1=ALU.add),
                             reads=[f'S32{a_}', 'egl' + hfx, PS[7]], writes=[f'Sbf{a_}'])
                    P.op('dve', lambda a_=a_, eg=eg: nc.vector.scalar_tensor_tensor(out=S32[a_], in0=S32[a_], scalar=eg[:, a_:a_ + 1], in1=ps[7][:, 256 + a_ * 128:384 + a_ * 128],
                                                                                 op0=ALU.mult, op1=ALU.add),
                         reads=[f'S32{a_}', 'egl' + hfx, PS[7]], writes=[f'S32{a_}'])
                    if samp:
                        P.dma('sp', lambda a_=a_, b_=b_, hd=hd: nc.sync.dma_start(out=sg_out[b_, hd], in_=S32[a_]), reads=[f'S32{a_}'], writes=['sg_out'])
                    else:
                        pass
                        if ck == 15:
                            P.dma('sp', lambda a_=a_, hd=hd: nc.sync.dma_start(out=gp_out[hd], in_=S32[a_]), reads=[f'S32{a_}'], writes=['gp_out'])
                yield
                if samp:
                    for a_ in range(2):
                        gated_norm_out(po[:, 256 + a_ * 128:256 + a_ * 128 + C], C, cols, pok, vecT[:, V_GN:V_GN + 1], sz[a_], og[a_], f'og{a_}', tmpk)
                else:
                    o32, sqb_, lnr_, rn_, t3_ = tmpk2
                    P.op('act', lambda: nc.scalar.copy(out=o32, in_=po[:, 256:512]), reads=[pok], writes=['gn_o32'])
                    act(sqb_, o32, AF.Square, reads=['gn_o32'], writes=['gn_sq'])
                    mm(ps[7][:, 0:256], onesb, sqb_, True, True, reads=['gn_sq'], writes=[PS[7]])
                    act(lnr_, ps[7][:, 0:256], AF.Ln, reads=[PS[7]], writes=['gn_ln'], scale=1.0 / 128, bias=EPS_AP)
                    act(rn_, lnr_, AF.Exp, reads=['gn_ln'], writes=['gn_rn'], scale=-0.5)
                    P.op('dve', lambda: nc.vector.scalar_tensor_tensor(out=t3_, in0=o32, scalar=vecT[:, V_GN:V_GN + 1], in1=rn_, op0=ALU.mult, op1=ALU.mult),
                         reads=['gn_o32', 'gn_rn'], writes=['gn_t3'])
                    for a_ in range(2):
                        P.op('pool', lambda a_=a_, cols=cols: nc.gpsimd.tensor_tensor(out=og[a_][:, cols], in0=t3_[:, a_ * 128:(a_ + 1) * 128], in1=sz[a_][:, cols], op=ALU.mult),
                             reads=['gn_t3', 'siluz'], writes=[f'og{a_}'])
                yield

        def seqB(k0):
            for kk_ in (k0, k0 + 1):
                for _ in phaseB(kk_):
                    yield

        def run_zip(gens):
            alive = [True] * len(gens)
            while any(alive):
                for gi_, g_ in enumerate(gens):
                    if alive[gi_]:
                        try:
                            next(g_)
                        except StopIteration:
                            alive[gi_] = False
        run_zip([phaseA(0), phaseA(1)])
        for k0 in range(2, 32, 2):
            if k0 == 16:
                run_zip([seqB(14)])
                run_zip([phaseA(16), phaseA(17)])
            else:
                run_zip([phaseA(k0), phaseA(k0 + 1), seqB(k0 - 2)])
        run_zip([seqB(30)])
        P.fence()
        out_proj_acc(og, lambda inp, kh=kh: inp['gdn_w_out'][0][2 * kh * 128:(2 * kh + 2) * 128], ['og0', 'og1'])

    def hgrn(i, j):
        rmsnorm(V_NMIX + i * 8, f'mix{i}')
        P.fence()
        m0 = A.mark()
        seg = A.f32(TOK)
        P.dma('sp', lambda: nc.sync.dma_start(out=seg, in_=seg_in), writes=['seg'])
        rowmask32 = A.f32(4)
        P.dma('sp', lambda: nc.sync.dma_start(out=rowmask32, in_=cmat_in[:, 896:900]), writes=['rowmask32'])
        assert i == 2
        lbe = A.f32(32).rearrange("p (l c) -> p l c", l=4)
        lbs = A.f32(8)
        lb = A.f32(8)
        oml = A.f32(8)
        act(lbe, vecT[:, V_LB:V_LB + 32].rearrange("p (l c) -> p l c", l=4), AF.Exp, reads=['vecT'], writes=['lbe'])
        P.op('dve', lambda: nc.vector.tensor_tensor(out=lbs, in0=lbe[:, 0, :], in1=lbe[:, 1, :], op=ALU.add), reads=['lbe'], writes=['lbs'])
        P.op('dve', lambda: nc.vector.tensor_tensor(out=lbs, in0=lbs, in1=lbe[:, 2, :], op=ALU.add), reads=['lbe', 'lbs'], writes=['lbs'])
        P.op('dve', lambda: nc.vector.tensor_tensor(out=lbs, in0=lbs, in1=lbe[:, 3, :], op=ALU.add), reads=['lbe', 'lbs'], writes=['lbs'])
        P.op('dve', lambda: nc.vector.reciprocal(out=lbs, in_=lbs), reads=['lbs'], writes=['lbs'])
        P.op('dve', lambda: nc.vector.tensor_tensor(out=lb, in0=lbe[:, 1, :], in1=lbe[:, 2, :], op=ALU.add), reads=['lbe'], writes=['lb'])
        P.op('dve', lambda: nc.vector.tensor_tensor(out=lb, in0=lb, in1=lbs, op=ALU.mult), reads=['lb', 'lbs'], writes=['lb'])
        P.op('dve', lambda: nc.vector.tensor_scalar(out=oml, in0=lb, scalar1=-1.0, scalar2=1.0, op0=ALU.mult, op1=ALU.add), reads=['lb'], writes=['oml'])
        m1 = A.mark()
        for hd in range(8):
            A.release(m1)
            hgrn_head(i, hd, seg, lb, oml, rowmask32)
        A.release(m0)
        P.fence()

    def hgrn_head(i, hd, seg, lb, oml, rowmask32):
        q32 = A.f32(TOK)
        k32_ = A.f32(TOK)
        bc = A.f32(TOK)
        ex = A.f32(TOK)
        qtT = A.bf16(TOK)
        ktT = A.bf16(TOK)
        khT = A.bf16(TOK)
        sz = A.bf16(TOK)
        og = A.bf16(TOK)
        Vtm = A.bf16(16 * 128).rearrange("p (b n) -> p b n", b=16)
        Vts = A.bf16(16 * 128).rearrange("p (b n) -> p b n", b=16)
        Khm = A.bf16(16 * 128).rearrange("p (b n) -> p b n", b=16)
        Khs = A.bf16(16 * 128).rearrange("p (b n) -> p b n", b=16)
        ebl = A.f32(80)
        lbc, omc = lb[:, hd:hd + 1], oml[:, hd:hd + 1]

        def wblk(c0, c1):
            def fn(inp, c0=c0, c1=c1):
                w = inp['hgrn_w_in'][0]
                cols = np.concatenate([np.arange(c0 * 128, (c0 + 1) * 128), np.arange(c1 * 128, (c1 + 1) * 128)])
                return w[:, cols].reshape(8, 128, 256).transpose(1, 0, 2).reshape(128, 2048)
            return fn
        got = w_request(2048, wblk(hd, 8 + hd))
        if got is not None:
            wt, wkey = got
            wv = wt[:, 0:2048].rearrange("p (k n) -> p k n", k=8)
            proj_fm(wv, wkey, 0, lambda ti, t0, tn, pap, pk: act(q32[:, t0:t0 + tn], pap, AF.Silu, reads=[pk], writes=['q32']))

            def dfz(ti, t0, tn, pap, pk):
                act(k32_[:, t0:t0 + tn], pap, AF.Sigmoid, reads=[pk], writes=['k32_'])
            proj_fm(wv, wkey, 128, dfz)
        P.op('dve', lambda: nc.vector.tensor_scalar(out=k32_, in0=k32_, scalar1=omc, scalar2=lbc, op0=ALU.mult, op1=ALU.add),
             reads=['k32_', 'lb', 'oml'], writes=['k32_'])
        act(bc, k32_, AF.Ln, reads=['k32_'], writes=['bc'])
        P.op('dve', lambda: nc.vector.tensor_scalar(out=k32_, in0=k32_, scalar1=-1.0, scalar2=1.0, op0=ALU.mult, op1=ALU.add),
             reads=['k32_'], writes=['k32_'])
        P.op('dve', lambda: nc.vector.tensor_tensor_scan(out=bc, data0=seg, data1=bc, initial=0.0, op0=ALU.mult, op1=ALU.add),
             reads=['bc', 'seg'], writes=['bc'])
        act(ex, bc, AF.Exp, reads=['bc'], writes=['ex'])
        P.op('dve', lambda: nc.vector.tensor_tensor(out=qtT, in0=q32, in1=ex, op=ALU.mult), reads=['q32', 'ex'], writes=['qtT'])
        act(ex, bc, AF.Exp, reads=['bc', 'qtT'], writes=['ex'], scale=-1.0)
        P.op('dve', lambda: nc.vector.tensor_tensor(out=ktT, in0=k32_, in1=ex, op=ALU.mult), reads=['k32_', 'ex'], writes=['ktT'])
        blp = bc[:, 0:NPR].rearrange("p (c t) -> p c t", t=32)
        bls = bc[:, NPR:TOK].rearrange("p (c t) -> p c t", t=4)
        P.op('dve', lambda: nc.vector.tensor_tensor(out=ex[:, 0:NPR].rearrange("p (c t) -> p c t", t=32), in0=blp[:, :, 31:32].broadcast_to([128, 64, 32]),
                                                   in1=blp, op=ALU.subtract), reads=['bc', 'ktT'], writes=['ex'])
        P.op('dve', lambda: nc.vector.tensor_tensor(out=ex[:, NPR:TOK].rearrange("p (c t) -> p c t", t=4), in0=bls[:, :, 3:4].broadcast_to([128, 16, 4]),
                                                   in1=bls, op=ALU.subtract), reads=['bc', 'ktT'], writes=['ex'])
        act(ex, ex, AF.Exp, reads=['ex'], writes=['ex'])
        P.op('dve', lambda: nc.vector.tensor_tensor(out=khT, in0=k32_, in1=ex, op=ALU.mult), reads=['k32_', 'ex'], writes=['khT'])
        act(ebl[:, 0:64].unsqueeze(2), blp[:, :, 31:32], AF.Exp, reads=['bc'], writes=['ebl'])
        act(ebl[:, 64:80].unsqueeze(2), bls[:, :, 3:4], AF.Exp, reads=['bc'], writes=['ebl'])
        got = w_request(2048, wblk(16 + hd, 24 + hd))
        if got is not None:
            wt, wkey = got
            wv = wt[:, 0:2048].rearrange("p (k n) -> p k n", k=8)
            for blk in range(16):
                pb = blk % 2
                for kc in range(8):
                    mm(ps[pb][:, 0:128], h[:, kc, blk * 128:(blk + 1) * 128], wv[:, kc, 0:128], kc == 0, kc == 7,
                       reads=[wkey, f'h{kc}.{blk // 4}'], writes=[PS[pb]])
                P.op('act', lambda blk=blk, pb=pb: nc.scalar.copy(out=Vtm[:, blk, :], in_=ps[pb][:, 0:128]), reads=[PS[pb]], writes=['Vtm'])
            for b_ in range(NSB):
                pb = b_ % 2
                for kc in range(8):
                    mm(ps[pb][0:4, 0:128], h[:, kc, NPR + b_ * 4:NPR + b_ * 4 + 4], wv[:, kc, 0:128], kc == 0, kc == 7,
                       reads=[wkey, f'h{kc}.4'], writes=[PS[pb]])
                P.op('act', lambda b_=b_, pb=pb: nc.scalar.copy(out=Vts[0:4, b_, :], in_=ps[pb][0:4, 0:128]), reads=[PS[pb]], writes=['Vts'])
            proj_fm(wv, wkey, 128, lambda ti, t0, tn, pap, pk: act(sz[:, t0:t0 + tn], pap, AF.Silu, reads=[pk], writes=['siluz']))
        tps = ps[5][:, 0:64].bitcast(BF16)
        for blk in range(16):
            P.op('pe', lambda blk=blk: nc.tensor.transpose(tps, khT[:, blk * 128:(blk + 1) * 128], identb), reads=['khT'], writes=[PS[5]])
            P.op('act', lambda blk=blk: nc.scalar.copy(out=Khm[:, blk, :], in_=tps), reads=[PS[5]], writes=['Khm'])
        for b_ in range(NSB):
            P.op('pe', lambda b_=b_: nc.tensor.transpose(tps[0:4, :], khT[:, NPR + b_ * 4:NPR + b_ * 4 + 4], identb), reads=['khT'], writes=[PS[5]])
            P.op('act', lambda b_=b_: nc.scalar.copy(out=Khs[0:4, b_, :], in_=tps[0:4, :]), reads=[PS[5]], writes=['Khs'])
        S32 = A.f32(128)
        Sbf = A.bf16(128)
        attT = [A.bf16(128) for _ in range(2)]
        Kmk = [A.bf16(128) for _ in range(4)]
        tmpk = (A.f32(128), A.bf16(128), A.f32(128), A.f32(128), A.f32(128))
        P.op('dve', lambda: nc.vector.memset(S32, 0.0), writes=['S32'])
        P.op('pool', lambda: nc.gpsimd.memset(Sbf, 0.0), writes=['Sbf'])
        gcol = vecT[:, V_HN:V_HN + 1]
        for blk in range(16):
            b2 = blk % 2
            cols = slice(blk * 128, (blk + 1) * 128)
            mm(ps[2 + b2][:, 0:128], ktT[:, cols], qtT[:, cols], True, True, reads=['ktT', 'qtT'], writes=[PS[2 + b2]])
            P.op('dve', lambda b2=b2: nc.vector.tensor_tensor(out=attT[b2], in0=ps[2 + b2][:, 0:128], in1=Mblkb, op=ALU.mult),
                 reads=[PS[2 + b2]], writes=[f'attT{b2}'])
            po = ps[6]
            mm(po[:, 0:128], Vtm[:, blk, :], attT[b2], True, False, reads=['Vtm', f'attT{b2}'], writes=[PS[6]])
            for sc in range(4):
                P.op('act', lambda blk=blk, sc=sc: nc.scalar.activation(out=Kmk[sc], in_=Khm[:, blk, :], func=AF.Identity, scale=rowmask32[:, sc:sc + 1]),
                     reads=['Khm', 'rowmask32'], writes=[f'Kmk{sc}'])
            for sc in range(4):
                c32 = slice(blk * 128 + sc * 32, blk * 128 + sc * 32 + 32)
                mm(po[:, sc * 32:(sc + 1) * 32], Sbf, qtT[:, c32], False, sc == 3, reads=['Sbf', 'qtT'], writes=[PS[6]])
                mm(ps[4][:, 0:128], Kmk[sc], Vtm[:, blk, :], True, True, reads=[f'Kmk{sc}', 'Vtm'], writes=[PS[4]])
                P.op('dve', lambda blk=blk, sc=sc: nc.vector.scalar_tensor_tensor(out=Sbf, in0=S32, scalar=ebl[:, blk * 4 + sc:blk * 4 + sc + 1], in1=ps[4][:, 0:128],
                                                                              op0=ALU.mult, op1=ALU.add), reads=['S32', 'ebl', PS[4]], writes=['Sbf'])
                P.op('dve', lambda blk=blk, sc=sc: nc.vector.scalar_tensor_tensor(out=S32, in0=S32, scalar=ebl[:, blk * 4 + sc:blk * 4 + sc + 1], in1=ps[4][:, 0:128],
                                                                              op0=ALU.mult, op1=ALU.add), reads=['S32', 'ebl', PS[4]], writes=['S32'])
            gated_norm_out(po[:, 0:128], 128, cols, PS[6], gcol, sz, og, 'og', tmpk)
        P.dma('sp', lambda: nc.sync.dma_start(out=hp_out[hd], in_=S32), reads=['S32'], writes=['hp_out'])
        for b_ in range(NSB):
            cols = slice(NPR + b_ * 4, NPR + b_ * 4 + 4)
            P.dma('sp', lambda b_=b_: nc.sync.dma_start(out=S32, in_=sh_in[b_, hd]), writes=['S32'])
            P.op('act', lambda: nc.scalar.copy(out=Sbf, in_=S32), reads=['S32'], writes=['Sbf'])
            b2 = b_ % 2
            mm(ps[2 + b2][0:4, 0:4], ktT[:, cols], qtT[:, cols], True, True, reads=['ktT', 'qtT'], writes=[PS[2 + b2]])
            P.op('dve', lambda b2=b2: nc.vector.tensor_tensor(out=attT[b2][0:4, 0:4], in0=ps[2 + b2][0:4, 0:4], in1=Mcurb[0:4, 0:4], op=ALU.mult),
                 reads=[PS[2 + b2]], writes=[f'attT{b2}'])
            po = ps[6]
            mm(po[:, 0:4], Vts[0:4, b_, :], attT[b2][0:4, 0:4], True, False, reads=['Vts', f'attT{b2}'], writes=[PS[6]])
            mm(po[:, 0:4], Sbf, qtT[:, cols], False, True, reads=['Sbf', 'qtT'], writes=[PS[6]])
            mm(ps[4][:, 0:128], Khs[0:4, b_, :], Vts[0:4, b_, :], True, True, reads=['Khs', 'Vts'], writes=[PS[4]])
            P.op('dve', lambda b_=b_: nc.vector.scalar_tensor_tensor(out=S32, in0=S32, scalar=ebl[:, 64 + b_:65 + b_], in1=ps[4][:, 0:128],
                                                                 op0=ALU.mult, op1=ALU.add), reads=['S32', 'ebl', PS[4]], writes=['S32'])
            P.dma('sp', lambda b_=b_: nc.sync.dma_start(out=sh_out[b_, hd], in_=S32), reads=['S32'], writes=['sh_out'])
            gated_norm_out(po[:, 0:4], 4, cols, PS[6], gcol, sz, og, 'og', tmpk)
        out_proj_acc([og], lambda inp, hd=hd: inp['hgrn_w_out'][0][hd * 128:(hd + 1) * 128], ['og'])

    eps_t = nc.alloc_sbuf_tensor("eps_t", [128, 2], F32)
    EPS_AP = eps_t[:, 0:1]
    ONE_AP = eps_t[:, 1:2]

    def body():
        P.op('dve', lambda: nc.vector.memset(eps_t[:, 0:1], EPS), writes=['eps'])
        P.op('dve', lambda: nc.vector.memset(eps_t[:, 1:2], 1.0), writes=['eps'])
        P.dma('sp', lambda: nc.sync.dma_start(out=vecT[:], in_=vecT_in), writes=['vecT'])
        P.dma('sp', lambda: nc.sync.dma_start(out=cm32[:], in_=cmat_in), writes=['cm32'])
        P.op('pool', lambda: nc.gpsimd.tensor_copy(out=cmbf[:], in_=cm32[:]), reads=['cm32'], writes=['cmbf'])
        P.dma('sp', lambda: nc.sync.dma_start(out=sink32[:], in_=sink_in), writes=['sink32'])
        act(esink[:], sink32[:], AF.Exp, reads=['sink32'], writes=['esink'])
        for c in range(8):
            P.dma('sp', lambda c=c: nc.sync.dma_start(out=x[:, c, :], in_=xT_in[:, c, :]), writes=[f'x{c}.{ti}' for ti in range(5)])
        P.fence()
        for i in range(DEPTH):
            kind, j = i % 3, i // 3
            ffn(i, 0)
            if stop_after == (i, 'a'):
                break
            if kind == 0:
                swa(i, j)
            elif kind == 1:
                gdn(i, j)
            else:
                hgrn(i, j)
            if stop_after == (i, 'm'):
                break
            ffn(i, 1)
            if stop_after == (i, 'b'):
                break
        if stop_after is None:
            rmsnorm(V_FIN, 'final', final=True)
        P.fence()
        for c in range(8):
            P.dma('sp', lambda c=c: nc.sync.dma_start(out=yT_out[:, c, :], in_=x[:, c, :]), reads=[f'x{c}.{ti}' for ti in range(5)], writes=['yT_out'])
        P.op('sp', lambda: None, reads=['yT_out', 'kp_out', 'vp_out', 'ks_out', 'vs_out', 'convp_out', 'convs_out',
                                        'sg_out', 'gp_out', 'sh_out', 'hp_out'])

    P.plan = True
    body()
    P.plan = False
    total = sum(n for n, _ in WS.plan_list)
    wflat_holder['ap'] = din("wflat", [128, total])
    A.top = 0
    body()
    stats = P.emit()
    stats['arena_peak'] = A.peak
    stats['wtotal'] = total
    return nc, WS.specs, stats


def pack_weights(specs, inputs, total):
    wflat = np.empty((128, total), np.float32)
    for off, n, fn in specs:
        wflat[:, off:off + n] = fn(inputs)
    return wflat


def host_inputs(inputs, specs, total):
    cst = _consts()
    wflat = pack_weights(specs, inputs, total)
    vec = np.concatenate([
        inputs['norm_ffn'].reshape(64, 128), inputs['norm_mix'].reshape(32, 128),
        inputs['final_norm'].reshape(8, 128), inputs['gdn_conv_w'].reshape(128, 128),
        inputs['hgrn_lb_logits'].reshape(32, 128), inputs['gdn_norm'].reshape(1, 128),
        inputs['hgrn_norm'].reshape(1, 128), np.zeros((6, 128), np.float32)], axis=0)
    vecT = np.ascontiguousarray(vec.T)
    sinks = np.ascontiguousarray(np.broadcast_to(inputs['swa_sinks'].reshape(1, 32), (128, 32))).astype(np.float32)
    gdnvec = np.ascontiguousarray(np.broadcast_to(
        np.concatenate([inputs['gdn_dt_bias'].reshape(16), inputs['gdn_a_log'].reshape(16)])[None, :], (128, 32))).astype(np.float32)
    maps = []
    for c in range(NCORE):
        xp = inputs['x_prompt'][c]
        xs = inputs['x_sample'][c * NSB:(c + 1) * NSB].reshape(NSM, D)
        xa = np.concatenate([xp, xs], axis=0)
        xT = np.ascontiguousarray(xa.T.reshape(8, 128, TOK).transpose(1, 0, 2))
        ck = inputs['cache_swa_k'][:, c * NSB:(c + 1) * NSB]
        kcache = ck.reshape(2, NSB, 128, 2, 2, 64).transpose(0, 4, 5, 3, 1, 2).reshape(2, 128, 2, NSB, 128)
        cv = inputs['cache_swa_v'][:, c * NSB:(c + 1) * NSB]
        vcache = cv.transpose(0, 2, 1, 3, 4).reshape(2, 128, NSB, 256)
        cs = inputs['state_gdn_conv'][0, c * NSB:(c + 1) * NSB]
        convs = np.ascontiguousarray(cs.reshape(NSB, 3, 32, 128).transpose(3, 2, 0, 1))
        sg = np.ascontiguousarray(inputs['state_gdn'][0, c * NSB:(c + 1) * NSB])
        sh = np.ascontiguousarray(inputs['state_hgrn'][0, c * NSB:(c + 1) * NSB])
        maps.append(dict(gdnvec=gdnvec, gmask=cst['gmask'], seg=cst['seg'], convs=convs, sg=sg, sh=sh, xT=xT, wflat=wflat, vecT=vecT, cmat=cst['cmat'], rope_c=cst['rope_c'], rope_s=cst['rope_s'],
                         sinks=sinks, kcache=np.ascontiguousarray(kcache), vcache=np.ascontiguousarray(vcache)))
    return maps


_CACHE = {}


def kernel(**inputs):
    inputs = {k: np.asarray(v) for k, v in inputs.items()}
    if 'prog' not in _CACHE:
        _CACHE['prog'] = build_program()
    nc, specs, stats = _CACHE['prog']
    maps = host_inputs(inputs, specs, stats['wtotal'])
    res = run_bass_kernel_spmd(nc, maps, core_ids=list(range(NCORE)))
    R = res.results
    B = NCORE
    y_prompt = np.empty((B, NPR, D), np.float32)
    y_sample = np.empty((B * NSB, 4, D), np.float32)
    pk = np.empty((2, B, 128, 4, 64), np.float32)
    pv = np.empty((2, B, 128, 4, 64), np.float32)
    pg = np.empty((1, B, 16, 128, 128), np.float32)
    pc = np.empty((1, B, 3, 4096), np.float32)
    ph = np.empty((1, B, 8, 128, 128), np.float32)
    sk = np.empty((2, B * NSB, 128, 4, 64), np.float32)
    sv = np.empty((2, B * NSB, 128, 4, 64), np.float32)
    sg = np.empty((1, B * NSB, 16, 128, 128), np.float32)
    sc = np.empty((1, B * NSB, 3, 4096), np.float32)
    sh = np.empty((1, B * NSB, 8, 128, 128), np.float32)
    for c in range(B):
        r = R[c]
        y = np.asarray(r['yT']).transpose(2, 1, 0).reshape(TOK, D)
        y_prompt[c] = y[:NPR]
        y_sample[c * NSB:(c + 1) * NSB] = y[NPR:].reshape(NSB, 4, D)
        bs = slice(c * NSB, (c + 1) * NSB)
        for j in range(2):
            pk[j, c] = np.asarray(r['kp'])[j].transpose(2, 1, 0)
            pv[j, c] = np.asarray(r['vp'])[j].reshape(128, 4, 64)
            sk[j, bs] = np.asarray(r['ks'])[j].transpose(0, 3, 1, 2)
            sv[j, bs] = np.asarray(r['vs'])[j].reshape(NSB, 128, 4, 64)
        pg[0, c] = np.asarray(r['gp_o'])
        ph[0, c] = np.asarray(r['hp_o'])
        sg[0, bs] = np.asarray(r['sg_o'])
        sh[0, bs] = np.asarray(r['sh_o'])
        pc[0, c] = np.asarray(r['convp']).transpose(2, 1, 0).reshape(3, 4096)
        sc[0, bs] = np.asarray(r['convs_o']).transpose(2, 3, 1, 0).reshape(NSB, 3, 4096)
    return (y_prompt, y_sample, pk, pv, pg, pc, ph, sk, sv, sg, sc, sh)
```
